# Optimizing a Trainium2 kernel written in Bass

```python
import math
import jax, jax.numpy as jnp
from jax import lax
import numpy as np

D_MODEL = 1024
BATCH = 16
SEQ = 256
DEPTH = 1
DEC_BATCH = 2
DEC_SEQ = 2048
PAST_LEN = 256

GRID_W = 64
N_HEADS_A = 4
DH_A = 64
DV_A = 2 * DH_A
QK_A = N_HEADS_A * 2 * DH_A
W_A = N_HEADS_A * DV_A
ROPE_BASE = 10000.0
Q_BLOCK = 128
N_HEADS_H = 4
DK_H = 128
DV_H = 128
QK_H = N_HEADS_H * DK_H
W_H = N_HEADS_H * DV_H
CHUNK = 16
D_FF = 2816
MIX_IN = 2 * QK_A + W_A + 3 * QK_H + 2 * W_H + 2 * D_MODEL
ALPHA = (2 * DEPTH) ** 0.25
INIT_BETA = (8 * DEPTH) ** -0.25
LN_EPS = 1e-5
RMS_EPS = 1e-6

kernel_name = 'hybrid_diffattn_hgrn2_dit_step'


def _layer_norm(x, g, b):
    xf = x.astype(jnp.float32)
    mu = jnp.mean(xf, axis=-1, keepdims=True)
    var = jnp.mean(jnp.square(xf - mu), axis=-1, keepdims=True)
    return ((xf - mu) * lax.rsqrt(var + LN_EPS)).astype(x.dtype) * g + b


def _rms_norm(x, g):
    xf = x.astype(jnp.float32)
    return (xf * lax.rsqrt(jnp.mean(xf * xf, axis=-1, keepdims=True) + RMS_EPS)).astype(x.dtype) * g


def _swiglu(h, w_in, w_out):
    gate, up = jnp.split(h @ w_in, 2, axis=-1)
    return (jax.nn.silu(gate) * up) @ w_out


def _split_mix(proj):
    sizes = (QK_A, QK_A, W_A, QK_H, QK_H, QK_H, W_H, W_H, D_MODEL, D_MODEL)
    points, acc = [], 0
    for s in sizes[:-1]:
        acc += s
        points.append(acc)
    return jnp.split(proj, points, axis=-1)


def _axial_rope(n_tok, dtype):
    rows = n_tok // GRID_W
    row = jnp.repeat(jnp.arange(rows, dtype=jnp.float32), GRID_W)
    col = jnp.tile(jnp.arange(GRID_W, dtype=jnp.float32), rows)
    half = DH_A // 2
    inv = ROPE_BASE ** (-jnp.arange(0, half, 2, dtype=jnp.float32) / half)
    ar = row[:, None] * inv
    ac = col[:, None] * inv
    ang = jnp.concatenate([ar, ar, ac, ac], axis=-1)
    return (jnp.cos(ang).astype(dtype)[None, :, None, None, :],
            jnp.sin(ang).astype(dtype)[None, :, None, None, :])


def _rotate_half_axial(x):
    xr, xc = jnp.split(x, 2, axis=-1)
    def rh(p):
        a, b = jnp.split(p, 2, axis=-1)
        return jnp.concatenate([-b, a], axis=-1)
    return jnp.concatenate([rh(xr), rh(xc)], axis=-1)


def _apply_rope(x, cos, sin):
    return x * cos + _rotate_half_axial(x) * sin


def _diff_attention(q, k, v, lam):
    bsz, t = q.shape[:2]
    nb = t // Q_BLOCK
    qb = q.reshape(bsz, nb, Q_BLOCK, N_HEADS_A, 2, DH_A).swapaxes(0, 1)
    scale = DH_A ** -0.5
    def one_block(qi):
        s = jnp.einsum('bqhcd,bkhcd->bhcqk', qi, k).astype(jnp.float32) * scale
        p = jax.nn.softmax(s, axis=-1)
        a = p[:, :, 0] - lam * p[:, :, 1]
        return jnp.einsum('bhqk,bkhv->bqhv', a.astype(v.dtype), v)
    o = lax.map(one_block, qb)
    return o.swapaxes(0, 1).reshape(bsz, t, N_HEADS_A, DV_A)


def _hgrn2_chunk_scan(q, k, log_f, v, s0):
    bsz, t = q.shape[:2]
    nc = t // CHUNK
    def to_chunks(a):
        return a.reshape((bsz, nc, CHUNK) + a.shape[2:]).swapaxes(0, 1)
    causal = jnp.tril(jnp.ones((CHUNK, CHUNK), dtype=bool))[None, :, :, None, None]
    def step(s, inp):
        qc, kc, gc, vc = inp
        b = jnp.cumsum(gc, axis=1)
        rel = jnp.where(causal, b[:, :, None] - b[:, None, :], -jnp.inf)
        a = jnp.einsum('bthk,bshk,btshk->bhts', qc, kc, jnp.exp(rel))
        o = (jnp.einsum('bhts,bshv->bthv', a, vc)
             + jnp.einsum('bthk,bhkv->bthv', qc * jnp.exp(b), s))
        b_end = b[:, -1]
        s_new = (jnp.exp(b_end)[..., None] * s
                 + jnp.einsum('bshk,bshv->bhkv', kc * jnp.exp(b_end[:, None] - b), vc))
        return s_new, o
    s_fin, o = lax.scan(step, s0, (to_chunks(q), to_chunks(k), to_chunks(log_f), to_chunks(v)))
    return o.swapaxes(0, 1).reshape(bsz, t, N_HEADS_H, DV_H), s_fin


def _hgrn2(hq, hf_f, hf_b, hi, hg, lb, norm_g, s0):
    bsz, t, _ = hq.shape
    shp_k = (bsz, t, N_HEADS_H, DK_H)
    q = jax.nn.silu(hq).reshape(shp_k).astype(jnp.float32)
    v = hi.reshape(bsz, t, N_HEADS_H, DV_H).astype(jnp.float32)
    outs, states = [], []
    for d, (zf, reverse) in enumerate(((hf_f, False), (hf_b, True))):
        z = zf.reshape(shp_k).astype(jnp.float32)
        lbd = lb[d].reshape(N_HEADS_H, DK_H)
        log_f = jnp.logaddexp(jnp.log(lbd), jnp.log1p(-lbd) + jax.nn.log_sigmoid(z))
        kd = (1.0 - lbd) * jax.nn.sigmoid(-z)
        qd, vd = q, v
        if reverse:
            qd, kd, log_f, vd = (jnp.flip(a, axis=1) for a in (qd, kd, log_f, vd))
        o, s = _hgrn2_chunk_scan(qd, kd, log_f, vd, s0[:, d].astype(jnp.float32))
        if reverse:
            o = jnp.flip(o, axis=1)
        outs.append(o)
        states.append(s)
    o = _rms_norm(outs[0] + outs[1], norm_g.astype(jnp.float32))
    o = (o * jax.nn.silu(hg.reshape(bsz, t, N_HEADS_H, DV_H).astype(jnp.float32))).astype(hq.dtype)
    return o.reshape(bsz, t, W_H), jnp.stack(states, axis=1).astype(hq.dtype)


def _token_mixer(h, lw, lam_init, ctx):
    bsz, t, _ = h.shape
    aq, ak, av, hq, hf_f, hf_b, hi, hg, ga, gh = _split_mix(h @ lw['w_mix_in'])
    q = aq.reshape(bsz, t, N_HEADS_A, 2, DH_A)
    k = ak.reshape(bsz, t, N_HEADS_A, 2, DH_A)
    v = av.reshape(bsz, t, N_HEADS_A, DV_A)
    if ctx is None:
        k_all, v_all = k, v
        s0 = jnp.zeros((bsz, 2, N_HEADS_H, DK_H, DV_H), h.dtype)
    else:
        k_ctx, v_ctx, s0 = ctx
        cos, sin = _axial_rope(t, h.dtype)
        q = _apply_rope(q, cos, sin)
        k_lat = _apply_rope(k, cos, sin)
        k_all = jnp.concatenate([k_ctx.astype(h.dtype), k_lat], axis=1)
        v_all = jnp.concatenate([v_ctx.astype(h.dtype), v], axis=1)
    lam = (jnp.exp(jnp.sum(lw['lambda_q1'].astype(jnp.float32) * lw['lambda_k1'].astype(jnp.float32)))
           - jnp.exp(jnp.sum(lw['lambda_q2'].astype(jnp.float32) * lw['lambda_k2'].astype(jnp.float32)))
           + lam_init)
    o_a = _diff_attention(q, k_all, v_all, lam)
    o_a = (_rms_norm(o_a, lw['attn_subln_g']) * (1.0 - lam_init)).reshape(bsz, t, W_A)
    o_h, s_fin = _hgrn2(hq, hf_f, hf_b, hi, hg, lw['lb'], lw['hgrn_norm_g'], s0)
    merged = (jax.nn.sigmoid(ga) * (o_a @ lw['w_branch_a'])
              + jax.nn.sigmoid(gh) * (o_h @ lw['w_branch_h']))
    out = merged @ lw['w_mix_out']
    ctx_tensors = (k, v, s_fin) if ctx is None else None
    return out, ctx_tensors


def _layer(x, cond, lw, lam_init, ctx):
    nb = cond.shape[0]
    mod = (jax.nn.silu(cond) @ lw['w_ada'] + lw['b_ada']).reshape(nb, 3, 3, 1, D_MODEL)
    def modulate(hh, s):
        return hh * (1.0 + mod[:, s, 1]) + mod[:, s, 0]
    f1 = _swiglu(modulate(x, 0), lw['ffn1_w_in'], lw['ffn1_w_out'])
    x = _layer_norm(ALPHA * x + 0.5 * mod[:, 0, 2] * f1, lw['ln_g'][0], lw['ln_b'][0])
    mix, ctx_tensors = _token_mixer(modulate(x, 1), lw, lam_init, ctx)
    x = _layer_norm(ALPHA * x + mod[:, 1, 2] * mix, lw['ln_g'][1], lw['ln_b'][1])
    f2 = _swiglu(modulate(x, 2), lw['ffn2_w_in'], lw['ffn2_w_out'])
    x = _layer_norm(ALPHA * x + 0.5 * mod[:, 2, 2] * f2, lw['ln_g'][2], lw['ln_b'][2])
    return x, ctx_tensors


def setup_inputs(seed: int = 0) -> dict:
    key = jax.random.key(seed)
    ks = jax.random.split(key, 32)
    nrm = jax.random.normal
    f32 = jnp.float32
    return {
        'x_prompt': nrm(ks[0], (BATCH, SEQ, D_MODEL), f32),
        'x_sample': nrm(ks[1], (DEC_BATCH, DEC_SEQ, D_MODEL), f32),
        'cache_k': nrm(ks[2], (DEC_BATCH, DEPTH, PAST_LEN, N_HEADS_A, 2, DH_A), f32),
        'cache_v': nrm(ks[3], (DEC_BATCH, DEPTH, PAST_LEN, N_HEADS_A, DV_A), f32),
        'state_hgrn': nrm(ks[4], (DEC_BATCH, DEPTH, 2, N_HEADS_H, DK_H, DV_H), f32),
        'c': nrm(ks[5], (DEC_BATCH, D_MODEL), f32),
        'c_ctx': nrm(ks[6], (D_MODEL,), f32),
        'w_ada': nrm(ks[7], (DEPTH, D_MODEL, 9 * D_MODEL), f32) * D_MODEL ** -0.5,
        'b_ada': 0.02 * nrm(ks[8], (DEPTH, 9 * D_MODEL), f32),
        'ffn1_w_in': nrm(ks[9], (DEPTH, D_MODEL, 2 * D_FF), f32) * D_MODEL ** -0.5,
        'ffn1_w_out': nrm(ks[10], (DEPTH, D_FF, D_MODEL), f32) * (D_FF ** -0.5 * INIT_BETA),
        'w_mix_in': nrm(ks[11], (DEPTH, D_MODEL, MIX_IN), f32) * D_MODEL ** -0.5,
        'lambda_q1': 0.1 * nrm(ks[12], (DEPTH, DH_A), f32),
        'lambda_k1': 0.1 * nrm(ks[13], (DEPTH, DH_A), f32),
        'lambda_q2': 0.1 * nrm(ks[14], (DEPTH, DH_A), f32),
        'lambda_k2': 0.1 * nrm(ks[15], (DEPTH, DH_A), f32),
        'attn_subln_g': 1.0 + 0.02 * nrm(ks[16], (DEPTH, DV_A), f32),
        'hgrn_lb_logits': 0.5 * nrm(ks[17], (2, DEPTH + 1, QK_H), f32),
        'hgrn_norm_g': 1.0 + 0.02 * nrm(ks[18], (DEPTH, DV_H), f32),
        'w_branch_a': nrm(ks[19], (DEPTH, W_A, D_MODEL), f32) * W_A ** -0.5,
        'w_branch_h': nrm(ks[20], (DEPTH, W_H, D_MODEL), f32) * W_H ** -0.5,
        'w_mix_out': nrm(ks[21], (DEPTH, D_MODEL, D_MODEL), f32) * (D_MODEL ** -0.5 * INIT_BETA),
        'ffn2_w_in': nrm(ks[22], (DEPTH, D_MODEL, 2 * D_FF), f32) * D_MODEL ** -0.5,
        'ffn2_w_out': nrm(ks[23], (DEPTH, D_FF, D_MODEL), f32) * (D_FF ** -0.5 * INIT_BETA),
        'ln_g': 1.0 + 0.02 * nrm(ks[24], (DEPTH, 3, D_MODEL), f32),
        'ln_b': 0.02 * nrm(ks[25], (DEPTH, 3, D_MODEL), f32),
    }


def reference(x_prompt, x_sample, cache_k, cache_v, state_hgrn, c, c_ctx,
              w_ada, b_ada, ffn1_w_in, ffn1_w_out, w_mix_in,
              lambda_q1, lambda_k1, lambda_q2, lambda_k2, attn_subln_g,
              hgrn_lb_logits, hgrn_norm_g, w_branch_a, w_branch_h, w_mix_out,
              ffn2_w_in, ffn2_w_out, ln_g, ln_b):
    lb_all = jnp.cumsum(jax.nn.softmax(hgrn_lb_logits.astype(jnp.float32), axis=1), axis=1)
    layers = []
    for l in range(DEPTH):
        layers.append({
            'w_ada': w_ada[l], 'b_ada': b_ada[l],
            'ffn1_w_in': ffn1_w_in[l], 'ffn1_w_out': ffn1_w_out[l],
            'w_mix_in': w_mix_in[l],
            'lambda_q1': lambda_q1[l], 'lambda_k1': lambda_k1[l],
            'lambda_q2': lambda_q2[l], 'lambda_k2': lambda_k2[l],
            'attn_subln_g': attn_subln_g[l],
            'lb': lb_all[:, l], 'hgrn_norm_g': hgrn_norm_g[l],
            'w_branch_a': w_branch_a[l], 'w_branch_h': w_branch_h[l], 'w_mix_out': w_mix_out[l],
            'ffn2_w_in': ffn2_w_in[l], 'ffn2_w_out': ffn2_w_out[l],
            'ln_g': ln_g[l], 'ln_b': ln_b[l],
        })
    lam_inits = [0.8 - 0.6 * math.exp(-0.3 * l) for l in range(DEPTH)]

    xp = x_prompt
    cond_ctx = c_ctx[None, :]
    ks_list, vs_list, ss_list = [], [], []
    for l in range(DEPTH):
        xp, (k_l, v_l, s_l) = _layer(xp, cond_ctx, layers[l], lam_inits[l], None)
        ks_list.append(k_l)
        vs_list.append(v_l)
        ss_list.append(s_l)
    y_prompt = xp
    new_cache_k = jnp.stack(ks_list, axis=1)
    new_cache_v = jnp.stack(vs_list, axis=1)
    new_state_hgrn = jnp.stack(ss_list, axis=1)

    xs = x_sample
    for l in range(DEPTH):
        xs, _ = _layer(xs, c, layers[l], lam_inits[l],
                       (cache_k[:, l], cache_v[:, l], state_hgrn[:, l]))
    y_sample = xs
    return (y_prompt, y_sample, new_cache_k, new_cache_v, new_state_hgrn)
```

```python
import numpy as np
import concourse.bass as bass
import concourse.mybir as mybir
from concourse.bass_utils import run_bass_kernel_spmd

F32 = mybir.dt.float32
BF16 = mybir.dt.bfloat16
AF = mybir.ActivationFunctionType
ALU = mybir.AluOpType

D = 1024
NC8 = 8
T = 1024
TK = 2048
TB = 512
NTB = T // TB
DFF = 2816
NF = DFF // 128
ALPHA = 2.0 ** 0.25
LN_EPS = 1e-5
STAGE = 1


class Op:
    __slots__ = ("idx", "eng", "fn", "dma", "deps", "needs_inc", "sem", "tick", "waits", "cc")

    def __init__(self, idx, eng, fn, dma):
        self.idx = idx
        self.eng = eng
        self.fn = fn
        self.dma = dma
        self.deps = set()
        self.needs_inc = False
        self.sem = None
        self.tick = 0
        self.waits = []
        self.cc = None


class Prog:
    ENGS = ("pe", "act", "dve", "pool", "sp")
    NDSEM = 8

    def __init__(self):
        self.ops = []
        self.last_w = {}
        self.last_r = {}
        self.bar = None
        self.bar_start = 0

    def op(self, eng, fn, r=(), w=(), dma=False, cc=None):
        if cc is not None:
            dma = True
        o = Op(len(self.ops), eng, fn, dma)
        o.cc = cc
        for k in r:
            lw = self.last_w.get(k)
            if lw is not None:
                o.deps.add(lw)
        for k in w:
            lw = self.last_w.get(k)
            if lw is not None:
                o.deps.add(lw)
            for ridx in self.last_r.get(k, {}).values():
                o.deps.add(ridx)
        rk = ("dma", o.idx) if dma else eng
        for k in r:
            self.last_r.setdefault(k, {})[rk] = o.idx
        for k in w:
            self.last_w[k] = o.idx
            self.last_r[k] = {}
        if self.bar is not None:
            o.deps.add(self.bar)
        o.deps.discard(o.idx)
        self.ops.append(o)
        return o

    def barrier(self, fn):
        o = Op(len(self.ops), "dve", fn, False)
        last = {}
        for p in self.ops[self.bar_start:]:
            if p.dma:
                o.deps.add(p.idx)
            else:
                last[p.eng] = p.idx
        for v in last.values():
            o.deps.add(v)
        if self.bar is not None:
            o.deps.add(self.bar)
        self.ops.append(o)
        self.bar = o.idx
        self.bar_start = o.idx
        self.last_w = {}
        self.last_r = {}
        return o

    def finalize(self, nc, sems, dsems):
        ops = self.ops
        for o in ops:
            for d in o.deps:
                do = ops[d]
                if do.dma:
                    continue
                if do.eng == "pe" and o.eng == "pe" and not o.dma:
                    continue
                do.needs_inc = True
        tick = {e: 0 for e in self.ENGS}
        dcount = {e: 0 for e in self.ENGS}
        dtick = {}
        prev_on_sem = {}
        for o in ops:
            if o.cc is not None:
                o.sem = o.cc
                o.tick = 1
            elif o.dma:
                q = o.eng
                si = dcount[q] % self.NDSEM
                dcount[q] += 1
                s = dsems[q][si]
                o.sem = s
                key = (q, si)
                dtick[key] = dtick.get(key, 0) + 16
                o.tick = dtick[key]
                if key in prev_on_sem:
                    o.waits.append((s, prev_on_sem[key]))
                prev_on_sem[key] = o.tick
            elif o.needs_inc:
                tick[o.eng] += 1
                o.sem = sems[o.eng]
                o.tick = tick[o.eng]
        for o in ops:
            for d in sorted(o.deps):
                do = ops[d]
                if do.dma:
                    o.waits.append((do.sem, do.tick))
                elif do.eng == "pe" and o.eng == "pe" and not o.dma:
                    continue
                else:
                    o.waits.append((do.sem, do.tick))
        self.final_dma = {k: v for k, v in dtick.items()}
        self.dsems = dsems

    def emit(self, eng, handle):
        waited = {}
        for o in self.ops:
            if o.eng != eng:
                continue
            for (s, t) in o.waits:
                sid = id(s)
                if waited.get(sid, 0) >= t:
                    continue
                waited[sid] = t
                handle.wait_ge(s, t)
            inst = o.fn(handle)
            if o.cc is not None:
                inst.then_inc(o.sem, 1)
            elif o.dma:
                inst.then_inc(o.sem, 16)
            elif o.needs_inc:
                inst.then_inc(o.sem, 1)
        if eng == "sp":
            for (q, si), t in self.final_dma.items():
                handle.wait_ge(self.dsems[q][si], t)


NH = 4
CTX = 256
NKT = (CTX + TK) // 128
CH = 32
NCH = T // CH
SEQ = 256
BIG = 65536.0
LAM_INIT = 0.2
RMS_EPS = 1e-6
ARENA_BYTES = 128 * 1024
PAIRS = [[0, 1], [2, 3], [4, 5], [6, 7]]
DEBUG = False


class Arena:
    def __init__(self, hb, hf):
        self.hb = hb
        self.hf = hf
        self.off = 0

    def mark(self):
        return self.off

    def reset(self, m=0):
        self.off = m

    def alloc(self, shape, dt):
        n = 1
        for v in shape:
            n *= v
        esz = 4 if dt == F32 else 2
        self.off = (self.off + 63) // 64 * 64
        o = self.off
        self.off += n * esz
        assert self.off <= ARENA_BYTES, ("arena overflow", self.off)
        if dt == F32:
            ap = self.hf[:, o // 4:o // 4 + n]
        else:
            ap = self.hb[:, o // 2:o // 2 + n]
        if len(shape) == 1:
            return ap
        if len(shape) == 2:
            return ap.rearrange("p (a b) -> p a b", b=shape[1])
        if len(shape) == 3:
            return ap.rearrange("p (a b c) -> p a b c", b=shape[1], c=shape[2])
        raise ValueError(shape)


def build_program():
    nc = bass.Bass("TRN2", target_bir_lowering=False)
    P = Prog()

    def din(name, shape, dt=F32):
        return nc.dram_tensor(name, list(shape), dt, kind="ExternalInput").ap()

    def dout(name, shape, dt=F32):
        return nc.dram_tensor(name, list(shape), dt, kind="ExternalOutput").ap()

    xT_d = din("xT", [D, T])
    cond_d = din("cond", [128, NC8])
    w_ada_d = din("w_ada", [D, 3072])
    w_ada0_d = din("w_ada0", [D, 3072])
    b_ada_d = din("b_ada", [128, 36])
    b_ada0_d = din("b_ada0", [128, 24])
    mb_src = nc.dram_tensor("mb_src", [128, 12], F32, kind="Internal").ap()
    mb_dst = nc.dram_tensor("mb_dst", [256, 12], F32, kind="Internal").ap()
    mc_src = nc.dram_tensor("mc_src", [128, 12], F32, kind="Internal").ap()
    mc_dst = nc.dram_tensor("mc_dst", [256, 12], F32, kind="Internal").ap()
    f1_win_d = din("ffn1_w_in", [D, 2 * DFF])
    f1_wout_d = din("ffn1_w_out", [DFF, D])
    f2_win_d = din("ffn2_w_in", [D, 2 * DFF])
    f2_wout_d = din("ffn2_w_out", [DFF, D])
    lng_d = din("ln_g", [128, 24])
    lnb_d = din("ln_b", [128, 24])
    wmix_d = din("w_mix_in", [D, 6144])
    wbra_d = din("w_branch_a", [512, D])
    wbrh_d = din("w_branch_h", [512, D])
    wmo_d = din("w_mix_out", [D, D])
    kctx_d = din("kctxT", [512, CTX])
    vctx_d = din("vctx", [CTX, 512])
    s0_d = din("s0", [2, NH, 128, 128])
    cos_d = din("cosT", [128, T])
    sin_d = din("sinT", [128, T])
    mbias_d = din("mbias", [128, NKT * (T // 256)])
    xw_d = din("xw", [128, 3])
    xk_src = nc.dram_tensor("xk_src", [512, T], BF16, kind="Internal").ap()
    xk_dst = nc.dram_tensor("xk_dst", [1024, T], BF16, kind="Internal").ap()
    xv_src = nc.dram_tensor("xv_src", [T, 512], BF16, kind="Internal").ap()
    xv_dst = nc.dram_tensor("xv_dst", [2 * T, 512], BF16, kind="Internal").ap()
    xs_src = nc.dram_tensor("xs_src", [512, 128], F32, kind="Internal").ap()
    xs_dst = nc.dram_tensor("xs_dst", [1024, 128], F32, kind="Internal").ap()
    keep_d = din("keep", [128, 1])
    lam_d = din("lamv", [128, 4, 64])
    subg_d = din("subln_g", [128, 1])
    hng_d = din("hnorm_g", [128, 1])
    lbl_d = din("lb_logits", [128, 2, 2, NH])
    ident_d = din("ident", [128, 128])
    rm_d = din("rotm", [128, 128])
    mf_d = din("maskf", [128, 128])
    mb_d = din("maskb", [128, 128])
    rowm_d = din("rowmask", [128, 4])

    yT_d = dout("yT", [D, T])
    kout_d = dout("kT_out", [512, T])
    vout_d = dout("v_out", [T, 512])
    sout_d = dout("s_out", [T // SEQ, 2, NH, 128, 128])
    if DEBUG:
        dbg_oa_d = dout("dbg_oa", [512, T])
        dbg_oh_d = dout("dbg_oh", [512, T])

    from contextlib import ExitStack
    es = ExitStack()

    def sb(name, shape, dt):
        return es.enter_context(nc.sbuf_tensor(name, list(shape), dt))

    def ps(name, shape, dt=F32):
        return es.enter_context(nc.psum_tensor(name, list(shape), dt))

    with es:
        x = sb("x", [128, NC8, T], F32)
        hmod = sb("hmod", [128, NC8, T], BF16)
        arena_b = sb("arena", [128, ARENA_BYTES // 2], BF16)
        arena_f = arena_b.bitcast(F32)
        AR = Arena(arena_b, arena_f)
        cond_sb = sb("cond_sb", [128, NC8], F32)
        scond = sb("scond", [128, NC8], BF16)
        bada = sb("bada", [128, 36], F32)
        modh = sb("modh", [128, 36], F32)
        bada0 = sb("bada0", [128, 24], F32)
        mod = sb("mod", [128, 72], F32)
        lng = sb("lng", [128, 24], F32)
        lnb = sb("lnb", [128, 24], F32)
        sc1 = sb("sc1", [128, 24], F32)
        gp = sb("gp", [128, 24], F32)
        lnA = sb("lnA", [128, 24], F32)
        lnB = sb("lnB", [128, 24], F32)
        ones_bf = sb("ones_bf", [128, 128], BF16)
        tmpf = [sb(f"tmpf{i}", [128, TB], F32) for i in range(9)]
        sg = tmpf[0:2]
        xn = tmpf[2:4]
        mean, msq, var, rstd, nmr = tmpf[4:9]
        rb = [sb(f"rb{i}", [128, TB], BF16) for i in range(2)]
        rsq = [sb(f"rsq{i}", [128, TB], BF16) for i in range(2)]
        ident = sb("ident_sb", [128, 128], BF16)
        rotm = sb("rotm_sb", [128, 128], F32)
        maskf = sb("maskf_sb", [128, 128], F32)
        maskb = sb("maskb_sb", [128, 128], F32)
        keep = sb("keep_sb", [128, 1], F32)
        xw = sb("xw_sb", [128, 3], F32)
        mbias = sb("mbias_sb", [128, NKT * (T // 256)], F32)
        rowmask = sb("rowmask_sb", [128, 4], F32)
        lamv = sb("lamv_sb", [128, 4, 64], F32)
        lamt = sb("lamt", [128, 2, 64], F32)
        lams = sb("lams", [128, 4], F32)
        neglam = sb("neglam", [128, 1], F32)
        subg = sb("subg", [128, 1], F32)
        hng = sb("hng", [128, 1], F32)
        lbl = sb("lbl", [128, 2, 2, NH], F32)
        lbv = sb("lbv", [128, 2, NH], F32)
        olb = sb("olb", [128, 2, NH], F32)
        dbar = sb("dbar", [128, 1], F32)
        pb = [ps(f"pb{i}", [128, 512]) for i in range(8)]
        pbb = [p_.bitcast(BF16) for p_ in pb]
        wada = [hmod[:, 4 * i:4 * i + 4, :].rearrange("p a (b n) -> p (a b) n", n=512) for i in range(2)]

        sems = {e: es.enter_context(nc.semaphore("s_" + e)) for e in ("pe", "act", "dve", "pool")}
        ccsem = [es.enter_context(nc.semaphore(f"cc{i}")) for i in range(5)]
        dsems = {q: [es.enter_context(nc.semaphore(f"d_{q}{i}")) for i in range(Prog.NDSEM)]
                 for q in ("sp", "pool", "act")}

        def DMA(q, out, in_, r=(), w=()):
            P.op(q, lambda h, out=out, in_=in_: h.dma_start(out=out, in_=in_), r=r, w=w, dma=True)

        def MM(out, lhsT, rhs, start, stop, r=(), w=()):
            P.op("pe", lambda h, out=out, lhsT=lhsT, rhs=rhs, start=start, stop=stop:
                 h.matmul(out, lhsT=lhsT, rhs=rhs, start=start, stop=stop), r=r, w=w)

        def TR(out, in_, r=(), w=()):
            P.op("pe", lambda h, out=out, in_=in_: h.transpose(out, in_, ident[:, :]), r=list(r) + ["ident"], w=w)

        def ACT(out, in_, func, r=(), w=(), bias=None, scale=None):
            def fn(h, out=out, in_=in_, func=func, bias=bias, scale=scale):
                kw = {}
                if bias is not None:
                    kw["bias"] = bias
                if scale is not None:
                    kw["scale"] = scale
                return h.activation(out=out, in_=in_, func=func, **kw)
            P.op("act", fn, r=r, w=w)

        def TS(eng, out, in0, s1, s2, op0, op1, r=(), w=()):
            P.op(eng, lambda h, out=out, in0=in0, s1=s1, s2=s2, op0=op0, op1=op1:
                 h.tensor_scalar(out=out, in0=in0, scalar1=s1, scalar2=s2, op0=op0, op1=op1), r=r, w=w)

        def TT(eng, out, in0, in1, op, r=(), w=()):
            P.op(eng, lambda h, out=out, in0=in0, in1=in1, op=op:
                 h.tensor_tensor(out=out, in0=in0, in1=in1, op=op), r=r, w=w)

        def STT(out, in0, scalar, in1, op0, op1, r=(), w=()):
            P.op("dve", lambda h, out=out, in0=in0, scalar=scalar, in1=in1, op0=op0, op1=op1:
                 h.scalar_tensor_tensor(out=out, in0=in0, scalar=scalar, in1=in1, op0=op0, op1=op1), r=r, w=w)

        def RECIP(out, in_, r=(), w=()):
            P.op("dve", lambda h, out=out, in_=in_: h.reciprocal(out=out, in_=in_), r=r, w=w)

        def BARRIER():
            P.barrier(lambda h: h.memset(dbar[:, :], 0.0))

        def tok(tb):
            return slice(tb * TB, (tb + 1) * TB)

        def wview(w_d, c0, n):
            return w_d[:, c0:c0 + n].rearrange("(k p) n -> p k n", p=128)

        P.op("dve", lambda h: h.memset(ones_bf[:, :], 1.0), w=["ones"])
        DMA("sp", cond_sb[:, :], cond_d[:, :], w=["cond"])
        DMA("sp", bada[:, :], b_ada_d[:, :], w=["bada"])
        DMA("sp", bada0[:, :], b_ada0_d[:, :], w=["bada0"])
        DMA("sp", lng[:, :], lng_d[:, :], w=["lng"])
        DMA("sp", lnb[:, :], lnb_d[:, :], w=["lnb"])
        for tb in range(NTB):
            for c in range(NC8):
                DMA("sp", x[:, c, tok(tb)], xT_d[c * 128:(c + 1) * 128, tok(tb)], w=[("x", c, tb)])
        DMA("sp", rotm[:, :], rm_d[:, :], w=["rotm"])
        DMA("sp", maskf[:, :], mf_d[:, :], w=["maskf"])
        DMA("sp", maskb[:, :], mb_d[:, :], w=["maskb"])
        DMA("sp", keep[:, :], keep_d[:, :], w=["keep"])
        DMA("sp", xw[:, :], xw_d[:, :], w=["xw"])
        DMA("sp", mbias[:, :], mbias_d[:, :], w=["mbias"])
        DMA("sp", rowmask[:, :], rowm_d[:, :], w=["rowmask"])
        DMA("sp", lamv[:, :, :], lam_d[:, :, :], w=["lamv"])
        DMA("sp", subg[:, :], subg_d[:, :], w=["subg"])
        DMA("sp", hng[:, :], hng_d[:, :], w=["hng"])
        DMA("sp", lbl[:, :, :, :], lbl_d[:, :, :, :], w=["lbl"])
        DMA("pool", ident[:, :], ident_d[:, :], w=["ident"])
        ACT(scond[:, :], cond_sb[:, :], AF.Silu, r=["cond"], w=["scond"])

        TT("dve", lamt[:, 0, :], lamv[:, 0, :], lamv[:, 1, :], ALU.mult, r=["lamv"], w=["lamt"])
        TT("dve", lamt[:, 1, :], lamv[:, 2, :], lamv[:, 3, :], ALU.mult, r=["lamv", "lamt"], w=["lamt"])
        for i in range(2):
            P.op("dve", lambda h, i=i: h.tensor_reduce(out=lams[:, i:i + 1], in_=lamt[:, i, :],
                                                       axis=mybir.AxisListType.X, op=ALU.add),
                 r=["lamt"], w=[("lams", i)])
        ACT(lams[:, 2:4], lams[:, 0:2], AF.Exp, r=[("lams", 0), ("lams", 1)], w=[("lams", 2)])
        TT("dve", neglam[:, :], lams[:, 3:4], lams[:, 2:3], ALU.subtract, r=[("lams", 2)], w=["neglam"])
        TS("dve", neglam[:, :], neglam[:, :], -LAM_INIT, None, ALU.add, ALU.bypass, r=["neglam"], w=["neglam"])
        TS("dve", subg[:, :], subg[:, :], 1.0 - LAM_INIT, None, ALU.mult, ALU.bypass, r=["subg"], w=["subg"])
        TT("dve", lbv[:, :, :], lbl[:, :, 0, :], lbl[:, :, 1, :], ALU.subtract, r=["lbl"], w=["lbv"])
        ACT(lbv[:, :, :], lbv[:, :, :], AF.Sigmoid, r=["lbv"], w=["lbv"])
        TS("dve", olb[:, :, :], lbv[:, :, :], -1.0, 1.0, ALU.mult, ALU.add, r=["lbv"], w=["olb"])

        state = {"win": 0, "wout": 0, "gu": 0, "dn": 0, "ln": 0}
        WIN_COLS = 256
        WOUT_COLS = 256
        def adaln_blocks(blks, stage, bank, col0, src=None):
            for blk in blks:
                slot = blk % 2
                if src is None:
                    DMA("pool", stage[slot], wview(w_ada_d, (blk - 3) * 512, 512), w=[("wada", slot)])
                else:
                    DMA("pool", stage[slot], wview(src, blk * 512, 512), w=[("wada", slot)])
                for jj in range(4):
                    j = blk * 4 + jj - col0
                    for k in range(NC8):
                        MM(pb[bank][:, j:j + 1], stage[slot][:, k, jj * 128:(jj + 1) * 128], scond[:, k:k + 1],
                           start=(k == 0), stop=(k == NC8 - 1),
                           r=[("wada", slot), "scond"], w=[("ps", bank)])

        def adaln_derive(s):
            base = s * 24
            gsc = (0.5 if s != 1 else 1.0) / ALPHA
            TS("dve", sc1[:, s * 8:(s + 1) * 8], mod[:, base + 8:base + 16], 1.0, None, ALU.add, ALU.bypass,
               r=[("mod", s)], w=[("sc1", s)])
            TS("dve", gp[:, s * 8:(s + 1) * 8], mod[:, base + 16:base + 24], gsc, None, ALU.mult, ALU.bypass,
               r=[("mod", s)], w=[("gp", s)])

        adaln_blocks(range(6), wada, 0, 0, src=w_ada0_d)
        TT("dve", mod[:, 0:24], pb[0][:, 0:24], bada0[:, :], ALU.add, r=[("ps", 0), "bada0"], w=[("mod", 0)])
        adaln_derive(0)
        pre_win = {}

        def prefetch_win(blk, win_g, win_u):
            slot = state["win"] % 2
            state["win"] += 1
            c0 = blk * WIN_COLS
            DMA("pool", win_g[slot], wview(f1_win_d, c0, WIN_COLS), w=[("wing", slot)])
            DMA("pool", win_u[slot], wview(f1_win_d, DFF + c0, WIN_COLS), w=[("winu", slot)])
            pre_win[blk] = slot

        PRE = {"fn": prefetch_win}
        def adaln_gather_a():
            pass

        rest_state = {}

        def ln_ab(s):
            TT("dve", lnA[:, s * 8:(s + 1) * 8], lng[:, s * 8:(s + 1) * 8], sc1[:, (s + 1) * 8:(s + 2) * 8],
               ALU.mult, r=["lng", ("sc1", s + 1)], w=[("lnA", s)])
            TT("dve", lnB[:, s * 8:(s + 1) * 8], lnb[:, s * 8:(s + 1) * 8], sc1[:, (s + 1) * 8:(s + 2) * 8],
               ALU.mult, r=["lnb", ("sc1", s + 1)], w=[("lnB", s)])
            TT("dve", lnB[:, s * 8:(s + 1) * 8], lnB[:, s * 8:(s + 1) * 8],
               mod[:, (s + 1) * 24:(s + 1) * 24 + 8], ALU.add,
               r=[("lnB", s), ("mod", s + 1)], w=[("lnB", s)])

        def adaln_gather(s, src_d, dst_d, sem):
            DMA("sp", src_d[:, :], modh[:, s * 12:(s + 1) * 12], r=[("modh", s)], w=[("mbs", s)])
            P.op("pool", lambda hh: hh.collective_compute("AllGather", ALU.bypass, replica_groups=PAIRS,
                                                          ins=[src_d], outs=[dst_d]),
                 r=[("mbs", s)], w=[("mbd", s)], cc=sem)
            for rk in range(2):
                DMA("sp", mod[:, s * 24 + rk * 12:s * 24 + (rk + 1) * 12], dst_d[rk * 128:(rk + 1) * 128, :],
                    r=[("mbd", s)], w=[("mod", s)])
            adaln_derive(s)
            ln_ab(s - 1)

        def adaln_rest_group(g):
            if g == 0:
                rest_state["stage"] = [AR.alloc([NC8, 512], BF16) for _ in range(2)]
                adaln_blocks([3, 4], rest_state["stage"], 7, 12)
            elif g == 1:
                adaln_blocks([5], rest_state["stage"], 7, 12)
            else:
                TT("dve", modh[:, 12:24], pb[7][:, 0:12], bada[:, 12:24], ALU.add, r=[("ps", 7), "bada"],
                   w=[("modh", 1)])
                adaln_gather(1, mb_src, mb_dst, ccsem[4])


        def layer_norm(s, tbs):
            for tb in tbs:
                for c in range(NC8):
                    bi = state["ln"] % 2
                    state["ln"] += 1
                    ACT(rb[bi][:, :], x[:, c, tok(tb)], AF.Copy, r=[("x", c, tb)], w=[("rb", bi)])
                    ACT(rsq[bi][:, :], x[:, c, tok(tb)], AF.Square, r=[("x", c, tb)], w=[("rsq", bi)])
                    MM(pb[6][:, :], ones_bf[:, :], rb[bi][:, :], start=(c == 0), stop=(c == NC8 - 1),
                       r=["ones", ("rb", bi)], w=[("ps", 6)])
                    MM(pb[7][:, :], ones_bf[:, :], rsq[bi][:, :], start=(c == 0), stop=(c == NC8 - 1),
                       r=["ones", ("rsq", bi)], w=[("ps", 7)])
                TS("dve", mean[:, :], pb[6][:, :], 1.0 / D, None, ALU.mult, ALU.bypass, r=[("ps", 6)], w=[("tmpf", 4)])
                TT("dve", msq[:, :], mean[:, :], mean[:, :], ALU.mult, r=[("tmpf", 4)], w=[("tmpf", 5)])
                STT(var[:, :], pb[7][:, :], 1.0 / D, msq[:, :], ALU.mult, ALU.subtract,
                    r=[("ps", 7), ("tmpf", 5)], w=[("tmpf", 6)])
                ACT(var[:, :], var[:, :], AF.Sqrt, r=[("tmpf", 6)], w=[("tmpf", 6)], bias=LN_EPS / (ALPHA * ALPHA))
                RECIP(rstd[:, :], var[:, :], r=[("tmpf", 6)], w=[("tmpf", 7)])
                STT(nmr[:, :], mean[:, :], -1.0, rstd[:, :], ALU.mult, ALU.mult, r=[("tmpf", 4), ("tmpf", 7)], w=[("tmpf", 8)])
                for c in range(NC8):
                    xi = c % 2
                    TT("dve", xn[xi][:, :], x[:, c, tok(tb)], rstd[:, :], ALU.mult,
                       r=[("x", c, tb), ("tmpf", 7)], w=[("tmpf", 2 + xi)])
                    TT("dve", xn[xi][:, :], xn[xi][:, :], nmr[:, :], ALU.add, r=[("tmpf", 2 + xi), ("tmpf", 8)], w=[("tmpf", 2 + xi)])
                    ACT(x[:, c, tok(tb)], xn[xi][:, :], AF.Identity, r=[("tmpf", 2 + xi), "lng", "lnb"], w=[("x", c, tb)],
                        scale=lng[:, s * 8 + c:s * 8 + c + 1], bias=lnb[:, s * 8 + c:s * 8 + c + 1])
                    if s < 2:
                        ACT(hmod[:, c, tok(tb)], xn[xi][:, :], AF.Identity,
                            r=[("tmpf", 2 + xi), ("lnA", s), ("lnB", s)], w=[("hmod", c, tb)],
                            scale=lnA[:, s * 8 + c:s * 8 + c + 1], bias=lnB[:, s * 8 + c:s * 8 + c + 1])

        def ffn_ln(s, win_d, wout_d, hook=None, pre=None, early_barrier=False):
            AR.reset(0)
            hbuf = AR.alloc([NF, 1024], BF16)
            win_g = [AR.alloc([NC8, WIN_COLS], BF16) for _ in range(2)]
            win_u = [AR.alloc([NC8, WIN_COLS], BF16) for _ in range(2)]
            wout = AR.alloc([NF, D], BF16)
            tbs = list(range(NTB))
            if pre is not None:
                pre(win_g, win_u)
            for blk in range(DFF // WIN_COLS):
                c0 = blk * WIN_COLS
                if blk in pre_win:
                    slot = pre_win.pop(blk)
                else:
                    slot = state["win"] % 2
                    state["win"] += 1
                    DMA("pool", win_g[slot], wview(win_d, c0, WIN_COLS), w=[("wing", slot)])
                    DMA("pool", win_u[slot], wview(win_d, DFF + c0, WIN_COLS), w=[("winu", slot)])
                wo0 = 3 if s == 0 else 1
                if wo0 <= blk < wo0 + 8:
                    p8 = blk - wo0
                    q4, jh = p8 // 2, p8 % 2
                    j0 = jh * (NF // 2)
                    DMA("pool", wout[:, j0:j0 + NF // 2, q4 * 256:(q4 + 1) * 256],
                        wout_d[j0 * 128:(j0 + NF // 2) * 128, q4 * 256:(q4 + 1) * 256].rearrange(
                            "(j p) n -> p j n", p=128), w=[("wout", q4, jh)])
                for jj in range(WIN_COLS // 128):
                    j = blk * (WIN_COLS // 128) + jj
                    for tb in tbs:
                        gi = state["gu"] % 2
                        state["gu"] += 1
                        pg, pu = pb[gi], pb[2 + gi]
                        for k in range(NC8):
                            MM(pg[:, :], win_g[slot][:, k, jj * 128:(jj + 1) * 128], hmod[:, k, tok(tb)],
                               start=(k == 0), stop=(k == NC8 - 1),
                               r=[("wing", slot), ("hmod", k, tb)], w=[("ps", gi)])
                        for k in range(NC8):
                            MM(pu[:, :], win_u[slot][:, k, jj * 128:(jj + 1) * 128], hmod[:, k, tok(tb)],
                               start=(k == 0), stop=(k == NC8 - 1),
                               r=[("winu", slot), ("hmod", k, tb)], w=[("ps", 2 + gi)])
                        ACT(sg[gi][:, :], pg[:, :], AF.Silu, r=[("ps", gi)], w=[("tmpf", gi)])
                        TT("dve", hbuf[:, j, tok(tb)], sg[gi][:, :], pu[:, :], ALU.mult,
                           r=[("tmpf", gi), ("ps", 2 + gi)], w=[("h", j, tb)])
            for tb in tbs:
                for c in range(NC8):
                    if hook is not None and tb == 0 and c in (2, 5):
                        hook((c - 2) // 3)
                    di = 4 + state["dn"] % 2
                    state["dn"] += 1
                    for j in range(NF):
                        MM(pb[di][:, :], wout[:, j, c * 128:(c + 1) * 128], hbuf[:, j, tok(tb)],
                           start=(j == 0), stop=(j == NF - 1),
                           r=[("wout", c // 2, 0), ("wout", c // 2, 1), ("h", j, tb)], w=[("ps", di)])
                    STT(x[:, c, tok(tb)], pb[di][:, :], gp[:, s * 8 + c:s * 8 + c + 1], x[:, c, tok(tb)],
                        ALU.mult, ALU.add, r=[("ps", di), ("gp", s), ("x", c, tb)], w=[("x", c, tb)])
                if hook is not None and tb == 0:
                    hook(2)
                if early_barrier and tb == tbs[-1]:
                    BARRIER()
                layer_norm(s, [tb])

        def ffn1_pre(win_g, win_u):
            prefetch_win(0, win_g, win_u)
            prefetch_win(1, win_g, win_u)
            adaln_gather_a()
            for tb in range(NTB):
                for c in range(NC8):
                    TS("dve", hmod[:, c, tok(tb)], x[:, c, tok(tb)], sc1[:, c:c + 1], mod[:, c:c + 1],
                       ALU.mult, ALU.add, r=[("x", c, tb), ("sc1", 0), ("mod", 0)], w=[("hmod", c, tb)])

        ffn_ln(0, f1_win_d, f1_wout_d, hook=adaln_rest_group, pre=ffn1_pre, early_barrier=True)

        AR.reset(0)
        oA = AR.alloc([NH, T], BF16)
        oH = AR.alloc([NH, T], BF16)
        m_mix = AR.mark()
        stg = tmpf

        cosT = AR.alloc([T], F32)
        sinT = AR.alloc([T], F32)
        wk = [AR.alloc([NC8, 128], BF16) for _ in range(2)]
        wv_all = AR.alloc([NC8, 512], BF16)
        wq = [AR.alloc([NC8, 128], BF16) for _ in range(2)]
        kown = [AR.alloc([T], BF16) for _ in range(2)]
        vown_all = AR.alloc([T // 128, 512], BF16)
        qPs = [AR.alloc([T // 256, 512], BF16) for _ in range(NH)]
        kT = [AR.alloc([CTX + TK], BF16) for _ in range(2)]
        Vt = [AR.alloc([NKT, 128], BF16) for _ in range(2)]
        Eb = [AR.alloc([TB], BF16) for _ in range(5)]
        sqb2 = [AR.alloc([256], BF16) for _ in range(2)]
        rms_eps_t = AR.alloc([1], F32)
        P.op("dve", lambda h: h.memset(rms_eps_t, RMS_EPS), w=["rmseps"])
        stage2 = [AR.alloc([NC8, 512], BF16) for _ in range(3)]
        DMA("sp", cosT, cos_d[:, :], w=["cos"])
        DMA("sp", sinT, sin_d[:, :], w=["sin"])
        for h_ in range(NH):
            P.op("dve", lambda h, h_=h_: h.memset(qPs[h_], 0.0), w=[("qT", h_, tb) for tb in range(NTB)])

        def rope(raw, rot_ps, key_raw, key_ps, tb, outs, okeys, wkeys, split=False):
            t1, t2 = stg[2], stg[3]
            TT("dve", t1[:, :], raw, cosT[:, tok(tb)], ALU.mult, r=[key_raw, "cos"], w=[("tmpf", 2)])
            TT("dve", t2[:, :], rot_ps, sinT[:, tok(tb)], ALU.mult, r=[key_ps, "sin"], w=[("tmpf", 3)])
            for (o_ap, psl) in outs:
                a1, a2 = t1[psl, :], t2[psl, :]
                if split:
                    a1 = a1.rearrange("p (a b) -> p a b", b=256)
                    a2 = a2.rearrange("p (a b) -> p a b", b=256)
                TT("dve", o_ap, a1, a2, ALU.add, r=[("tmpf", 2), ("tmpf", 3)] + okeys, w=wkeys)

        DMA("pool", wv_all, wview(wmix_d, 1024, 512), w=["wv_all"])
        for tt in range(T // 128):
            slot = 4 + tt % 2
            for k in range(NC8):
                MM(pb[slot][:, :], hmod[:, k, tt * 128:(tt + 1) * 128], wv_all[:, k, :],
                   start=(k == 0), stop=(k == NC8 - 1),
                   r=["wv_all", ("hmod", k, tt // 4)], w=[("ps", slot)])
            vst = stg[4 + tt % 2]
            ACT(vst[:, :], pb[slot][:, :], AF.Copy, r=[("ps", slot)], w=[("tmpf", 4 + tt % 2)])
            DMA("sp", vout_d[tt * 128:(tt + 1) * 128, :], vst[:, :], r=[("tmpf", 4 + tt % 2)])
            TS("dve", vown_all[:, tt, :], vst[:, :], 1.0, None, ALU.mult, ALU.bypass,
               r=[("tmpf", 4 + tt % 2)], w=["vown_all"])
        DMA("sp", xv_src.rearrange("(t p) n -> p t n", p=128), vown_all, r=["vown_all"], w=["xv_src"])
        for h in range(NH):
            sl = h % 2
            DMA("pool", wk[sl], wview(wmix_d, 512 + h * 128, 128), w=[("wk", sl)])
            if h == NH - 1:
                for i_ in range(3):
                    DMA("pool", stage2[i_], wview(w_ada_d, (3 + i_) * 512, 512), w=[("wada2", i_)])
            for tb in range(NTB):
                pp = pb[tb % 2]
                for k in range(NC8):
                    MM(pp[:, :], wk[sl][:, k, :], hmod[:, k, tok(tb)], start=(k == 0), stop=(k == NC8 - 1),
                       r=[("wk", sl), ("hmod", k, tb)], w=[("ps", tb % 2)])
            for tb in range(NTB):
                pp = pb[tb % 2]
                raw = stg[tb % 2]
                ACT(raw[:, :], pp[:, :], AF.Copy, r=[("ps", tb % 2)], w=[("tmpf", tb % 2)])
                DMA("sp", kout_d[h * 128:(h + 1) * 128, tok(tb)], raw[:, :], r=[("tmpf", tb % 2)])
                MM(pb[2 + tb % 2][:, :], rotm[:, :], raw[:, :], start=True, stop=True,
                   r=["rotm", ("tmpf", tb % 2)], w=[("ps", 2 + tb % 2)])
            for tb in range(NTB):
                raw = stg[tb % 2]
                rope(raw[:, :], pb[2 + tb % 2][:, :], ("tmpf", tb % 2), ("ps", 2 + tb % 2), tb,
                     [(kown[sl][:, tok(tb)], slice(0, 128))], [], [("kown", sl)])
            DMA("sp", xk_src[h * 128:(h + 1) * 128, :], kown[sl], r=[("kown", sl)], w=["xk_src"])
        def a2_loads(h):
            sl = h % 2
            DMA("pool", kT[sl][:, 0:CTX], kctx_d[h * 128:(h + 1) * 128, :], w=[("kT", sl)])
            DMA("pool", Vt[sl][:, 0:CTX // 128, :],
                vctx_d[:, h * 128:(h + 1) * 128].rearrange("(t p) n -> p t n", p=128), w=[("Vt", sl)])

        DMA("pool", wq[0], wview(wmix_d, 0, 128), w=[("wq", 0)])
        DMA("pool", wq[1], wview(wmix_d, 128, 128), w=[("wq", 1)])
        a2_loads(0)
        P.op("pool", lambda hh: hh.collective_compute("AllGather", ALU.bypass, replica_groups=PAIRS,
                                                      ins=[xk_src], outs=[xk_dst]),
             r=["xk_src"], w=["xk_dst"], cc=ccsem[0])
        P.op("pool", lambda hh: hh.collective_compute("AllGather", ALU.bypass, replica_groups=PAIRS,
                                                      ins=[xv_src], outs=[xv_dst]),
             r=["xv_src"], w=["xv_dst"], cc=ccsem[1])

        def adaln_b2():
            for i_ in range(3):
                for jj in range(4):
                    j = i_ * 4 + jj
                    for k in range(NC8):
                        MM(pb[7][:, j:j + 1], stage2[i_][:, k, jj * 128:(jj + 1) * 128], scond[:, k:k + 1],
                           start=(k == 0), stop=(k == NC8 - 1), r=[("wada2", i_), "scond"], w=[("ps", 7)])
            TT("dve", modh[:, 24:36], pb[7][:, 0:12], bada[:, 24:36], ALU.add, r=[("ps", 7), "bada"],
               w=[("modh", 2)])
            adaln_gather(2, mc_src, mc_dst, ccsem[3])

        for h in range(NH):
            sl = h % 2
            if h >= 2:
                DMA("pool", wq[sl], wview(wmix_d, h * 128, 128), w=[("wq", sl)])
            for tb in range(NTB):
                pp = pb[tb % 2]
                for k in range(NC8):
                    MM(pp[:, :], wq[sl][:, k, :], hmod[:, k, tok(tb)], start=(k == 0), stop=(k == NC8 - 1),
                       r=[("wq", sl), ("hmod", k, tb)], w=[("ps", tb % 2)])
                raw = stg[tb % 2]
                ACT(raw[:, :], pp[:, :], AF.Copy, r=[("ps", tb % 2)], w=[("tmpf", tb % 2)])
                MM(pb[2 + tb % 2][:, :], rotm[:, :], raw[:, :], start=True, stop=True,
                   r=["rotm", ("tmpf", tb % 2)], w=[("ps", 2 + tb % 2)])
                rope(raw[:, :], pb[2 + tb % 2][:, :], ("tmpf", tb % 2), ("ps", 2 + tb % 2), tb,
                     [(qPs[h][0:64, 2 * tb:2 * tb + 2, 0:256], slice(0, 64)),
                      (qPs[h][64:128, 2 * tb:2 * tb + 2, 256:512], slice(64, 128))],
                     [("qT", h, tb)], [("qT", h, tb)], split=True)

        cnt = {"e": 0, "s": 0, "it": 0}
        pend = []
        for h in range(NH):
            sl = h % 2
            if h > 0:
                a2_loads(h)
            for rk in range(2):
                DMA("sp", kT[sl][:, CTX + rk * T:CTX + (rk + 1) * T], xk_dst[rk * 512 + h * 128:rk * 512 + (h + 1) * 128, :],
                    r=["xk_dst"], w=[("kT", sl)])
                DMA("sp", Vt[sl][:, CTX // 128 + rk * (T // 128):CTX // 128 + (rk + 1) * (T // 128), :],
                    xv_dst[rk * T:(rk + 1) * T, h * 128:(h + 1) * 128].rearrange("(t p) n -> p t n", p=128),
                    r=["xv_dst"], w=[("Vt", sl)])
            if h == 0:
                adaln_b2()
            NQ = T // 256
            its = [(qi, kt) for qi in range(NQ) for kt in range(NKT)]
            slots = {}

            def front(i, its=its, slots=slots, sl=sl, h=h):
                qi, kt = its[i]
                si = (0, 1, 2, 7)[cnt["s"] % 4]
                cnt["s"] += 1
                ei = cnt["e"] % 5
                cnt["e"] += 1
                slots[i] = ei
                MM(pb[si][:, :], kT[sl][:, kt * 128:(kt + 1) * 128], qPs[h][:, qi, :], start=True, stop=True,
                   r=[("kT", sl), ("qT", h, qi // 2)], w=[("ps", si)])
                mcol = kt * NQ + qi
                ACT(Eb[ei][:, :], pb[si][:, :], AF.Exp, r=[("ps", si), "mbias"], w=[("E", ei)], scale=0.125,
                    bias=mbias[:, mcol:mcol + 1])

            def back(i, its=its, slots=slots, sl=sl):
                qi, kt = its[i]
                ei = slots[i]
                MM(pb[3 + qi % 2][:, :], Vt[sl][:, kt, :], Eb[ei][:, :], start=(kt == 0), stop=(kt == NKT - 1),
                   r=[("Vt", sl), ("E", ei)], w=[("ps", 3 + qi % 2)])
                MM(pb[5 + qi % 2][:, :], ones_bf[:, :], Eb[ei][:, :], start=(kt == 0), stop=(kt == NKT - 1),
                   r=["ones", ("E", ei)], w=[("ps", 5 + qi % 2)])

            def combine_stages(qi, h=h):
                par = qi % 2
                rz, tt_ = stg[4 + par], stg[6 + par]
                qs_ = slice(qi * 256, (qi + 1) * 256)

                def s1():
                    RECIP(rz[:, :], pb[5 + par][:, :], r=[("ps", 5 + par)], w=[("tmpf", 4 + par)])
                    TT("dve", tt_[:, :], pb[3 + par][:, :], rz[:, :], ALU.mult,
                       r=[("ps", 3 + par), ("tmpf", 4 + par)], w=[("tmpf", 6 + par)])
                    STT(tt_[:, 0:256], tt_[:, 256:512], neglam[:, 0:1], tt_[:, 0:256], ALU.mult, ALU.add,
                        r=[("tmpf", 6 + par), "neglam"], w=[("tmpf", 6 + par)])
                    TT("dve", sqb2[par][:, 0:256], tt_[:, 0:256], tt_[:, 0:256], ALU.mult,
                       r=[("tmpf", 6 + par)], w=[("sqb", par)])

                def s2():
                    MM(pb[3 + par][:, 0:256], ones_bf[:, :], sqb2[par][:, 0:256], start=True, stop=True,
                       r=["ones", ("sqb", par)], w=[("ps", 3 + par)])

                def s3():
                    ACT(rz[:, 0:256], pb[3 + par][:, 0:256], AF.Ln, r=[("ps", 3 + par)],
                        w=[("tmpf", 4 + par)], scale=1.0 / 128, bias=rms_eps_t[:, 0:1])
                    ACT(rz[:, 0:256], rz[:, 0:256], AF.Exp, r=[("tmpf", 4 + par)], w=[("tmpf", 4 + par)], scale=-0.5)
                    TT("dve", tt_[:, 0:256], tt_[:, 0:256], rz[:, 0:256], ALU.mult,
                       r=[("tmpf", 6 + par), ("tmpf", 4 + par)], w=[("tmpf", 6 + par)])
                    TS("dve", oA[:, h, qs_], tt_[:, 0:256], subg[:, 0:1], None, ALU.mult, ALU.bypass,
                       r=[("tmpf", 6 + par), "subg"], w=[("oA", h, qi // 2)])
                return [(0, s1), (5, s2), (9, s3)]

            PF = 3
            for i in range(min(PF, len(its))):
                front(i)
            for i in range(len(its)):
                if i + PF < len(its):
                    front(i + PF)
                back(i)
                if its[i][1] == NKT - 1:
                    for (dl, fn_) in combine_stages(its[i][0]):
                        pend.append([cnt["it"] + dl, fn_])
                cnt["it"] += 1
                for pe_ in [p_ for p_ in pend if p_[0] <= cnt["it"]]:
                    pe_[1]()
                    pend.remove(pe_)
        for pe_ in sorted(pend, key=lambda p_: p_[0]):
            pe_[1]()
        if DEBUG:
            for h in range(NH):
                DMA("pool", dbg_oa_d[h * 128:(h + 1) * 128, :], oA[:, h, :],
                    r=[("oA", h, qb) for qb in range(NTB)])
        BARRIER()

        AR.reset(m_mix)
        NB128 = T // 128
        Qt = [[AR.alloc([T], BF16) for _ in range(2)] for _ in range(NH)]
        Khm = [[AR.alloc([NB128, 128], BF16) for _ in range(2)] for _ in range(NH)]
        decs = [[AR.alloc([NCH], F32) for _ in range(2)] for _ in range(NH)]
        Vh = [AR.alloc([NB128, 128], BF16) for _ in range(NH)]
        gsil = [AR.alloc([T], BF16) for _ in range(NH)]
        opart = [AR.alloc([T], F32) for _ in range(NH)]
        Sf = [AR.alloc([128], F32) for _ in range(NH)]
        Sb = [AR.alloc([128], BF16) for _ in range(NH)]
        Sst = [AR.alloc([128], F32) for _ in range(4)]
        Sx = [[AR.alloc([128], F32) for _ in range(2)] for _ in range(NH)]
        ATm = [AR.alloc([128], BF16) for _ in range(4)]
        Vmk = [AR.alloc([128], BF16) for _ in range(4)]
        whq = [AR.alloc([NC8, 128], BF16) for _ in range(2)]
        whf0 = [AR.alloc([NC8, 128], BF16) for _ in range(2)]
        whf1 = [AR.alloc([NC8, 128], BF16) for _ in range(2)]
        whi = [AR.alloc([NC8, 128], BF16) for _ in range(2)]
        whg = [AR.alloc([NC8, 128], BF16) for _ in range(2)]
        tG2 = [AR.alloc([TB + 1], F32) for _ in range(2)]
        qs = tmpf[0][:, :]
        onesf = tmpf[1][:, :]
        osum = tmpf[2][:, :]
        tE = tmpf[3][:, :]
        tf2 = [tmpf[4][:, :], tmpf[5][:, :]]
        tlog2 = [tmpf[6][:, :], tmpf[7][:, :]]
        tkk2 = [tmpf[8][:, :], AR.alloc([TB], F32)]
        tE2 = [AR.alloc([TB], F32) for _ in range(2)]
        tEi2 = [AR.alloc([TB], F32) for _ in range(2)]
        Kt2 = [rb[1][:, :], rsq[1][:, :]]
        Kh2 = [rsq[0][:, :], AR.alloc([TB], BF16)]
        P.op("dve", lambda h: h.memset(onesf, 1.0), w=["onesf"])
        one_t = AR.alloc([1], F32)
        P.op("dve", lambda h: h.memset(one_t, 1.0), w=["one_t"])
        eps_t = AR.alloc([1], F32)
        P.op("dve", lambda h: h.memset(eps_t, RMS_EPS), w=["eps_t"])
        for d in range(2):
            P.op("dve", lambda h, d=d: h.memset(tG2[d][:, 0:1], 0.0), w=[("tG", d)])
        NCB = TB // CH
        scnt = {"st": 0, "kv": 0, "oi": 0, "at": 0}
        whf = (whf0, whf1)

        def head_loads(h):
            sl = h % 2
            DMA("pool", whq[sl], wview(wmix_d, 1536 + h * 128, 128), w=[("whq", sl)])
            DMA("pool", whf0[sl], wview(wmix_d, 2048 + h * 128, 128), w=[("whf", 0, sl)])
            DMA("pool", whf1[sl], wview(wmix_d, 2560 + h * 128, 128), w=[("whf", 1, sl)])
            DMA("pool", whi[sl], wview(wmix_d, 3072 + h * 128, 128), w=[("whi", sl)])
            DMA("pool", whg[sl], wview(wmix_d, 3584 + h * 128, 128), w=[("whg", sl)])
            DMA("sp", Sf[h], s0_d[0, h, :, :], w=[("Sf", h)])

        def vproj(h):
            sl = h % 2
            for tt in range(NB128):
                for k in range(NC8):
                    MM(pb[6][:, 0:128], hmod[:, k, tt * 128:(tt + 1) * 128], whi[sl][:, k, :],
                       start=(k == 0), stop=(k == NC8 - 1), r=[("whi", sl), ("hmod", k, tt // 4)], w=[("ps", 6)])
                ACT(Vh[h][:, tt, :], pb[6][:, 0:128], AF.Copy, r=[("ps", 6)], w=[("Vh", h)])

        def proj(h, tb):
            sl = h % 2
            for k in range(NC8):
                MM(pb[2][:, :], whq[sl][:, k, :], hmod[:, k, tok(tb)], start=(k == 0), stop=(k == NC8 - 1),
                   r=[("whq", sl), ("hmod", k, tb)], w=[("ps", 2)])
            for k in range(NC8):
                MM(pb[tb][:, :], whg[sl][:, k, :], hmod[:, k, tok(tb)], start=(k == 0), stop=(k == NC8 - 1),
                   r=[("whg", sl), ("hmod", k, tb)], w=[("ps", tb)])
            for d in range(2):
                for k in range(NC8):
                    MM(pb[4 + d][:, :], whf[d][sl][:, k, :], hmod[:, k, tok(tb)], start=(k == 0),
                       stop=(k == NC8 - 1), r=[("whf", d, sl), ("hmod", k, tb)], w=[("ps", 4 + d)])

        items = [(h, tb) for h in range(NH) for tb in range(NTB)]
        head_loads(0)
        vproj(0)
        proj(0, 0)
        for idx, (h, tb) in enumerate(items):
            sl = h % 2
            if True:
                ACT(qs, pb[2][:, :], AF.Silu, r=[("ps", 2)], w=["qs"])
                ACT(gsil[h][:, tok(tb)], pb[tb][:, :], AF.Silu, r=[("ps", tb)], w=[("gsil", h)])
                def gen(d, h=h, sl=sl, tb=tb):
                    tf, tlog, tkk, tG, tE, tEi, Kt, Kh = (tf2[d], tlog2[d], tkk2[d], tG2[d], tE2[d], tEi2[d],
                                                          Kt2[d], Kh2[d])
                    kd = lambda n: (n, d)
                    pz_ = pb[4 + d]
                    ACT(tf, pz_[:, :], AF.Sigmoid, r=[("ps", 4 + d)], w=[kd("tf")])
                    yield
                    TS("dve", tf, tf, olb[:, d, h:h + 1], lbv[:, d, h:h + 1], ALU.mult, ALU.add,
                       r=[kd("tf"), "olb", "lbv"], w=[kd("tf")])
                    yield
                    ACT(tlog, tf, AF.Ln, r=[kd("tf")], w=[kd("tlog")])
                    yield
                    ACT(tkk, tf, AF.Identity, r=[kd("tf")], w=[kd("tkk")], scale=-1.0, bias=one_t[:, 0:1])
                    P.op("dve", lambda hh, tG=tG, tlog=tlog: hh.tensor_tensor_scan(
                        out=tG[:, 1:TB + 1], data0=onesf, data1=tlog, initial=0.0, op0=ALU.mult, op1=ALU.add),
                         r=["onesf", kd("tlog")], w=[kd("tG")])
                    yield
                    G3 = tG[:, 1:TB + 1].rearrange("p (c j) -> p c j", j=CH)
                    if d == 0:
                        gprev = tG[:, 0:TB].rearrange("p (c j) -> p c j", j=CH)[:, :, 0:1].to_broadcast([128, NCB, CH])
                        TT("dve", tE.rearrange("p (c j) -> p c j", j=CH), G3, gprev, ALU.subtract,
                           r=[kd("tG")], w=[kd("tE")])
                    else:
                        gend = G3[:, :, CH - 1:CH].to_broadcast([128, NCB, CH])
                        TT("dve", tE, tlog, tG[:, 1:TB + 1], ALU.subtract, r=[kd("tG"), kd("tlog")], w=[kd("tE")])
                        yield
                        TT("dve", tE.rearrange("p (c j) -> p c j", j=CH), tE.rearrange("p (c j) -> p c j", j=CH),
                           gend, ALU.add, r=[kd("tE"), kd("tG")], w=[kd("tE")])
                    yield
                    ACT(tEi, tE, AF.Exp, r=[kd("tE")], w=[kd("tEi")], scale=-1.0)
                    ACT(tE, tE, AF.Exp, r=[kd("tE")], w=[kd("tE")])
                    yield
                    E3 = tE.rearrange("p (c j) -> p c j", j=CH)
                    edge = E3[:, :, CH - 1:CH] if d == 0 else E3[:, :, 0:1]
                    TT("dve", Qt[h][d][:, tok(tb)], qs, tE, ALU.mult, r=["qs", kd("tE")], w=[("Qt", h, d)])
                    TT("dve", Kt, tkk, tEi, ALU.mult, r=[kd("tkk"), kd("tEi")], w=[kd("Kt")])
                    yield
                    TT("dve", Kh.rearrange("p (c j) -> p c j", j=CH), Kt.rearrange("p (c j) -> p c j", j=CH),
                       edge.to_broadcast([128, NCB, CH]), ALU.mult, r=[kd("Kt"), kd("tE")], w=[kd("Kh")])
                    TS("dve", decs[h][d][:, tb * NCB:(tb + 1) * NCB].rearrange("p (c o) -> p c o", o=1), edge,
                       1.0, None, ALU.mult, ALU.bypass, r=[kd("tE")], w=[("dec", h, d)])
                    yield
                    tbank = 7
                    abank = 3 if d == 0 else 6
                    for bb in range(TB // 128):
                        blk = tb * 4 + bb
                        bs = slice(bb * 128, (bb + 1) * 128)
                        gs = slice(tb * TB + bb * 128, tb * TB + (bb + 1) * 128)
                        TR(pbb[tbank][:, 0:128], Kh[:, bs], r=[kd("Kh")], w=[("ps", tbank)])
                        ACT(Khm[h][d][:, blk, :], pbb[tbank][:, 0:128], AF.Copy, r=[("ps", tbank)], w=[("Khm", h, d)])
                        MM(pb[abank][:, 0:128], Kt[:, bs], Qt[h][d][:, gs], start=True, stop=True,
                           r=[kd("Kt"), ("Qt", h, d)], w=[("ps", abank)])
                        yield
                        ai = d * 2 + bb % 2
                        TT("dve", ATm[ai], pb[abank][:, 0:128], (maskf, maskb)[d][:, :], ALU.mult,
                           r=[("ps", abank), "maskf", "maskb"], w=[("ATm", ai)])
                        MM(pb[tb][:, bs], Vh[h][:, blk, :], ATm[ai], start=(d == 0 and bb == 0),
                           stop=(d == 1 and bb == TB // 128 - 1),
                           r=[("Vh", h), ("ATm", ai)], w=[("ps", tb)])
                        yield

                alive = [gen(0), gen(1)]
                rounds = 0
                while alive:
                    for g_ in list(alive):
                        try:
                            next(g_)
                        except StopIteration:
                            alive.remove(g_)
                    rounds += 1
                    if rounds == 1 and idx + 1 < len(items):
                        nh, ntb = items[idx + 1]
                        if ntb == 0:
                            head_loads(nh)
                            vproj(nh)
                        proj(nh, ntb)
                ACT(opart[h][:, tok(tb)], pb[tb][:, :], AF.Copy, r=[("ps", tb)], w=[("opart", h)])
                if tb == NTB - 1:
                    ACT(Sb[h], Sf[h], AF.Copy, r=[("Sf", h)], w=[("Sb", h)])

        def chain_step(h, d, c):
            gcol = slice(c * CH, (c + 1) * CH)
            blk = (c * CH) // 128
            jrow = ((c * CH) % 128) // CH
            oi = scnt["oi"] % 4
            scnt["oi"] += 1
            MM(pb[oi][:, 0:CH], Sb[h], Qt[h][d][:, gcol], start=True, stop=True,
               r=[("Sb", h), ("Qt", h, d)], w=[("ps", oi)])
            TT("dve", opart[h][:, gcol], opart[h][:, gcol], pb[oi][:, 0:CH], ALU.add,
               r=[("opart", h), ("ps", oi)], w=[("opart", h)])
            kv = 4 + scnt["kv"] % 4
            vmi = scnt["kv"] % 4
            scnt["kv"] += 1
            TS("pool", Vmk[vmi], Vh[h][:, blk, :], rowmask[:, jrow:jrow + 1], 1.0, ALU.mult, ALU.mult,
               r=[("Vh", h), "rowmask"], w=[("Vmk", vmi)])
            MM(pb[kv][:, 0:128], Khm[h][d][:, blk, :], Vmk[vmi], start=True, stop=True,
               r=[("Khm", h, d), ("Vmk", vmi)], w=[("ps", kv)])
            STT(Sf[h], Sf[h], decs[h][d][:, c:c + 1], pb[kv][:, 0:128], ALU.mult, ALU.add,
                r=[("Sf", h), ("dec", h, d), ("ps", kv)], w=[("Sf", h)])
            at_boundary = ((c + 1) % (SEQ // CH) == 0) if d == 0 else (c % (SEQ // CH) == 0)
            if at_boundary:
                seq = c // (SEQ // CH)
                si = scnt["st"] % 4
                scnt["st"] += 1
                ACT(Sst[si], Sf[h], AF.Copy, r=[("Sf", h)], w=[("Sst", si)])
                DMA("sp", sout_d[seq, d, h, :, :], Sst[si], r=[("Sst", si)])
                TS("dve", Sf[h], Sf[h], keep[:, 0:1], None, ALU.mult, ALU.bypass,
                   r=[("Sf", h), "keep"], w=[("Sf", h)])
            ACT(Sb[h], Sf[h], AF.Copy, r=[("Sf", h)], w=[("Sb", h)])

        for step in range(NCH):
            for h in range(NH):
                chain_step(h, 0, step)
        for h in range(NH):
            DMA("sp", xs_src[h * 128:(h + 1) * 128, :], Sf[h], r=[("Sf", h)], w=["xs_src"])
        P.op("pool", lambda hh: hh.collective_compute("AllGather", ALU.bypass, replica_groups=PAIRS,
                                                      ins=[xs_src], outs=[xs_dst]),
             r=["xs_src"], w=["xs_dst"], cc=ccsem[2])
        for h in range(NH):
            for rk in range(2):
                DMA("sp", Sx[h][rk], xs_dst[rk * 512 + h * 128:rk * 512 + (h + 1) * 128, :], r=["xs_dst"],
                    w=[("Sx", h, rk)])
            DMA("sp", Sf[h], s0_d[1, h, :, :], w=[("Sf", h)])
            TS("dve", Sf[h], Sf[h], xw[:, 2:3], None, ALU.mult, ALU.bypass, r=[("Sf", h), "xw"], w=[("Sf", h)])
            STT(Sf[h], Sx[h][0], xw[:, 0:1], Sf[h], ALU.mult, ALU.add, r=[("Sx", h, 0), "xw", ("Sf", h)],
                w=[("Sf", h)])
            STT(Sf[h], Sx[h][1], xw[:, 1:2], Sf[h], ALU.mult, ALU.add, r=[("Sx", h, 1), "xw", ("Sf", h)],
                w=[("Sf", h)])
            ACT(Sb[h], Sf[h], AF.Copy, r=[("Sf", h)], w=[("Sb", h)])
        for step in range(NCH):
            for h in range(NH):
                chain_step(h, 1, NCH - 1 - step)
        BARRIER()
        fsq = [rb[0][:, :], rb[1][:, :]]
        frs = [tmpf[4][:, :], tmpf[5][:, :]]
        fos = [tmpf[6][:, :], tmpf[7][:, :]]
        fi = 0
        for h in range(NH):
            for tb in range(NTB):
                q_ = fi % 2
                bank = fi % 4
                fi += 1
                TT("dve", fsq[q_], opart[h][:, tok(tb)], opart[h][:, tok(tb)], ALU.mult,
                   r=[("opart", h)], w=[("fsq", q_)])
                MM(pb[bank][:, :], ones_bf[:, :], fsq[q_], start=True, stop=True, r=["ones", ("fsq", q_)],
                   w=[("ps", bank)])
                ACT(frs[q_], pb[bank][:, :], AF.Ln, r=[("ps", bank)], w=[("frs", q_)], scale=1.0 / 128,
                    bias=eps_t[:, 0:1])
                ACT(frs[q_], frs[q_], AF.Exp, r=[("frs", q_)], w=[("frs", q_)], scale=-0.5)
                TT("dve", fos[q_], opart[h][:, tok(tb)], frs[q_], ALU.mult, r=[("opart", h), ("frs", q_)],
                   w=[("fos", q_)])
                STT(oH[:, h, tok(tb)], fos[q_], hng[:, 0:1], gsil[h][:, tok(tb)], ALU.mult, ALU.mult,
                    r=[("fos", q_), "hng", ("gsil", h)], w=[("oH", h, tb)])
        if DEBUG:
            for h in range(NH):
                DMA("pool", dbg_oh_d[h * 128:(h + 1) * 128, :], oH[:, h, :],
                    r=[("oH", h, tb) for tb in range(NTB)])
        BARRIER()

        AR.reset(m_mix)
        merged = AR.alloc([NC8, 1024], BF16)
        CB = 256
        wga = [AR.alloc([NC8, CB], BF16) for _ in range(2)]
        wgh = [AR.alloc([NC8, CB], BF16) for _ in range(2)]
        wba = [AR.alloc([NH, CB], BF16) for _ in range(2)]
        wbh = [AR.alloc([NH, CB], BF16) for _ in range(2)]
        wmo_all = AR.alloc([NC8, D], BF16)
        mcnt = {"w": 0, "o": 0}
        for half in range(T // 1024):
            tbs = [half * 2, half * 2 + 1]
            for blk in range(D // CB):
                sl = mcnt["w"] % 2
                mcnt["w"] += 1
                c0 = blk * CB
                DMA("pool", wga[sl], wview(wmix_d, 4096 + c0, CB), w=[("wga", sl)])
                DMA("pool", wgh[sl], wview(wmix_d, 5120 + c0, CB), w=[("wgh", sl)])
                DMA("pool", wba[sl], wview(wbra_d, c0, CB), w=[("wba", sl)])
                DMA("pool", wbh[sl], wview(wbrh_d, c0, CB), w=[("wbh", sl)])
                DMA("pool", wmo_all[:, :, c0:c0 + CB], wview(wmo_d, c0, CB), w=[("wmo", blk)])
                for cc in range(CB // 128):
                    dc = blk * (CB // 128) + cc
                    cs = slice(cc * 128, (cc + 1) * 128)
                    for tb in tbs:
                        lt = tb - half * 2
                        o4 = 4 * (mcnt["o"] % 2)
                        o2 = 2 * (mcnt["o"] % 2)
                        mcnt["o"] += 1
                        for k in range(NC8):
                            MM(pb[o4 + 0][:, :], wga[sl][:, k, cs], hmod[:, k, tok(tb)], start=(k == 0),
                               stop=(k == NC8 - 1), r=[("wga", sl), ("hmod", k, tb)], w=[("ps", o4 + 0)])
                        for k in range(NC8):
                            MM(pb[o4 + 1][:, :], wgh[sl][:, k, cs], hmod[:, k, tok(tb)], start=(k == 0),
                               stop=(k == NC8 - 1), r=[("wgh", sl), ("hmod", k, tb)], w=[("ps", o4 + 1)])
                        for hh in range(NH):
                            MM(pb[o4 + 2][:, :], wba[sl][:, hh, cs], oA[:, hh, tok(tb)], start=(hh == 0),
                               stop=(hh == NH - 1), r=[("wba", sl)], w=[("ps", o4 + 2)])
                        for hh in range(NH):
                            MM(pb[o4 + 3][:, :], wbh[sl][:, hh, cs], oH[:, hh, tok(tb)], start=(hh == 0),
                               stop=(hh == NH - 1), r=[("wbh", sl)], w=[("ps", o4 + 3)])
                        sa, sb_ = stg[o2], stg[o2 + 1]
                        ACT(sa[:, :], pb[o4 + 0][:, :], AF.Sigmoid, r=[("ps", o4 + 0)], w=[("tmpf", o2)])
                        ACT(sb_[:, :], pb[o4 + 1][:, :], AF.Sigmoid, r=[("ps", o4 + 1)], w=[("tmpf", o2 + 1)])
                        TT("dve", sa[:, :], sa[:, :], pb[o4 + 2][:, :], ALU.mult, r=[("tmpf", o2), ("ps", o4 + 2)],
                           w=[("tmpf", o2)])
                        TT("dve", sb_[:, :], sb_[:, :], pb[o4 + 3][:, :], ALU.mult,
                           r=[("tmpf", o2 + 1), ("ps", o4 + 3)], w=[("tmpf", o2 + 1)])
                        TT("dve", merged[:, dc, lt * TB:(lt + 1) * TB], sa[:, :], sb_[:, :], ALU.add,
                           r=[("tmpf", o2), ("tmpf", o2 + 1)], w=[("mg", dc, lt)])
        for tb in range(NTB):
            for dc in range(NC8):
                pi = 4 + (dc + tb) % 2
                cs = slice(dc * 128, (dc + 1) * 128)
                for k in range(NC8):
                    MM(pb[pi][:, :], wmo_all[:, k, cs], merged[:, k, tok(tb)],
                       start=(k == 0), stop=(k == NC8 - 1), r=[("wmo", dc // 2), ("mg", k, tb)], w=[("ps", pi)])
                STT(x[:, dc, tok(tb)], pb[pi][:, :], gp[:, 8 + dc:8 + dc + 1], x[:, dc, tok(tb)],
                    ALU.mult, ALU.add, r=[("ps", pi), ("gp", 1), ("x", dc, tb)], w=[("x", dc, tb)])
            if tb == NTB - 1:
                BARRIER()
            layer_norm(1, [tb])

        ffn_ln(2, f2_win_d, f2_wout_d)

        for c in range(NC8):
            for tb in range(NTB):
                DMA("sp", yT_d[c * 128:(c + 1) * 128, tok(tb)], x[:, c, tok(tb)], r=[("x", c, tb)])

        P.finalize(nc, sems, dsems)
        with nc.Block() as block:
            @block.tensor
            def _(h):
                P.emit("pe", h)

            @block.scalar
            def _(h):
                P.emit("act", h)

            @block.vector
            def _(h):
                P.emit("dve", h)

            @block.gpsimd
            def _(h):
                P.emit("pool", h)

            @block.sync
            def _(h):
                P.emit("sp", h)
    return nc


_NC_CACHE = {}


def _pc(v, ncols):
    return np.ascontiguousarray(np.asarray(v).reshape(ncols, 128).T)


def _host_constants():
    f32 = np.float32
    t = np.arange(TK)
    row = (t // 64).astype(f32)
    col = (t % 64).astype(f32)
    inv = (np.float32(10000.0) ** (-np.arange(0, 32, 2, dtype=f32) / np.float32(32))).astype(f32)
    ar = row[:, None] * inv
    ac = col[:, None] * inv
    ang = np.concatenate([ar, ar, ac, ac], axis=-1)
    cos = np.cos(ang).astype(f32).T
    sin = np.sin(ang).astype(f32).T
    d = np.arange(64)
    first = (d % 32) < 16
    sign = np.where(first, -1.0, 1.0).astype(f32)
    perm = np.where(first, d + 16, d - 16)
    cosT = np.concatenate([cos, cos], axis=0)
    sinT = np.concatenate([sin * sign[:, None], sin * sign[:, None]], axis=0)
    rot = np.zeros((128, 128), f32)
    for c in range(2):
        for dd in range(64):
            rot[c * 64 + perm[dd], c * 64 + dd] = 1.0
    i = np.arange(128)
    same = (i[:, None] // CH) == (i[None, :] // CH)
    maskf = (same & (i[:, None] <= i[None, :])).astype(f32)
    maskb = (same & (i[:, None] >= i[None, :])).astype(f32)
    nq = T // 256
    mbs = np.zeros((128, NKT * nq), f32)
    mbp = []
    for r in range(2):
        m = np.full((128, NKT * nq), -30000.0, f32)
        for kt in range(NKT):
            k0 = kt * 128 - CTX - r * T
            if 0 <= k0 < T:
                m[:, kt * nq + k0 // 256] = 0.0
        mbp.append(m)
    return dict(cos=[np.ascontiguousarray(cosT[:, 0:T]), np.ascontiguousarray(cosT[:, T:2 * T][:, ::-1])],
                sin=[np.ascontiguousarray(sinT[:, 0:T]), np.ascontiguousarray(sinT[:, T:2 * T][:, ::-1])],
                rot=rot, maskf=maskf, maskb=maskb, mbs=mbs, mbp=mbp, ident=np.eye(128, dtype=f32),
                rowmask=((i[:, None] // CH) == np.arange(4)[None, :]).astype(f32),
                cos1=np.ones((128, T), f32), sin0=np.zeros((128, T), f32))


def kernel(**inputs):
    inp = {k: np.asarray(v) for k, v in inputs.items()}
    if "nc" not in _NC_CACHE:
        _NC_CACHE["nc"] = build_program()
        _NC_CACHE["hc"] = _host_constants()
    nc = _NC_CACHE["nc"]
    hc = _NC_CACHE["hc"]
    f32 = np.float32
    xs, xp = inp["x_sample"], inp["x_prompt"]
    lamv = np.stack([inp["lambda_q1"][0], inp["lambda_k1"][0], inp["lambda_q2"][0], inp["lambda_k2"][0]], 0)
    lamv = np.ascontiguousarray(np.broadcast_to(lamv[None], (128, 4, 64))).astype(f32)
    lbl4 = inp["hgrn_lb_logits"].reshape(2, 2, NH, 128)
    lbl = np.ascontiguousarray(lbl4.transpose(3, 0, 1, 2)).astype(f32)
    lbl_sw = np.ascontiguousarray(lbl4[::-1].transpose(3, 0, 1, 2)).astype(f32)
    wmix = inp["w_mix_in"][0]
    wmix_sw = np.concatenate([wmix[:, :2048], wmix[:, 2560:3072], wmix[:, 2048:2560], wmix[:, 3072:]], axis=1)
    wmix_sw = np.ascontiguousarray(wmix_sw)
    wa = inp["w_ada"][0]
    ba = inp["b_ada"][0]
    wa_half = [np.ascontiguousarray(np.concatenate([wa[:, s * 3072 + r * 1536:s * 3072 + (r + 1) * 1536]
                                                    for s in (1, 2)], axis=1)) for r in range(2)]
    wa0 = np.ascontiguousarray(wa[:, 0:3072])
    ba0 = _pc(ba[0:3072], 24)
    ba_half = [_pc(np.concatenate([ba[s * 3072 + r * 1536:s * 3072 + (r + 1) * 1536] for s in range(3)]), 36)
               for r in range(2)]
    shared = {
        "ffn1_w_in": inp["ffn1_w_in"][0], "ffn1_w_out": inp["ffn1_w_out"][0],
        "ffn2_w_in": inp["ffn2_w_in"][0], "ffn2_w_out": inp["ffn2_w_out"][0],
        "ln_g": _pc(inp["ln_g"][0].reshape(-1), 24), "ln_b": _pc(inp["ln_b"][0].reshape(-1), 24),
        "w_branch_a": inp["w_branch_a"][0], "w_branch_h": inp["w_branch_h"][0],
        "w_mix_out": inp["w_mix_out"][0], "lamv": lamv,
        "subln_g": np.ascontiguousarray(inp["attn_subln_g"][0].reshape(128, 1)),
        "hnorm_g": np.ascontiguousarray(inp["hgrn_norm_g"][0].reshape(128, 1)),
        "ident": hc["ident"], "rotm": hc["rot"], "maskf": hc["maskf"], "maskb": hc["maskb"],
        "rowmask": hc["rowmask"],
    }
    in_maps = []
    for core in range(8):
        m = dict(shared)
        r = core % 2
        m["w_ada"], m["b_ada"] = wa_half[r], ba_half[r]
        m["w_ada0"], m["b_ada0"] = wa0, ba0
        if core < 4:
            b = core // 2
            xh = xs[b, r * T:(r + 1) * T]
            if r == 1:
                xh = xh[::-1]
            m["xT"] = np.ascontiguousarray(xh.T)
            m["cond"] = _pc(inp["c"][b], 8)
            m["kctxT"] = np.ascontiguousarray(inp["cache_k"][b, 0].reshape(CTX, 512).T)
            m["vctx"] = np.ascontiguousarray(inp["cache_v"][b, 0].reshape(CTX, 512))
            st = inp["state_hgrn"][b, 0]
            m["s0"] = np.ascontiguousarray(st if r == 0 else st[::-1])
            m["cosT"], m["sinT"] = hc["cos"][r], hc["sin"][r]
            m["mbias"] = hc["mbs"]
            m["keep"] = np.ones((128, 1), f32)
            xw = np.zeros((128, 3), f32)
            xw[:, 1 - r] = 1.0
            m["xw"] = xw
            m["w_mix_in"] = wmix if r == 0 else wmix_sw
            m["lb_logits"] = lbl if r == 0 else lbl_sw
        else:
            g0 = (core - 4) * 4
            m["xT"] = np.ascontiguousarray(xp[g0:g0 + 4].reshape(T, D).T)
            m["cond"] = _pc(inp["c_ctx"], 8)
            m["kctxT"] = np.zeros((512, CTX), f32)
            m["vctx"] = np.zeros((CTX, 512), f32)
            m["s0"] = np.zeros((2, NH, 128, 128), f32)
            m["cosT"], m["sinT"] = hc["cos1"], hc["sin0"]
            m["mbias"] = hc["mbp"][r]
            m["keep"] = np.zeros((128, 1), f32)
            m["xw"] = np.zeros((128, 3), f32)
            m["w_mix_in"] = wmix
            m["lb_logits"] = lbl
        in_maps.append(m)
    res = run_bass_kernel_spmd(nc, in_maps, core_ids=list(range(8)))
    R = res.results
    _NC_CACHE["last"] = R
    y_sample = np.stack([np.concatenate([R[2 * b]["yT"].T, R[2 * b + 1]["yT"].T[::-1]], axis=0) for b in range(2)], 0)
    y_prompt = np.concatenate([R[c]["yT"].T.reshape(4, SEQ, D) for c in range(4, 8)], axis=0)
    nk = np.concatenate([R[c]["kT_out"].T.reshape(4, SEQ, NH, 2, 64) for c in range(4, 8)], axis=0)
    nv = np.concatenate([R[c]["v_out"].reshape(4, SEQ, NH, 128) for c in range(4, 8)], axis=0)
    ns = np.concatenate([R[c]["s_out"] for c in range(4, 8)], axis=0)
    return (np.ascontiguousarray(y_prompt), np.ascontiguousarray(y_sample),
            np.ascontiguousarray(nk[:, None]), np.ascontiguousarray(nv[:, None]), np.ascontiguousarray(ns[:, None]))
```

```python
import numpy as np
import concourse.bass as bass
import concourse.mybir as mybir
from concourse.bass_utils import run_bass_kernel_spmd

F32 = mybir.dt.float32
BF16 = mybir.dt.bfloat16
AF = mybir.ActivationFunctionType
ALU = mybir.AluOpType

D = 1024
NC8 = 8
T = 1024
TK = 2048
TB = 512
NTB = T // TB
DFF = 2816
NF = DFF // 128
ALPHA = 2.0 ** 0.25
LN_EPS = 1e-5
STAGE = 1


class Op:
    __slots__ = ("idx", "eng", "fn", "dma", "deps", "needs_inc", "sem", "tick", "waits", "cc")

    def __init__(self, idx, eng, fn, dma):
        self.idx = idx
        self.eng = eng
        self.fn = fn
        self.dma = dma
        self.deps = set()
        self.needs_inc = False
        self.sem = None
        self.tick = 0
        self.waits = []
        self.cc = None


class Prog:
    ENGS = ("pe", "act", "dve", "pool", "sp")
    NDSEM = 8

    def __init__(self):
        self.ops = []
        self.last_w = {}
        self.last_r = {}
        self.bar = None
        self.bar_start = 0

    def op(self, eng, fn, r=(), w=(), dma=False, cc=None):
        if cc is not None:
            dma = True
        o = Op(len(self.ops), eng, fn, dma)
        o.cc = cc
        for k in r:
            lw = self.last_w.get(k)
            if lw is not None:
                o.deps.add(lw)
        for k in w:
            lw = self.last_w.get(k)
            if lw is not None:
                o.deps.add(lw)
            for ridx in self.last_r.get(k, {}).values():
                o.deps.add(ridx)
        rk = ("dma", o.idx) if dma else eng
        for k in r:
            self.last_r.setdefault(k, {})[rk] = o.idx
        for k in w:
            self.last_w[k] = o.idx
            self.last_r[k] = {}
        if self.bar is not None:
            o.deps.add(self.bar)
        o.deps.discard(o.idx)
        self.ops.append(o)
        return o

    def barrier(self, fn):
        o = Op(len(self.ops), "dve", fn, False)
        last = {}
        for p in self.ops[self.bar_start:]:
            if p.dma:
                o.deps.add(p.idx)
            else:
                last[p.eng] = p.idx
        for v in last.values():
            o.deps.add(v)
        if self.bar is not None:
            o.deps.add(self.bar)
        self.ops.append(o)
        self.bar = o.idx
        self.bar_start = o.idx
        self.last_w = {}
        self.last_r = {}
        return o

    def finalize(self, nc, sems, dsems):
        ops = self.ops
        for o in ops:
            for d in o.deps:
                do = ops[d]
                if do.dma:
                    continue
                if do.eng == "pe" and o.eng == "pe" and not o.dma:
                    continue
                do.needs_inc = True
        tick = {e: 0 for e in self.ENGS}
        dcount = {e: 0 for e in self.ENGS}
        dtick = {}
        prev_on_sem = {}
        for o in ops:
            if o.cc is not None:
                o.sem = o.cc
                o.tick = 1
            elif o.dma:
                q = o.eng
                si = dcount[q] % self.NDSEM
                dcount[q] += 1
                s = dsems[q][si]
                o.sem = s
                key = (q, si)
                dtick[key] = dtick.get(key, 0) + 16
                o.tick = dtick[key]
                if key in prev_on_sem:
                    o.waits.append((s, prev_on_sem[key]))
                prev_on_sem[key] = o.tick
            elif o.needs_inc:
                tick[o.eng] += 1
                o.sem = sems[o.eng]
                o.tick = tick[o.eng]
        for o in ops:
            for d in sorted(o.deps):
                do = ops[d]
                if do.dma:
                    o.waits.append((do.sem, do.tick))
                elif do.eng == "pe" and o.eng == "pe" and not o.dma:
                    continue
                else:
                    o.waits.append((do.sem, do.tick))
        self.final_dma = {k: v for k, v in dtick.items()}
        self.dsems = dsems

    def emit(self, eng, handle):
        waited = {}
        for o in self.ops:
            if o.eng != eng:
                continue
            for (s, t) in o.waits:
                sid = id(s)
                if waited.get(sid, 0) >= t:
                    continue
                waited[sid] = t
                handle.wait_ge(s, t)
            inst = o.fn(handle)
            if o.cc is not None:
                inst.then_inc(o.sem, 1)
            elif o.dma:
                inst.then_inc(o.sem, 16)
            elif o.needs_inc:
                inst.then_inc(o.sem, 1)
        if eng == "sp":
            for (q, si), t in self.final_dma.items():
                handle.wait_ge(self.dsems[q][si], t)


NH = 4
CTX = 256
NKT = (CTX + TK) // 128
CH = 32
NCH = T // CH
SEQ = 256
BIG = 65536.0
LAM_INIT = 0.2
RMS_EPS = 1e-6
ARENA_BYTES = 128 * 1024
PAIRS = [[0, 1], [2, 3], [4, 5], [6, 7]]
DEBUG = False


class Arena:
    def __init__(self, hb, hf):
        self.hb = hb
        self.hf = hf
        self.off = 0

    def mark(self):
        return self.off

    def reset(self, m=0):
        self.off = m

    def alloc(self, shape, dt):
        n = 1
        for v in shape:
            n *= v
        esz = 4 if dt == F32 else 2
        self.off = (self.off + 63) // 64 * 64
        o = self.off
        self.off += n * esz
        assert self.off <= ARENA_BYTES, ("arena overflow", self.off)
        if dt == F32:
            ap = self.hf[:, o // 4:o // 4 + n]
        else:
            ap = self.hb[:, o // 2:o // 2 + n]
        if len(shape) == 1:
            return ap
        if len(shape) == 2:
            return ap.rearrange("p (a b) -> p a b", b=shape[1])
        if len(shape) == 3:
            return ap.rearrange("p (a b c) -> p a b c", b=shape[1], c=shape[2])
        raise ValueError(shape)


def build_program():
    nc = bass.Bass("TRN2", target_bir_lowering=False)
    P = Prog()

    def din(name, shape, dt=F32):
        return nc.dram_tensor(name, list(shape), dt, kind="ExternalInput").ap()

    def dout(name, shape, dt=F32):
        return nc.dram_tensor(name, list(shape), dt, kind="ExternalOutput").ap()

    xT_d = din("xT", [D, T])
    cond_d = din("cond", [128, NC8])
    w_ada_d = din("w_ada", [D, 3072])
    w_ada0_d = din("w_ada0", [D, 3072])
    b_ada_d = din("b_ada", [128, 36])
    b_ada0_d = din("b_ada0", [128, 24])
    mb_src = nc.dram_tensor("mb_src", [128, 12], F32, kind="Internal").ap()
    mb_dst = nc.dram_tensor("mb_dst", [256, 12], F32, kind="Internal").ap()
    mc_src = nc.dram_tensor("mc_src", [128, 12], F32, kind="Internal").ap()
    mc_dst = nc.dram_tensor("mc_dst", [256, 12], F32, kind="Internal").ap()
    f1_win_d = din("ffn1_w_in", [D, 2 * DFF])
    f1_wout_d = din("ffn1_w_out", [DFF, D])
    f2_win_d = din("ffn2_w_in", [D, 2 * DFF])
    f2_wout_d = din("ffn2_w_out", [DFF, D])
    lng_d = din("ln_g", [128, 24])
    lnb_d = din("ln_b", [128, 24])
    wmix_d = din("w_mix_in", [D, 6144])
    wbra_d = din("w_branch_a", [512, D])
    wbrh_d = din("w_branch_h", [512, D])
    wmo_d = din("w_mix_out", [D, D])
    kctx_d = din("kctxT", [512, CTX])
    vctx_d = din("vctx", [CTX, 512])
    s0_d = din("s0", [2, NH, 128, 128])
    cos_d = din("cosT", [128, T])
    sin_d = din("sinT", [128, T])
    mbias_d = din("mbias", [128, NKT * (T // 256)])
    xw_d = din("xw", [128, 3])
    xk_src = nc.dram_tensor("xk_src", [512, T], BF16, kind="Internal").ap()
    xk_dst = nc.dram_tensor("xk_dst", [1024, T], BF16, kind="Internal").ap()
    xv_src = nc.dram_tensor("xv_src", [T, 512], BF16, kind="Internal").ap()
    xv_dst = nc.dram_tensor("xv_dst", [2 * T, 512], BF16, kind="Internal").ap()
    xs_src = nc.dram_tensor("xs_src", [512, 128], F32, kind="Internal").ap()
    xs_dst = nc.dram_tensor("xs_dst", [1024, 128], F32, kind="Internal").ap()
    keep_d = din("keep", [128, 1])
    lam_d = din("lamv", [128, 4, 64])
    subg_d = din("subln_g", [128, 1])
    hng_d = din("hnorm_g", [128, 1])
    lbl_d = din("lb_logits", [128, 2, 2, NH])
    ident_d = din("ident", [128, 128])
    rm_d = din("rotm", [128, 128])
    mf_d = din("maskf", [128, 128])
    mb_d = din("maskb", [128, 128])
    rowm_d = din("rowmask", [128, 4])

    yT_d = dout("yT", [D, T])
    kout_d = dout("kT_out", [512, T])
    vout_d = dout("v_out", [T, 512])
    sout_d = dout("s_out", [T // SEQ, 2, NH, 128, 128])
    if DEBUG:
        dbg_oa_d = dout("dbg_oa", [512, T])
        dbg_oh_d = dout("dbg_oh", [512, T])

    from contextlib import ExitStack
    es = ExitStack()

    def sb(name, shape, dt):
        return es.enter_context(nc.sbuf_tensor(name, list(shape), dt))

    def ps(name, shape, dt=F32):
        return es.enter_context(nc.psum_tensor(name, list(shape), dt))

    with es:
        x = sb("x", [128, NC8, T], F32)
        hmod = sb("hmod", [128, NC8, T], BF16)
        arena_b = sb("arena", [128, ARENA_BYTES // 2], BF16)
        arena_f = arena_b.bitcast(F32)
        AR = Arena(arena_b, arena_f)
        cond_sb = sb("cond_sb", [128, NC8], F32)
        scond = sb("scond", [128, NC8], BF16)
        bada = sb("bada", [128, 36], F32)
        modh = sb("modh", [128, 36], F32)
        bada0 = sb("bada0", [128, 24], F32)
        mod = sb("mod", [128, 72], F32)
        lng = sb("lng", [128, 24], F32)
        lnb = sb("lnb", [128, 24], F32)
        sc1 = sb("sc1", [128, 24], F32)
        gp = sb("gp", [128, 24], F32)
        lnA = sb("lnA", [128, 24], F32)
        lnB = sb("lnB", [128, 24], F32)
        ones_bf = sb("ones_bf", [128, 128], BF16)
        tmpf = [sb(f"tmpf{i}", [128, TB], F32) for i in range(9)]
        sg = tmpf[0:2]
        xn = tmpf[2:4]
        mean, msq, var, rstd, nmr = tmpf[4:9]
        rb = [sb(f"rb{i}", [128, TB], BF16) for i in range(2)]
        rsq = [sb(f"rsq{i}", [128, TB], BF16) for i in range(2)]
        ident = sb("ident_sb", [128, 128], BF16)
        rotm = sb("rotm_sb", [128, 128], F32)
        maskf = sb("maskf_sb", [128, 128], F32)
        maskb = sb("maskb_sb", [128, 128], F32)
        keep = sb("keep_sb", [128, 1], F32)
        xw = sb("xw_sb", [128, 3], F32)
        mbias = sb("mbias_sb", [128, NKT * (T // 256)], F32)
        rowmask = sb("rowmask_sb", [128, 4], F32)
        lamv = sb("lamv_sb", [128, 4, 64], F32)
        lamt = sb("lamt", [128, 2, 64], F32)
        lams = sb("lams", [128, 4], F32)
        neglam = sb("neglam", [128, 1], F32)
        subg = sb("subg", [128, 1], F32)
        hng = sb("hng", [128, 1], F32)
        lbl = sb("lbl", [128, 2, 2, NH], F32)
        lbv = sb("lbv", [128, 2, NH], F32)
        olb = sb("olb", [128, 2, NH], F32)
        dbar = sb("dbar", [128, 1], F32)
        pb = [ps(f"pb{i}", [128, 512]) for i in range(8)]
        pbb = [p_.bitcast(BF16) for p_ in pb]
        wada = [hmod[:, 4 * i:4 * i + 4, :].rearrange("p a (b n) -> p (a b) n", n=512) for i in range(2)]

        sems = {e: es.enter_context(nc.semaphore("s_" + e)) for e in ("pe", "act", "dve", "pool")}
        ccsem = [es.enter_context(nc.semaphore(f"cc{i}")) for i in range(5)]
        dsems = {q: [es.enter_context(nc.semaphore(f"d_{q}{i}")) for i in range(Prog.NDSEM)]
                 for q in ("sp", "pool", "act")}

        def DMA(q, out, in_, r=(), w=()):
            P.op(q, lambda h, out=out, in_=in_: h.dma_start(out=out, in_=in_), r=r, w=w, dma=True)

        def MM(out, lhsT, rhs, start, stop, r=(), w=()):
            P.op("pe", lambda h, out=out, lhsT=lhsT, rhs=rhs, start=start, stop=stop:
                 h.matmul(out, lhsT=lhsT, rhs=rhs, start=start, stop=stop), r=r, w=w)

        def TR(out, in_, r=(), w=()):
            P.op("pe", lambda h, out=out, in_=in_: h.transpose(out, in_, ident[:, :]), r=list(r) + ["ident"], w=w)

        def ACT(out, in_, func, r=(), w=(), bias=None, scale=None):
            def fn(h, out=out, in_=in_, func=func, bias=bias, scale=scale):
                kw = {}
                if bias is not None:
                    kw["bias"] = bias
                if scale is not None:
                    kw["scale"] = scale
                return h.activation(out=out, in_=in_, func=func, **kw)
            P.op("act", fn, r=r, w=w)

        def TS(eng, out, in0, s1, s2, op0, op1, r=(), w=()):
            P.op(eng, lambda h, out=out, in0=in0, s1=s1, s2=s2, op0=op0, op1=op1:
                 h.tensor_scalar(out=out, in0=in0, scalar1=s1, scalar2=s2, op0=op0, op1=op1), r=r, w=w)

        def TT(eng, out, in0, in1, op, r=(), w=()):
            P.op(eng, lambda h, out=out, in0=in0, in1=in1, op=op:
                 h.tensor_tensor(out=out, in0=in0, in1=in1, op=op), r=r, w=w)

        def STT(out, in0, scalar, in1, op0, op1, r=(), w=()):
            P.op("dve", lambda h, out=out, in0=in0, scalar=scalar, in1=in1, op0=op0, op1=op1:
                 h.scalar_tensor_tensor(out=out, in0=in0, scalar=scalar, in1=in1, op0=op0, op1=op1), r=r, w=w)

        def RECIP(out, in_, r=(), w=()):
            P.op("dve", lambda h, out=out, in_=in_: h.reciprocal(out=out, in_=in_), r=r, w=w)

        def BARRIER():
            P.barrier(lambda h: h.memset(dbar[:, :], 0.0))

        def tok(tb):
            return slice(tb * TB, (tb + 1) * TB)

        def wview(w_d, c0, n):
            return w_d[:, c0:c0 + n].rearrange("(k p) n -> p k n", p=128)

        P.op("dve", lambda h: h.memset(ones_bf[:, :], 1.0), w=["ones"])
        DMA("sp", cond_sb[:, :], cond_d[:, :], w=["cond"])
        DMA("sp", bada[:, :], b_ada_d[:, :], w=["bada"])
        DMA("sp", bada0[:, :], b_ada0_d[:, :], w=["bada0"])
        DMA("sp", lng[:, :], lng_d[:, :], w=["lng"])
        DMA("sp", lnb[:, :], lnb_d[:, :], w=["lnb"])
        for tb in range(NTB):
            for c in range(NC8):
                DMA("sp", x[:, c, tok(tb)], xT_d[c * 128:(c + 1) * 128, tok(tb)], w=[("x", c, tb)])
        DMA("sp", rotm[:, :], rm_d[:, :], w=["rotm"])
        DMA("sp", maskf[:, :], mf_d[:, :], w=["maskf"])
        DMA("sp", maskb[:, :], mb_d[:, :], w=["maskb"])
        DMA("sp", keep[:, :], keep_d[:, :], w=["keep"])
        DMA("sp", xw[:, :], xw_d[:, :], w=["xw"])
        DMA("sp", mbias[:, :], mbias_d[:, :], w=["mbias"])
        DMA("sp", rowmask[:, :], rowm_d[:, :], w=["rowmask"])
        DMA("sp", lamv[:, :, :], lam_d[:, :, :], w=["lamv"])
        DMA("sp", subg[:, :], subg_d[:, :], w=["subg"])
        DMA("sp", hng[:, :], hng_d[:, :], w=["hng"])
        DMA("sp", lbl[:, :, :, :], lbl_d[:, :, :, :], w=["lbl"])
        DMA("pool", ident[:, :], ident_d[:, :], w=["ident"])
        ACT(scond[:, :], cond_sb[:, :], AF.Silu, r=["cond"], w=["scond"])

        TT("dve", lamt[:, 0, :], lamv[:, 0, :], lamv[:, 1, :], ALU.mult, r=["lamv"], w=["lamt"])
        TT("dve", lamt[:, 1, :], lamv[:, 2, :], lamv[:, 3, :], ALU.mult, r=["lamv", "lamt"], w=["lamt"])
        for i in range(2):
            P.op("dve", lambda h, i=i: h.tensor_reduce(out=lams[:, i:i + 1], in_=lamt[:, i, :],
                                                       axis=mybir.AxisListType.X, op=ALU.add),
                 r=["lamt"], w=[("lams", i)])
        ACT(lams[:, 2:4], lams[:, 0:2], AF.Exp, r=[("lams", 0), ("lams", 1)], w=[("lams", 2)])
        TT("dve", neglam[:, :], lams[:, 3:4], lams[:, 2:3], ALU.subtract, r=[("lams", 2)], w=["neglam"])
        TS("dve", neglam[:, :], neglam[:, :], -LAM_INIT, None, ALU.add, ALU.bypass, r=["neglam"], w=["neglam"])
        TS("dve", subg[:, :], subg[:, :], 1.0 - LAM_INIT, None, ALU.mult, ALU.bypass, r=["subg"], w=["subg"])
        TT("dve", lbv[:, :, :], lbl[:, :, 0, :], lbl[:, :, 1, :], ALU.subtract, r=["lbl"], w=["lbv"])
        ACT(lbv[:, :, :], lbv[:, :, :], AF.Sigmoid, r=["lbv"], w=["lbv"])
        TS("dve", olb[:, :, :], lbv[:, :, :], -1.0, 1.0, ALU.mult, ALU.add, r=["lbv"], w=["olb"])

        state = {"win": 0, "wout": 0, "gu": 0, "dn": 0, "ln": 0}
        WIN_COLS = 256
        WOUT_COLS = 256
        def adaln_blocks(blks, stage, bank, col0, src=None):
            for blk in blks:
                slot = blk % 2
                if src is None:
                    DMA("pool", stage[slot], wview(w_ada_d, (blk - 3) * 512, 512), w=[("wada", slot)])
                else:
                    DMA("pool", stage[slot], wview(src, blk * 512, 512), w=[("wada", slot)])
                for jj in range(4):
                    j = blk * 4 + jj - col0
                    for k in range(NC8):
                        MM(pb[bank][:, j:j + 1], stage[slot][:, k, jj * 128:(jj + 1) * 128], scond[:, k:k + 1],
                           start=(k == 0), stop=(k == NC8 - 1),
                           r=[("wada", slot), "scond"], w=[("ps", bank)])

        def adaln_derive(s):
            base = s * 24
            gsc = (0.5 if s != 1 else 1.0) / ALPHA
            TS("dve", sc1[:, s * 8:(s + 1) * 8], mod[:, base + 8:base + 16], 1.0, None, ALU.add, ALU.bypass,
               r=[("mod", s)], w=[("sc1", s)])
            TS("dve", gp[:, s * 8:(s + 1) * 8], mod[:, base + 16:base + 24], gsc, None, ALU.mult, ALU.bypass,
               r=[("mod", s)], w=[("gp", s)])

        adaln_blocks(range(6), wada, 0, 0, src=w_ada0_d)
        TT("dve", mod[:, 0:24], pb[0][:, 0:24], bada0[:, :], ALU.add, r=[("ps", 0), "bada0"], w=[("mod", 0)])
        adaln_derive(0)
        pre_win = {}

        def prefetch_win(blk, win_g, win_u):
            slot = state["win"] % 2
            state["win"] += 1
            c0 = blk * WIN_COLS
            DMA("pool", win_g[slot], wview(f1_win_d, c0, WIN_COLS), w=[("wing", slot)])
            DMA("pool", win_u[slot], wview(f1_win_d, DFF + c0, WIN_COLS), w=[("winu", slot)])
            pre_win[blk] = slot

        PRE = {"fn": prefetch_win}
        def adaln_gather_a():
            pass

        rest_state = {}

        def ln_ab(s):
            TT("dve", lnA[:, s * 8:(s + 1) * 8], lng[:, s * 8:(s + 1) * 8], sc1[:, (s + 1) * 8:(s + 2) * 8],
               ALU.mult, r=["lng", ("sc1", s + 1)], w=[("lnA", s)])
            TT("dve", lnB[:, s * 8:(s + 1) * 8], lnb[:, s * 8:(s + 1) * 8], sc1[:, (s + 1) * 8:(s + 2) * 8],
               ALU.mult, r=["lnb", ("sc1", s + 1)], w=[("lnB", s)])
            TT("dve", lnB[:, s * 8:(s + 1) * 8], lnB[:, s * 8:(s + 1) * 8],
               mod[:, (s + 1) * 24:(s + 1) * 24 + 8], ALU.add,
               r=[("lnB", s), ("mod", s + 1)], w=[("lnB", s)])

        def adaln_gather(s, src_d, dst_d, sem):
            DMA("sp", src_d[:, :], modh[:, s * 12:(s + 1) * 12], r=[("modh", s)], w=[("mbs", s)])
            P.op("pool", lambda hh: hh.collective_compute("AllGather", ALU.bypass, replica_groups=PAIRS,
                                                          ins=[src_d], outs=[dst_d]),
                 r=[("mbs", s)], w=[("mbd", s)], cc=sem)
            for rk in range(2):
                DMA("sp", mod[:, s * 24 + rk * 12:s * 24 + (rk + 1) * 12], dst_d[rk * 128:(rk + 1) * 128, :],
                    r=[("mbd", s)], w=[("mod", s)])
            adaln_derive(s)
            ln_ab(s - 1)

        def adaln_rest_group(g):
            if g == 0:
                rest_state["stage"] = [AR.alloc([NC8, 512], BF16) for _ in range(2)]
                adaln_blocks([3, 4], rest_state["stage"], 7, 12)
            elif g == 1:
                adaln_blocks([5], rest_state["stage"], 7, 12)
            else:
                TT("dve", modh[:, 12:24], pb[7][:, 0:12], bada[:, 12:24], ALU.add, r=[("ps", 7), "bada"],
                   w=[("modh", 1)])
                adaln_gather(1, mb_src, mb_dst, ccsem[4])


        def layer_norm(s, tbs):
            for tb in tbs:
                for c in range(NC8):
                    bi = state["ln"] % 2
                    state["ln"] += 1
                    ACT(rb[bi][:, :], x[:, c, tok(tb)], AF.Copy, r=[("x", c, tb)], w=[("rb", bi)])
                    ACT(rsq[bi][:, :], x[:, c, tok(tb)], AF.Square, r=[("x", c, tb)], w=[("rsq", bi)])
                    MM(pb[6][:, :], ones_bf[:, :], rb[bi][:, :], start=(c == 0), stop=(c == NC8 - 1),
                       r=["ones", ("rb", bi)], w=[("ps", 6)])
                    MM(pb[7][:, :], ones_bf[:, :], rsq[bi][:, :], start=(c == 0), stop=(c == NC8 - 1),
                       r=["ones", ("rsq", bi)], w=[("ps", 7)])
                TS("dve", mean[:, :], pb[6][:, :], 1.0 / D, None, ALU.mult, ALU.bypass, r=[("ps", 6)], w=[("tmpf", 4)])
                TT("dve", msq[:, :], mean[:, :], mean[:, :], ALU.mult, r=[("tmpf", 4)], w=[("tmpf", 5)])
                STT(var[:, :], pb[7][:, :], 1.0 / D, msq[:, :], ALU.mult, ALU.subtract,
                    r=[("ps", 7), ("tmpf", 5)], w=[("tmpf", 6)])
                ACT(var[:, :], var[:, :], AF.Sqrt, r=[("tmpf", 6)], w=[("tmpf", 6)], bias=LN_EPS / (ALPHA * ALPHA))
                RECIP(rstd[:, :], var[:, :], r=[("tmpf", 6)], w=[("tmpf", 7)])
                STT(nmr[:, :], mean[:, :], -1.0, rstd[:, :], ALU.mult, ALU.mult, r=[("tmpf", 4), ("tmpf", 7)], w=[("tmpf", 8)])
                for c in range(NC8):
                    xi = c % 2
                    TT("dve", xn[xi][:, :], x[:, c, tok(tb)], rstd[:, :], ALU.mult,
                       r=[("x", c, tb), ("tmpf", 7)], w=[("tmpf", 2 + xi)])
                    TT("dve", xn[xi][:, :], xn[xi][:, :], nmr[:, :], ALU.add, r=[("tmpf", 2 + xi), ("tmpf", 8)], w=[("tmpf", 2 + xi)])
                    ACT(x[:, c, tok(tb)], xn[xi][:, :], AF.Identity, r=[("tmpf", 2 + xi), "lng", "lnb"], w=[("x", c, tb)],
                        scale=lng[:, s * 8 + c:s * 8 + c + 1], bias=lnb[:, s * 8 + c:s * 8 + c + 1])
                    if s < 2:
                        ACT(hmod[:, c, tok(tb)], xn[xi][:, :], AF.Identity,
                            r=[("tmpf", 2 + xi), ("lnA", s), ("lnB", s)], w=[("hmod", c, tb)],
                            scale=lnA[:, s * 8 + c:s * 8 + c + 1], bias=lnB[:, s * 8 + c:s * 8 + c + 1])

        def ffn_ln(s, win_d, wout_d, hook=None, pre=None, early_barrier=False):
            AR.reset(0)
            hbuf = AR.alloc([NF, 1024], BF16)
            win_g = [AR.alloc([NC8, WIN_COLS], BF16) for _ in range(2)]
            win_u = [AR.alloc([NC8, WIN_COLS], BF16) for _ in range(2)]
            wout = AR.alloc([NF, D], BF16)
            tbs = list(range(NTB))
            if pre is not None:
                pre(win_g, win_u)
            for blk in range(DFF // WIN_COLS):
                c0 = blk * WIN_COLS
                if blk in pre_win:
                    slot = pre_win.pop(blk)
                else:
                    slot = state["win"] % 2
                    state["win"] += 1
                    DMA("pool", win_g[slot], wview(win_d, c0, WIN_COLS), w=[("wing", slot)])
                    DMA("pool", win_u[slot], wview(win_d, DFF + c0, WIN_COLS), w=[("winu", slot)])
                wo0 = 3 if s == 0 else 2
                if wo0 <= blk < wo0 + 8:
                    p8 = blk - wo0
                    q4, jh = p8 // 2, p8 % 2
                    j0 = jh * (NF // 2)
                    DMA("pool", wout[:, j0:j0 + NF // 2, q4 * 256:(q4 + 1) * 256],
                        wout_d[j0 * 128:(j0 + NF // 2) * 128, q4 * 256:(q4 + 1) * 256].rearrange(
                            "(j p) n -> p j n", p=128), w=[("wout", q4, jh)])
                for jj in range(WIN_COLS // 128):
                    j = blk * (WIN_COLS // 128) + jj
                    for tb in tbs:
                        gi = state["gu"] % 2
                        state["gu"] += 1
                        pg, pu = pb[gi], pb[2 + gi]
                        for k in range(NC8):
                            MM(pg[:, :], win_g[slot][:, k, jj * 128:(jj + 1) * 128], hmod[:, k, tok(tb)],
                               start=(k == 0), stop=(k == NC8 - 1),
                               r=[("wing", slot), ("hmod", k, tb)], w=[("ps", gi)])
                        for k in range(NC8):
                            MM(pu[:, :], win_u[slot][:, k, jj * 128:(jj + 1) * 128], hmod[:, k, tok(tb)],
                               start=(k == 0), stop=(k == NC8 - 1),
                               r=[("winu", slot), ("hmod", k, tb)], w=[("ps", 2 + gi)])
                        ACT(sg[gi][:, :], pg[:, :], AF.Silu, r=[("ps", gi)], w=[("tmpf", gi)])
                        TT("dve", hbuf[:, j, tok(tb)], sg[gi][:, :], pu[:, :], ALU.mult,
                           r=[("tmpf", gi), ("ps", 2 + gi)], w=[("h", j, tb)])
            for tb in tbs:
                for c in range(NC8):
                    if hook is not None and tb == 0 and c in (2, 5):
                        hook((c - 2) // 3)
                    di = 4 + state["dn"] % 2
                    state["dn"] += 1
                    for j in range(NF):
                        MM(pb[di][:, :], wout[:, j, c * 128:(c + 1) * 128], hbuf[:, j, tok(tb)],
                           start=(j == 0), stop=(j == NF - 1),
                           r=[("wout", c // 2, 0), ("wout", c // 2, 1), ("h", j, tb)], w=[("ps", di)])
                    STT(x[:, c, tok(tb)], pb[di][:, :], gp[:, s * 8 + c:s * 8 + c + 1], x[:, c, tok(tb)],
                        ALU.mult, ALU.add, r=[("ps", di), ("gp", s), ("x", c, tb)], w=[("x", c, tb)])
                if hook is not None and tb == 0:
                    hook(2)
                if early_barrier and tb == tbs[-1]:
                    BARRIER()
                layer_norm(s, [tb])

        def ffn1_pre(win_g, win_u):
            prefetch_win(0, win_g, win_u)
            prefetch_win(1, win_g, win_u)
            adaln_gather_a()
            for tb in range(NTB):
                for c in range(NC8):
                    TS("dve", hmod[:, c, tok(tb)], x[:, c, tok(tb)], sc1[:, c:c + 1], mod[:, c:c + 1],
                       ALU.mult, ALU.add, r=[("x", c, tb), ("sc1", 0), ("mod", 0)], w=[("hmod", c, tb)])

        ffn_ln(0, f1_win_d, f1_wout_d, hook=adaln_rest_group, pre=ffn1_pre, early_barrier=True)

        AR.reset(0)
        oA = AR.alloc([NH, T], BF16)
        oH = AR.alloc([NH, T], BF16)
        m_mix = AR.mark()
        stg = tmpf

        cosT = AR.alloc([T], F32)
        sinT = AR.alloc([T], F32)
        wk = [AR.alloc([NC8, 128], BF16) for _ in range(2)]
        wv_all = AR.alloc([NC8, 512], BF16)
        wq = [AR.alloc([NC8, 128], BF16) for _ in range(2)]
        kown = [AR.alloc([T], BF16) for _ in range(2)]
        vown_all = AR.alloc([T // 128, 512], BF16)
        qPs = [AR.alloc([T // 256, 512], BF16) for _ in range(NH)]
        kT = [AR.alloc([CTX + TK], BF16) for _ in range(2)]
        Vt = [AR.alloc([NKT, 128], BF16) for _ in range(2)]
        Eb = [AR.alloc([TB], BF16) for _ in range(5)]
        sqb2 = [AR.alloc([256], BF16) for _ in range(2)]
        rms_eps_t = AR.alloc([1], F32)
        P.op("dve", lambda h: h.memset(rms_eps_t, RMS_EPS), w=["rmseps"])
        stage2 = [AR.alloc([NC8, 512], BF16) for _ in range(3)]
        DMA("sp", cosT, cos_d[:, :], w=["cos"])
        DMA("sp", sinT, sin_d[:, :], w=["sin"])
        for h_ in range(NH):
            P.op("dve", lambda h, h_=h_: h.memset(qPs[h_], 0.0), w=[("qT", h_, tb) for tb in range(NTB)])

        def rope(raw, rot_ps, key_raw, key_ps, tb, outs, okeys, wkeys, split=False):
            t1, t2 = stg[2], stg[3]
            TT("dve", t1[:, :], raw, cosT[:, tok(tb)], ALU.mult, r=[key_raw, "cos"], w=[("tmpf", 2)])
            TT("dve", t2[:, :], rot_ps, sinT[:, tok(tb)], ALU.mult, r=[key_ps, "sin"], w=[("tmpf", 3)])
            for (o_ap, psl) in outs:
                a1, a2 = t1[psl, :], t2[psl, :]
                if split:
                    a1 = a1.rearrange("p (a b) -> p a b", b=256)
                    a2 = a2.rearrange("p (a b) -> p a b", b=256)
                TT("dve", o_ap, a1, a2, ALU.add, r=[("tmpf", 2), ("tmpf", 3)] + okeys, w=wkeys)

        DMA("pool", wv_all, wview(wmix_d, 1024, 512), w=["wv_all"])
        for tt in range(T // 128):
            slot = 4 + tt % 2
            for k in range(NC8):
                MM(pb[slot][:, :], hmod[:, k, tt * 128:(tt + 1) * 128], wv_all[:, k, :],
                   start=(k == 0), stop=(k == NC8 - 1),
                   r=["wv_all", ("hmod", k, tt // 4)], w=[("ps", slot)])
            vst = stg[4 + tt % 2]
            ACT(vst[:, :], pb[slot][:, :], AF.Copy, r=[("ps", slot)], w=[("tmpf", 4 + tt % 2)])
            DMA("sp", vout_d[tt * 128:(tt + 1) * 128, :], vst[:, :], r=[("tmpf", 4 + tt % 2)])
            TS("dve", vown_all[:, tt, :], vst[:, :], 1.0, None, ALU.mult, ALU.bypass,
               r=[("tmpf", 4 + tt % 2)], w=["vown_all"])
        DMA("sp", xv_src.rearrange("(t p) n -> p t n", p=128), vown_all, r=["vown_all"], w=["xv_src"])
        for h in range(NH):
            sl = h % 2
            DMA("pool", wk[sl], wview(wmix_d, 512 + h * 128, 128), w=[("wk", sl)])
            if h == NH - 1:
                for i_ in range(3):
                    DMA("pool", stage2[i_], wview(w_ada_d, (3 + i_) * 512, 512), w=[("wada2", i_)])
            for tb in range(NTB):
                pp = pb[tb % 2]
                for k in range(NC8):
                    MM(pp[:, :], wk[sl][:, k, :], hmod[:, k, tok(tb)], start=(k == 0), stop=(k == NC8 - 1),
                       r=[("wk", sl), ("hmod", k, tb)], w=[("ps", tb % 2)])
            for tb in range(NTB):
                pp = pb[tb % 2]
                raw = stg[tb % 2]
                ACT(raw[:, :], pp[:, :], AF.Copy, r=[("ps", tb % 2)], w=[("tmpf", tb % 2)])
                DMA("sp", kout_d[h * 128:(h + 1) * 128, tok(tb)], raw[:, :], r=[("tmpf", tb % 2)])
                MM(pb[2 + tb % 2][:, :], rotm[:, :], raw[:, :], start=True, stop=True,
                   r=["rotm", ("tmpf", tb % 2)], w=[("ps", 2 + tb % 2)])
            for tb in range(NTB):
                raw = stg[tb % 2]
                rope(raw[:, :], pb[2 + tb % 2][:, :], ("tmpf", tb % 2), ("ps", 2 + tb % 2), tb,
                     [(kown[sl][:, tok(tb)], slice(0, 128))], [], [("kown", sl)])
            DMA("sp", xk_src[h * 128:(h + 1) * 128, :], kown[sl], r=[("kown", sl)], w=["xk_src"])
        def a2_loads(h):
            sl = h % 2
            DMA("pool", kT[sl][:, 0:CTX], kctx_d[h * 128:(h + 1) * 128, :], w=[("kT", sl)])
            DMA("pool", Vt[sl][:, 0:CTX // 128, :],
                vctx_d[:, h * 128:(h + 1) * 128].rearrange("(t p) n -> p t n", p=128), w=[("Vt", sl)])

        DMA("pool", wq[0], wview(wmix_d, 0, 128), w=[("wq", 0)])
        DMA("pool", wq[1], wview(wmix_d, 128, 128), w=[("wq", 1)])
        a2_loads(0)
        P.op("pool", lambda hh: hh.collective_compute("AllGather", ALU.bypass, replica_groups=PAIRS,
                                                      ins=[xk_src], outs=[xk_dst]),
             r=["xk_src"], w=["xk_dst"], cc=ccsem[0])
        P.op("pool", lambda hh: hh.collective_compute("AllGather", ALU.bypass, replica_groups=PAIRS,
                                                      ins=[xv_src], outs=[xv_dst]),
             r=["xv_src"], w=["xv_dst"], cc=ccsem[1])

        def adaln_b2():
            for i_ in range(3):
                for jj in range(4):
                    j = i_ * 4 + jj
                    for k in range(NC8):
                        MM(pb[7][:, j:j + 1], stage2[i_][:, k, jj * 128:(jj + 1) * 128], scond[:, k:k + 1],
                           start=(k == 0), stop=(k == NC8 - 1), r=[("wada2", i_), "scond"], w=[("ps", 7)])
            TT("dve", modh[:, 24:36], pb[7][:, 0:12], bada[:, 24:36], ALU.add, r=[("ps", 7), "bada"],
               w=[("modh", 2)])
            adaln_gather(2, mc_src, mc_dst, ccsem[3])

        for h in range(NH):
            sl = h % 2
            if h >= 2:
                DMA("pool", wq[sl], wview(wmix_d, h * 128, 128), w=[("wq", sl)])
            for tb in range(NTB):
                pp = pb[tb % 2]
                for k in range(NC8):
                    MM(pp[:, :], wq[sl][:, k, :], hmod[:, k, tok(tb)], start=(k == 0), stop=(k == NC8 - 1),
                       r=[("wq", sl), ("hmod", k, tb)], w=[("ps", tb % 2)])
                raw = stg[tb % 2]
                ACT(raw[:, :], pp[:, :], AF.Copy, r=[("ps", tb % 2)], w=[("tmpf", tb % 2)])
                MM(pb[2 + tb % 2][:, :], rotm[:, :], raw[:, :], start=True, stop=True,
                   r=["rotm", ("tmpf", tb % 2)], w=[("ps", 2 + tb % 2)])
                rope(raw[:, :], pb[2 + tb % 2][:, :], ("tmpf", tb % 2), ("ps", 2 + tb % 2), tb,
                     [(qPs[h][0:64, 2 * tb:2 * tb + 2, 0:256], slice(0, 64)),
                      (qPs[h][64:128, 2 * tb:2 * tb + 2, 256:512], slice(64, 128))],
                     [("qT", h, tb)], [("qT", h, tb)], split=True)

        cnt = {"e": 0, "s": 0, "it": 0}
        pend = []
        for h in range(NH):
            sl = h % 2
            if h > 0:
                a2_loads(h)
            for rk in range(2):
                DMA("sp", kT[sl][:, CTX + rk * T:CTX + (rk + 1) * T], xk_dst[rk * 512 + h * 128:rk * 512 + (h + 1) * 128, :],
                    r=["xk_dst"], w=[("kT", sl)])
                DMA("sp", Vt[sl][:, CTX // 128 + rk * (T // 128):CTX // 128 + (rk + 1) * (T // 128), :],
                    xv_dst[rk * T:(rk + 1) * T, h * 128:(h + 1) * 128].rearrange("(t p) n -> p t n", p=128),
                    r=["xv_dst"], w=[("Vt", sl)])
            if h == 0:
                adaln_b2()
            NQ = T // 256
            its = [(qi, kt) for qi in range(NQ) for kt in range(NKT)]
            slots = {}

            def front(i, its=its, slots=slots, sl=sl, h=h):
                qi, kt = its[i]
                si = (0, 1, 2, 7)[cnt["s"] % 4]
                cnt["s"] += 1
                ei = cnt["e"] % 5
                cnt["e"] += 1
                slots[i] = ei
                MM(pb[si][:, :], kT[sl][:, kt * 128:(kt + 1) * 128], qPs[h][:, qi, :], start=True, stop=True,
                   r=[("kT", sl), ("qT", h, qi // 2)], w=[("ps", si)])
                mcol = kt * NQ + qi
                ACT(Eb[ei][:, :], pb[si][:, :], AF.Exp, r=[("ps", si), "mbias"], w=[("E", ei)], scale=0.125,
                    bias=mbias[:, mcol:mcol + 1])

            def back(i, its=its, slots=slots, sl=sl):
                qi, kt = its[i]
                ei = slots[i]
                MM(pb[3 + qi % 2][:, :], Vt[sl][:, kt, :], Eb[ei][:, :], start=(kt == 0), stop=(kt == NKT - 1),
                   r=[("Vt", sl), ("E", ei)], w=[("ps", 3 + qi % 2)])
                MM(pb[5 + qi % 2][:, :], ones_bf[:, :], Eb[ei][:, :], start=(kt == 0), stop=(kt == NKT - 1),
                   r=["ones", ("E", ei)], w=[("ps", 5 + qi % 2)])

            def combine_stages(qi, h=h):
                par = qi % 2
                rz, tt_ = stg[4 + par], stg[6 + par]
                qs_ = slice(qi * 256, (qi + 1) * 256)

                def s1():
                    RECIP(rz[:, :], pb[5 + par][:, :], r=[("ps", 5 + par)], w=[("tmpf", 4 + par)])
                    TT("dve", tt_[:, :], pb[3 + par][:, :], rz[:, :], ALU.mult,
                       r=[("ps", 3 + par), ("tmpf", 4 + par)], w=[("tmpf", 6 + par)])
                    STT(tt_[:, 0:256], tt_[:, 256:512], neglam[:, 0:1], tt_[:, 0:256], ALU.mult, ALU.add,
                        r=[("tmpf", 6 + par), "neglam"], w=[("tmpf", 6 + par)])
                    TT("dve", sqb2[par][:, 0:256], tt_[:, 0:256], tt_[:, 0:256], ALU.mult,
                       r=[("tmpf", 6 + par)], w=[("sqb", par)])

                def s2():
                    MM(pb[3 + par][:, 0:256], ones_bf[:, :], sqb2[par][:, 0:256], start=True, stop=True,
                       r=["ones", ("sqb", par)], w=[("ps", 3 + par)])

                def s3():
                    ACT(rz[:, 0:256], pb[3 + par][:, 0:256], AF.Ln, r=[("ps", 3 + par)],
                        w=[("tmpf", 4 + par)], scale=1.0 / 128, bias=rms_eps_t[:, 0:1])
                    ACT(rz[:, 0:256], rz[:, 0:256], AF.Exp, r=[("tmpf", 4 + par)], w=[("tmpf", 4 + par)], scale=-0.5)
                    TT("dve", tt_[:, 0:256], tt_[:, 0:256], rz[:, 0:256], ALU.mult,
                       r=[("tmpf", 6 + par), ("tmpf", 4 + par)], w=[("tmpf", 6 + par)])
                    TS("dve", oA[:, h, qs_], tt_[:, 0:256], subg[:, 0:1], None, ALU.mult, ALU.bypass,
                       r=[("tmpf", 6 + par), "subg"], w=[("oA", h, qi // 2)])
                return [(0, s1), (5, s2), (9, s3)]

            PF = 3
            for i in range(min(PF, len(its))):
                front(i)
            for i in range(len(its)):
                if i + PF < len(its):
                    front(i + PF)
                back(i)
                if its[i][1] == NKT - 1:
                    for (dl, fn_) in combine_stages(its[i][0]):
                        pend.append([cnt["it"] + dl, fn_])
                cnt["it"] += 1
                for pe_ in [p_ for p_ in pend if p_[0] <= cnt["it"]]:
                    pe_[1]()
                    pend.remove(pe_)
        for pe_ in sorted(pend, key=lambda p_: p_[0]):
            pe_[1]()
        if DEBUG:
            for h in range(NH):
                DMA("pool", dbg_oa_d[h * 128:(h + 1) * 128, :], oA[:, h, :],
                    r=[("oA", h, qb) for qb in range(NTB)])
        BARRIER()

        AR.reset(m_mix)
        NB128 = T // 128
        Qt = [[AR.alloc([T], BF16) for _ in range(2)] for _ in range(NH)]
        Khm = [[AR.alloc([NB128, 128], BF16) for _ in range(2)] for _ in range(NH)]
        decs = [[AR.alloc([NCH], F32) for _ in range(2)] for _ in range(NH)]
        Vh = [AR.alloc([NB128, 128], BF16) for _ in range(NH)]
        gsil = [AR.alloc([T], BF16) for _ in range(NH)]
        opart = [AR.alloc([T], F32) for _ in range(NH)]
        Sf = [AR.alloc([128], F32) for _ in range(NH)]
        Sb = [AR.alloc([128], BF16) for _ in range(NH)]
        Sst = [AR.alloc([128], F32) for _ in range(4)]
        Sx = [[AR.alloc([128], F32) for _ in range(2)] for _ in range(NH)]
        ATm = [AR.alloc([128], BF16) for _ in range(4)]
        Vmk = [AR.alloc([128], BF16) for _ in range(4)]
        whq = [AR.alloc([NC8, 128], BF16) for _ in range(2)]
        whf0 = [AR.alloc([NC8, 128], BF16) for _ in range(2)]
        whf1 = [AR.alloc([NC8, 128], BF16) for _ in range(2)]
        whi = [AR.alloc([NC8, 128], BF16) for _ in range(2)]
        whg = [AR.alloc([NC8, 128], BF16) for _ in range(2)]
        tG2 = [AR.alloc([TB + 1], F32) for _ in range(2)]
        qs = tmpf[0][:, :]
        onesf = tmpf[1][:, :]
        osum = tmpf[2][:, :]
        tE = tmpf[3][:, :]
        tf2 = [tmpf[4][:, :], tmpf[5][:, :]]
        tlog2 = [tmpf[6][:, :], tmpf[7][:, :]]
        tkk2 = [tmpf[8][:, :], AR.alloc([TB], F32)]
        tE2 = [AR.alloc([TB], F32) for _ in range(2)]
        tEi2 = [AR.alloc([TB], F32) for _ in range(2)]
        Kt2 = [rb[1][:, :], rsq[1][:, :]]
        Kh2 = [rsq[0][:, :], AR.alloc([TB], BF16)]
        P.op("dve", lambda h: h.memset(onesf, 1.0), w=["onesf"])
        one_t = AR.alloc([1], F32)
        P.op("dve", lambda h: h.memset(one_t, 1.0), w=["one_t"])
        eps_t = AR.alloc([1], F32)
        P.op("dve", lambda h: h.memset(eps_t, RMS_EPS), w=["eps_t"])
        for d in range(2):
            P.op("dve", lambda h, d=d: h.memset(tG2[d][:, 0:1], 0.0), w=[("tG", d)])
        NCB = TB // CH
        scnt = {"st": 0, "kv": 0, "oi": 0, "at": 0}
        whf = (whf0, whf1)

        def head_loads(h):
            sl = h % 2
            DMA("pool", whq[sl], wview(wmix_d, 1536 + h * 128, 128), w=[("whq", sl)])
            DMA("pool", whf0[sl], wview(wmix_d, 2048 + h * 128, 128), w=[("whf", 0, sl)])
            DMA("pool", whf1[sl], wview(wmix_d, 2560 + h * 128, 128), w=[("whf", 1, sl)])
            DMA("pool", whi[sl], wview(wmix_d, 3072 + h * 128, 128), w=[("whi", sl)])
            DMA("pool", whg[sl], wview(wmix_d, 3584 + h * 128, 128), w=[("whg", sl)])
            DMA("sp", Sf[h], s0_d[0, h, :, :], w=[("Sf", h)])

        def vproj(h):
            sl = h % 2
            for tt in range(NB128):
                for k in range(NC8):
                    MM(pb[6][:, 0:128], hmod[:, k, tt * 128:(tt + 1) * 128], whi[sl][:, k, :],
                       start=(k == 0), stop=(k == NC8 - 1), r=[("whi", sl), ("hmod", k, tt // 4)], w=[("ps", 6)])
                ACT(Vh[h][:, tt, :], pb[6][:, 0:128], AF.Copy, r=[("ps", 6)], w=[("Vh", h)])

        def proj(h, tb):
            sl = h % 2
            for k in range(NC8):
                MM(pb[2][:, :], whq[sl][:, k, :], hmod[:, k, tok(tb)], start=(k == 0), stop=(k == NC8 - 1),
                   r=[("whq", sl), ("hmod", k, tb)], w=[("ps", 2)])
            for k in range(NC8):
                MM(pb[tb][:, :], whg[sl][:, k, :], hmod[:, k, tok(tb)], start=(k == 0), stop=(k == NC8 - 1),
                   r=[("whg", sl), ("hmod", k, tb)], w=[("ps", tb)])
            for d in range(2):
                for k in range(NC8):
                    MM(pb[4 + d][:, :], whf[d][sl][:, k, :], hmod[:, k, tok(tb)], start=(k == 0),
                       stop=(k == NC8 - 1), r=[("whf", d, sl), ("hmod", k, tb)], w=[("ps", 4 + d)])

        items = [(h, tb) for h in range(NH) for tb in range(NTB)]
        head_loads(0)
        vproj(0)
        proj(0, 0)
        for idx, (h, tb) in enumerate(items):
            sl = h % 2
            if True:
                ACT(qs, pb[2][:, :], AF.Silu, r=[("ps", 2)], w=["qs"])
                ACT(gsil[h][:, tok(tb)], pb[tb][:, :], AF.Silu, r=[("ps", tb)], w=[("gsil", h)])
                def gen(d, h=h, sl=sl, tb=tb):
                    tf, tlog, tkk, tG, tE, tEi, Kt, Kh = (tf2[d], tlog2[d], tkk2[d], tG2[d], tE2[d], tEi2[d],
                                                          Kt2[d], Kh2[d])
                    kd = lambda n: (n, d)
                    pz_ = pb[4 + d]
                    ACT(tf, pz_[:, :], AF.Sigmoid, r=[("ps", 4 + d)], w=[kd("tf")])
                    yield
                    TS("dve", tf, tf, olb[:, d, h:h + 1], lbv[:, d, h:h + 1], ALU.mult, ALU.add,
                       r=[kd("tf"), "olb", "lbv"], w=[kd("tf")])
                    yield
                    ACT(tlog, tf, AF.Ln, r=[kd("tf")], w=[kd("tlog")])
                    yield
                    ACT(tkk, tf, AF.Identity, r=[kd("tf")], w=[kd("tkk")], scale=-1.0, bias=one_t[:, 0:1])
                    P.op("dve", lambda hh, tG=tG, tlog=tlog: hh.tensor_tensor_scan(
                        out=tG[:, 1:TB + 1], data0=onesf, data1=tlog, initial=0.0, op0=ALU.mult, op1=ALU.add),
                         r=["onesf", kd("tlog")], w=[kd("tG")])
                    yield
                    G3 = tG[:, 1:TB + 1].rearrange("p (c j) -> p c j", j=CH)
                    if d == 0:
                        gprev = tG[:, 0:TB].rearrange("p (c j) -> p c j", j=CH)[:, :, 0:1].to_broadcast([128, NCB, CH])
                        TT("dve", tE.rearrange("p (c j) -> p c j", j=CH), G3, gprev, ALU.subtract,
                           r=[kd("tG")], w=[kd("tE")])
                    else:
                        gend = G3[:, :, CH - 1:CH].to_broadcast([128, NCB, CH])
                        TT("dve", tE, tlog, tG[:, 1:TB + 1], ALU.subtract, r=[kd("tG"), kd("tlog")], w=[kd("tE")])
                        yield
                        TT("dve", tE.rearrange("p (c j) -> p c j", j=CH), tE.rearrange("p (c j) -> p c j", j=CH),
                           gend, ALU.add, r=[kd("tE"), kd("tG")], w=[kd("tE")])
                    yield
                    ACT(tEi, tE, AF.Exp, r=[kd("tE")], w=[kd("tEi")], scale=-1.0)
                    ACT(tE, tE, AF.Exp, r=[kd("tE")], w=[kd("tE")])
                    yield
                    E3 = tE.rearrange("p (c j) -> p c j", j=CH)
                    edge = E3[:, :, CH - 1:CH] if d == 0 else E3[:, :, 0:1]
                    TT("dve", Qt[h][d][:, tok(tb)], qs, tE, ALU.mult, r=["qs", kd("tE")], w=[("Qt", h, d)])
                    TT("dve", Kt, tkk, tEi, ALU.mult, r=[kd("tkk"), kd("tEi")], w=[kd("Kt")])
                    yield
                    TT("dve", Kh.rearrange("p (c j) -> p c j", j=CH), Kt.rearrange("p (c j) -> p c j", j=CH),
                       edge.to_broadcast([128, NCB, CH]), ALU.mult, r=[kd("Kt"), kd("tE")], w=[kd("Kh")])
                    TS("dve", decs[h][d][:, tb * NCB:(tb + 1) * NCB].rearrange("p (c o) -> p c o", o=1), edge,
                       1.0, None, ALU.mult, ALU.bypass, r=[kd("tE")], w=[("dec", h, d)])
                    yield
                    tbank = 7
                    abank = 3 if d == 0 else 6
                    for bb in range(TB // 128):
                        blk = tb * 4 + bb
                        bs = slice(bb * 128, (bb + 1) * 128)
                        gs = slice(tb * TB + bb * 128, tb * TB + (bb + 1) * 128)
                        TR(pbb[tbank][:, 0:128], Kh[:, bs], r=[kd("Kh")], w=[("ps", tbank)])
                        ACT(Khm[h][d][:, blk, :], pbb[tbank][:, 0:128], AF.Copy, r=[("ps", tbank)], w=[("Khm", h, d)])
                        MM(pb[abank][:, 0:128], Kt[:, bs], Qt[h][d][:, gs], start=True, stop=True,
                           r=[kd("Kt"), ("Qt", h, d)], w=[("ps", abank)])
                        yield
                        ai = d * 2 + bb % 2
                        TT("dve", ATm[ai], pb[abank][:, 0:128], (maskf, maskb)[d][:, :], ALU.mult,
                           r=[("ps", abank), "maskf", "maskb"], w=[("ATm", ai)])
                        MM(pb[tb][:, bs], Vh[h][:, blk, :], ATm[ai], start=(d == 0 and bb == 0),
                           stop=(d == 1 and bb == TB // 128 - 1),
                           r=[("Vh", h), ("ATm", ai)], w=[("ps", tb)])
                        yield

                alive = [gen(0), gen(1)]
                rounds = 0
                while alive:
                    for g_ in list(alive):
                        try:
                            next(g_)
                        except StopIteration:
                            alive.remove(g_)
                    rounds += 1
                    if rounds == 1 and idx + 1 < len(items):
                        nh, ntb = items[idx + 1]
                        if ntb == 0:
                            head_loads(nh)
                            vproj(nh)
                        proj(nh, ntb)
                ACT(opart[h][:, tok(tb)], pb[tb][:, :], AF.Copy, r=[("ps", tb)], w=[("opart", h)])
                if tb == NTB - 1:
                    ACT(Sb[h], Sf[h], AF.Copy, r=[("Sf", h)], w=[("Sb", h)])

        def chain_step(h, d, c):
            gcol = slice(c * CH, (c + 1) * CH)
            blk = (c * CH) // 128
            jrow = ((c * CH) % 128) // CH
            oi = scnt["oi"] % 4
            scnt["oi"] += 1
            MM(pb[oi][:, 0:CH], Sb[h], Qt[h][d][:, gcol], start=True, stop=True,
               r=[("Sb", h), ("Qt", h, d)], w=[("ps", oi)])
            TT("dve", opart[h][:, gcol], opart[h][:, gcol], pb[oi][:, 0:CH], ALU.add,
               r=[("opart", h), ("ps", oi)], w=[("opart", h)])
            kv = 4 + scnt["kv"] % 4
            vmi = scnt["kv"] % 4
            scnt["kv"] += 1
            TS("pool", Vmk[vmi], Vh[h][:, blk, :], rowmask[:, jrow:jrow + 1], 1.0, ALU.mult, ALU.mult,
               r=[("Vh", h), "rowmask"], w=[("Vmk", vmi)])
            MM(pb[kv][:, 0:128], Khm[h][d][:, blk, :], Vmk[vmi], start=True, stop=True,
               r=[("Khm", h, d), ("Vmk", vmi)], w=[("ps", kv)])
            STT(Sf[h], Sf[h], decs[h][d][:, c:c + 1], pb[kv][:, 0:128], ALU.mult, ALU.add,
                r=[("Sf", h), ("dec", h, d), ("ps", kv)], w=[("Sf", h)])
            at_boundary = ((c + 1) % (SEQ // CH) == 0) if d == 0 else (c % (SEQ // CH) == 0)
            if at_boundary:
                seq = c // (SEQ // CH)
                si = scnt["st"] % 4
                scnt["st"] += 1
                ACT(Sst[si], Sf[h], AF.Copy, r=[("Sf", h)], w=[("Sst", si)])
                DMA("sp", sout_d[seq, d, h, :, :], Sst[si], r=[("Sst", si)])
                TS("dve", Sf[h], Sf[h], keep[:, 0:1], None, ALU.mult, ALU.bypass,
                   r=[("Sf", h), "keep"], w=[("Sf", h)])
            ACT(Sb[h], Sf[h], AF.Copy, r=[("Sf", h)], w=[("Sb", h)])

        for step in range(NCH):
            for h in range(NH):
                chain_step(h, 0, step)
        for h in range(NH):
            DMA("sp", xs_src[h * 128:(h + 1) * 128, :], Sf[h], r=[("Sf", h)], w=["xs_src"])
        P.op("pool", lambda hh: hh.collective_compute("AllGather", ALU.bypass, replica_groups=PAIRS,
                                                      ins=[xs_src], outs=[xs_dst]),
             r=["xs_src"], w=["xs_dst"], cc=ccsem[2])
        for h in range(NH):
            for rk in range(2):
                DMA("sp", Sx[h][rk], xs_dst[rk * 512 + h * 128:rk * 512 + (h + 1) * 128, :], r=["xs_dst"],
                    w=[("Sx", h, rk)])
            DMA("sp", Sf[h], s0_d[1, h, :, :], w=[("Sf", h)])
            TS("dve", Sf[h], Sf[h], xw[:, 2:3], None, ALU.mult, ALU.bypass, r=[("Sf", h), "xw"], w=[("Sf", h)])
            STT(Sf[h], Sx[h][0], xw[:, 0:1], Sf[h], ALU.mult, ALU.add, r=[("Sx", h, 0), "xw", ("Sf", h)],
                w=[("Sf", h)])
            STT(Sf[h], Sx[h][1], xw[:, 1:2], Sf[h], ALU.mult, ALU.add, r=[("Sx", h, 1), "xw", ("Sf", h)],
                w=[("Sf", h)])
            ACT(Sb[h], Sf[h], AF.Copy, r=[("Sf", h)], w=[("Sb", h)])
        for step in range(NCH):
            for h in range(NH):
                chain_step(h, 1, NCH - 1 - step)
        BARRIER()
        fsq = [rb[0][:, :], rb[1][:, :]]
        frs = [tmpf[4][:, :], tmpf[5][:, :]]
        fos = [tmpf[6][:, :], tmpf[7][:, :]]
        fi = 0
        for h in range(NH):
            for tb in range(NTB):
                q_ = fi % 2
                bank = fi % 4
                fi += 1
                TT("dve", fsq[q_], opart[h][:, tok(tb)], opart[h][:, tok(tb)], ALU.mult,
                   r=[("opart", h)], w=[("fsq", q_)])
                MM(pb[bank][:, :], ones_bf[:, :], fsq[q_], start=True, stop=True, r=["ones", ("fsq", q_)],
                   w=[("ps", bank)])
                ACT(frs[q_], pb[bank][:, :], AF.Ln, r=[("ps", bank)], w=[("frs", q_)], scale=1.0 / 128,
                    bias=eps_t[:, 0:1])
                ACT(frs[q_], frs[q_], AF.Exp, r=[("frs", q_)], w=[("frs", q_)], scale=-0.5)
                TT("dve", fos[q_], opart[h][:, tok(tb)], frs[q_], ALU.mult, r=[("opart", h), ("frs", q_)],
                   w=[("fos", q_)])
                STT(oH[:, h, tok(tb)], fos[q_], hng[:, 0:1], gsil[h][:, tok(tb)], ALU.mult, ALU.mult,
                    r=[("fos", q_), "hng", ("gsil", h)], w=[("oH", h, tb)])
        if DEBUG:
            for h in range(NH):
                DMA("pool", dbg_oh_d[h * 128:(h + 1) * 128, :], oH[:, h, :],
                    r=[("oH", h, tb) for tb in range(NTB)])
        BARRIER()

        AR.reset(m_mix)
        merged = AR.alloc([NC8, 1024], BF16)
        CB = 256
        wga = [AR.alloc([NC8, CB], BF16) for _ in range(2)]
        wgh = [AR.alloc([NC8, CB], BF16) for _ in range(2)]
        wba = [AR.alloc([NH, CB], BF16) for _ in range(2)]
        wbh = [AR.alloc([NH, CB], BF16) for _ in range(2)]
        wmo_all = AR.alloc([NC8, D], BF16)
        mcnt = {"w": 0, "o": 0}
        for half in range(T // 1024):
            tbs = [half * 2, half * 2 + 1]
            for blk in range(D // CB):
                sl = mcnt["w"] % 2
                mcnt["w"] += 1
                c0 = blk * CB
                DMA("pool", wga[sl], wview(wmix_d, 4096 + c0, CB), w=[("wga", sl)])
                DMA("pool", wgh[sl], wview(wmix_d, 5120 + c0, CB), w=[("wgh", sl)])
                DMA("pool", wba[sl], wview(wbra_d, c0, CB), w=[("wba", sl)])
                DMA("pool", wbh[sl], wview(wbrh_d, c0, CB), w=[("wbh", sl)])
                DMA("pool", wmo_all[:, :, c0:c0 + CB], wview(wmo_d, c0, CB), w=[("wmo", blk)])
                for cc in range(CB // 128):
                    dc = blk * (CB // 128) + cc
                    cs = slice(cc * 128, (cc + 1) * 128)
                    for tb in tbs:
                        lt = tb - half * 2
                        o4 = 4 * (mcnt["o"] % 2)
                        o2 = 2 * (mcnt["o"] % 2)
                        mcnt["o"] += 1
                        for k in range(NC8):
                            MM(pb[o4 + 0][:, :], wga[sl][:, k, cs], hmod[:, k, tok(tb)], start=(k == 0),
                               stop=(k == NC8 - 1), r=[("wga", sl), ("hmod", k, tb)], w=[("ps", o4 + 0)])
                        for k in range(NC8):
                            MM(pb[o4 + 1][:, :], wgh[sl][:, k, cs], hmod[:, k, tok(tb)], start=(k == 0),
                               stop=(k == NC8 - 1), r=[("wgh", sl), ("hmod", k, tb)], w=[("ps", o4 + 1)])
                        for hh in range(NH):
                            MM(pb[o4 + 2][:, :], wba[sl][:, hh, cs], oA[:, hh, tok(tb)], start=(hh == 0),
                               stop=(hh == NH - 1), r=[("wba", sl)], w=[("ps", o4 + 2)])
                        for hh in range(NH):
                            MM(pb[o4 + 3][:, :], wbh[sl][:, hh, cs], oH[:, hh, tok(tb)], start=(hh == 0),
                               stop=(hh == NH - 1), r=[("wbh", sl)], w=[("ps", o4 + 3)])
                        sa, sb_ = stg[o2], stg[o2 + 1]
                        ACT(sa[:, :], pb[o4 + 0][:, :], AF.Sigmoid, r=[("ps", o4 + 0)], w=[("tmpf", o2)])
                        ACT(sb_[:, :], pb[o4 + 1][:, :], AF.Sigmoid, r=[("ps", o4 + 1)], w=[("tmpf", o2 + 1)])
                        TT("dve", sa[:, :], sa[:, :], pb[o4 + 2][:, :], ALU.mult, r=[("tmpf", o2), ("ps", o4 + 2)],
                           w=[("tmpf", o2)])
                        TT("dve", sb_[:, :], sb_[:, :], pb[o4 + 3][:, :], ALU.mult,
                           r=[("tmpf", o2 + 1), ("ps", o4 + 3)], w=[("tmpf", o2 + 1)])
                        TT("dve", merged[:, dc, lt * TB:(lt + 1) * TB], sa[:, :], sb_[:, :], ALU.add,
                           r=[("tmpf", o2), ("tmpf", o2 + 1)], w=[("mg", dc, lt)])
        for tb in range(NTB):
            for dc in range(NC8):
                pi = 4 + (dc + tb) % 2
                cs = slice(dc * 128, (dc + 1) * 128)
                for k in range(NC8):
                    MM(pb[pi][:, :], wmo_all[:, k, cs], merged[:, k, tok(tb)],
                       start=(k == 0), stop=(k == NC8 - 1), r=[("wmo", dc // 2), ("mg", k, tb)], w=[("ps", pi)])
                STT(x[:, dc, tok(tb)], pb[pi][:, :], gp[:, 8 + dc:8 + dc + 1], x[:, dc, tok(tb)],
                    ALU.mult, ALU.add, r=[("ps", pi), ("gp", 1), ("x", dc, tb)], w=[("x", dc, tb)])
            if tb == NTB - 1:
                BARRIER()
            layer_norm(1, [tb])

        ffn_ln(2, f2_win_d, f2_wout_d)

        for c in range(NC8):
            for tb in range(NTB):
                DMA("sp", yT_d[c * 128:(c + 1) * 128, tok(tb)], x[:, c, tok(tb)], r=[("x", c, tb)])

        P.finalize(nc, sems, dsems)
        with nc.Block() as block:
            @block.tensor
            def _(h):
                P.emit("pe", h)

            @block.scalar
            def _(h):
                P.emit("act", h)

            @block.vector
            def _(h):
                P.emit("dve", h)

            @block.gpsimd
            def _(h):
                P.emit("pool", h)

            @block.sync
            def _(h):
                P.emit("sp", h)
    return nc


_NC_CACHE = {}


def _pc(v, ncols):
    return np.ascontiguousarray(np.asarray(v).reshape(ncols, 128).T)


def _host_constants():
    f32 = np.float32
    t = np.arange(TK)
    row = (t // 64).astype(f32)
    col = (t % 64).astype(f32)
    inv = (np.float32(10000.0) ** (-np.arange(0, 32, 2, dtype=f32) / np.float32(32))).astype(f32)
    ar = row[:, None] * inv
    ac = col[:, None] * inv
    ang = np.concatenate([ar, ar, ac, ac], axis=-1)
    cos = np.cos(ang).astype(f32).T
    sin = np.sin(ang).astype(f32).T
    d = np.arange(64)
    first = (d % 32) < 16
    sign = np.where(first, -1.0, 1.0).astype(f32)
    perm = np.where(first, d + 16, d - 16)
    cosT = np.concatenate([cos, cos], axis=0)
    sinT = np.concatenate([sin * sign[:, None], sin * sign[:, None]], axis=0)
    rot = np.zeros((128, 128), f32)
    for c in range(2):
        for dd in range(64):
            rot[c * 64 + perm[dd], c * 64 + dd] = 1.0
    i = np.arange(128)
    same = (i[:, None] // CH) == (i[None, :] // CH)
    maskf = (same & (i[:, None] <= i[None, :])).astype(f32)
    maskb = (same & (i[:, None] >= i[None, :])).astype(f32)
    nq = T // 256
    mbs = np.zeros((128, NKT * nq), f32)
    mbp = []
    for r in range(2):
        m = np.full((128, NKT * nq), -30000.0, f32)
        for kt in range(NKT):
            k0 = kt * 128 - CTX - r * T
            if 0 <= k0 < T:
                m[:, kt * nq + k0 // 256] = 0.0
        mbp.append(m)
    return dict(cos=[np.ascontiguousarray(cosT[:, 0:T]), np.ascontiguousarray(cosT[:, T:2 * T][:, ::-1])],
                sin=[np.ascontiguousarray(sinT[:, 0:T]), np.ascontiguousarray(sinT[:, T:2 * T][:, ::-1])],
                rot=rot, maskf=maskf, maskb=maskb, mbs=mbs, mbp=mbp, ident=np.eye(128, dtype=f32),
                rowmask=((i[:, None] // CH) == np.arange(4)[None, :]).astype(f32),
                cos1=np.ones((128, T), f32), sin0=np.zeros((128, T), f32))


def kernel(**inputs):
    inp = {k: np.asarray(v) for k, v in inputs.items()}
    if "nc" not in _NC_CACHE:
        _NC_CACHE["nc"] = build_program()
        _NC_CACHE["hc"] = _host_constants()
    nc = _NC_CACHE["nc"]
    hc = _NC_CACHE["hc"]
    f32 = np.float32
    xs, xp = inp["x_sample"], inp["x_prompt"]
    lamv = np.stack([inp["lambda_q1"][0], inp["lambda_k1"][0], inp["lambda_q2"][0], inp["lambda_k2"][0]], 0)
    lamv = np.ascontiguousarray(np.broadcast_to(lamv[None], (128, 4, 64))).astype(f32)
    lbl4 = inp["hgrn_lb_logits"].reshape(2, 2, NH, 128)
    lbl = np.ascontiguousarray(lbl4.transpose(3, 0, 1, 2)).astype(f32)
    lbl_sw = np.ascontiguousarray(lbl4[::-1].transpose(3, 0, 1, 2)).astype(f32)
    wmix = inp["w_mix_in"][0]
    wmix_sw = np.concatenate([wmix[:, :2048], wmix[:, 2560:3072], wmix[:, 2048:2560], wmix[:, 3072:]], axis=1)
    wmix_sw = np.ascontiguousarray(wmix_sw)
    wa = inp["w_ada"][0]
    ba = inp["b_ada"][0]
    wa_half = [np.ascontiguousarray(np.concatenate([wa[:, s * 3072 + r * 1536:s * 3072 + (r + 1) * 1536]
                                                    for s in (1, 2)], axis=1)) for r in range(2)]
    wa0 = np.ascontiguousarray(wa[:, 0:3072])
    ba0 = _pc(ba[0:3072], 24)
    ba_half = [_pc(np.concatenate([ba[s * 3072 + r * 1536:s * 3072 + (r + 1) * 1536] for s in range(3)]), 36)
               for r in range(2)]
    shared = {
        "ffn1_w_in": inp["ffn1_w_in"][0], "ffn1_w_out": inp["ffn1_w_out"][0],
        "ffn2_w_in": inp["ffn2_w_in"][0], "ffn2_w_out": inp["ffn2_w_out"][0],
        "ln_g": _pc(inp["ln_g"][0].reshape(-1), 24), "ln_b": _pc(inp["ln_b"][0].reshape(-1), 24),
        "w_branch_a": inp["w_branch_a"][0], "w_branch_h": inp["w_branch_h"][0],
        "w_mix_out": inp["w_mix_out"][0], "lamv": lamv,
        "subln_g": np.ascontiguousarray(inp["attn_subln_g"][0].reshape(128, 1)),
        "hnorm_g": np.ascontiguousarray(inp["hgrn_norm_g"][0].reshape(128, 1)),
        "ident": hc["ident"], "rotm": hc["rot"], "maskf": hc["maskf"], "maskb": hc["maskb"],
        "rowmask": hc["rowmask"],
    }
    in_maps = []
    for core in range(8):
        m = dict(shared)
        r = core % 2
        m["w_ada"], m["b_ada"] = wa_half[r], ba_half[r]
        m["w_ada0"], m["b_ada0"] = wa0, ba0
        if core < 4:
            b = core // 2
            xh = xs[b, r * T:(r + 1) * T]
            if r == 1:
                xh = xh[::-1]
            m["xT"] = np.ascontiguousarray(xh.T)
            m["cond"] = _pc(inp["c"][b], 8)
            m["kctxT"] = np.ascontiguousarray(inp["cache_k"][b, 0].reshape(CTX, 512).T)
            m["vctx"] = np.ascontiguousarray(inp["cache_v"][b, 0].reshape(CTX, 512))
            st = inp["state_hgrn"][b, 0]
            m["s0"] = np.ascontiguousarray(st if r == 0 else st[::-1])
            m["cosT"], m["sinT"] = hc["cos"][r], hc["sin"][r]
            m["mbias"] = hc["mbs"]
            m["keep"] = np.ones((128, 1), f32)
            xw = np.zeros((128, 3), f32)
            xw[:, 1 - r] = 1.0
            m["xw"] = xw
            m["w_mix_in"] = wmix if r == 0 else wmix_sw
            m["lb_logits"] = lbl if r == 0 else lbl_sw
        else:
            g0 = (core - 4) * 4
            m["xT"] = np.ascontiguousarray(xp[g0:g0 + 4].reshape(T, D).T)
            m["cond"] = _pc(inp["c_ctx"], 8)
            m["kctxT"] = np.zeros((512, CTX), f32)
            m["vctx"] = np.zeros((CTX, 512), f32)
            m["s0"] = np.zeros((2, NH, 128, 128), f32)
            m["cosT"], m["sinT"] = hc["cos1"], hc["sin0"]
            m["mbias"] = hc["mbp"][r]
            m["keep"] = np.zeros((128, 1), f32)
            m["xw"] = np.zeros((128, 3), f32)
            m["w_mix_in"] = wmix
            m["lb_logits"] = lbl
        in_maps.append(m)
    res = run_bass_kernel_spmd(nc, in_maps, core_ids=list(range(8)))
    R = res.results
    _NC_CACHE["last"] = R
    y_sample = np.stack([np.concatenate([R[2 * b]["yT"].T, R[2 * b + 1]["yT"].T[::-1]], axis=0) for b in range(2)], 0)
    y_prompt = np.concatenate([R[c]["yT"].T.reshape(4, SEQ, D) for c in range(4, 8)], axis=0)
    nk = np.concatenate([R[c]["kT_out"].T.reshape(4, SEQ, NH, 2, 64) for c in range(4, 8)], axis=0)
    nv = np.concatenate([R[c]["v_out"].reshape(4, SEQ, NH, 128) for c in range(4, 8)], axis=0)
    ns = np.concatenate([R[c]["s_out"] for c in range(4, 8)], axis=0)
    return (np.ascontiguousarray(y_prompt), np.ascontiguousarray(y_sample),
            np.ascontiguousarray(nk[:, None]), np.ascontiguousarray(nv[:, None]), np.ascontiguousarray(ns[:, None]))
```

```python
import numpy as np
import concourse.bass as bass
import concourse.mybir as mybir
from concourse.bass_utils import run_bass_kernel_spmd

F32 = mybir.dt.float32
BF16 = mybir.dt.bfloat16
AF = mybir.ActivationFunctionType
ALU = mybir.AluOpType

D = 1024
NC8 = 8
T = 1024
TK = 2048
TB = 512
NTB = T // TB
DFF = 2816
NF = DFF // 128
ALPHA = 2.0 ** 0.25
LN_EPS = 1e-5
STAGE = 1


class Op:
    __slots__ = ("idx", "eng", "fn", "dma", "deps", "needs_inc", "sem", "tick", "waits", "cc")

    def __init__(self, idx, eng, fn, dma):
        self.idx = idx
        self.eng = eng
        self.fn = fn
        self.dma = dma
        self.deps = set()
        self.needs_inc = False
        self.sem = None
        self.tick = 0
        self.waits = []
        self.cc = None


class Prog:
    ENGS = ("pe", "act", "dve", "pool", "sp")
    NDSEM = 8

    def __init__(self):
        self.ops = []
        self.last_w = {}
        self.last_r = {}
        self.bar = None
        self.bar_start = 0

    def op(self, eng, fn, r=(), w=(), dma=False, cc=None):
        if cc is not None:
            dma = True
        o = Op(len(self.ops), eng, fn, dma)
        o.cc = cc
        for k in r:
            lw = self.last_w.get(k)
            if lw is not None:
                o.deps.add(lw)
        for k in w:
            lw = self.last_w.get(k)
            if lw is not None:
                o.deps.add(lw)
            for ridx in self.last_r.get(k, {}).values():
                o.deps.add(ridx)
        rk = ("dma", o.idx) if dma else eng
        for k in r:
            self.last_r.setdefault(k, {})[rk] = o.idx
        for k in w:
            self.last_w[k] = o.idx
            self.last_r[k] = {}
        if self.bar is not None:
            o.deps.add(self.bar)
        o.deps.discard(o.idx)
        self.ops.append(o)
        return o

    def barrier(self, fn):
        o = Op(len(self.ops), "dve", fn, False)
        last = {}
        for p in self.ops[self.bar_start:]:
            if p.dma:
                o.deps.add(p.idx)
            else:
                last[p.eng] = p.idx
        for v in last.values():
            o.deps.add(v)
        if self.bar is not None:
            o.deps.add(self.bar)
        self.ops.append(o)
        self.bar = o.idx
        self.bar_start = o.idx
        self.last_w = {}
        self.last_r = {}
        return o

    def finalize(self, nc, sems, dsems):
        ops = self.ops
        for o in ops:
            for d in o.deps:
                do = ops[d]
                if do.dma:
                    continue
                if do.eng == "pe" and o.eng == "pe" and not o.dma:
                    continue
                do.needs_inc = True
        tick = {e: 0 for e in self.ENGS}
        dcount = {e: 0 for e in self.ENGS}
        dtick = {}
        prev_on_sem = {}
        for o in ops:
            if o.cc is not None:
                o.sem = o.cc
                o.tick = 1
            elif o.dma:
                q = o.eng
                si = dcount[q] % self.NDSEM
                dcount[q] += 1
                s = dsems[q][si]
                o.sem = s
                key = (q, si)
                dtick[key] = dtick.get(key, 0) + 16
                o.tick = dtick[key]
                if key in prev_on_sem:
                    o.waits.append((s, prev_on_sem[key]))
                prev_on_sem[key] = o.tick
            elif o.needs_inc:
                tick[o.eng] += 1
                o.sem = sems[o.eng]
                o.tick = tick[o.eng]
        for o in ops:
            for d in sorted(o.deps):
                do = ops[d]
                if do.dma:
                    o.waits.append((do.sem, do.tick))
                elif do.eng == "pe" and o.eng == "pe" and not o.dma:
                    continue
                else:
                    o.waits.append((do.sem, do.tick))
        self.final_dma = {k: v for k, v in dtick.items()}
        self.dsems = dsems

    def emit(self, eng, handle):
        waited = {}
        for o in self.ops:
            if o.eng != eng:
                continue
            for (s, t) in o.waits:
                sid = id(s)
                if waited.get(sid, 0) >= t:
                    continue
                waited[sid] = t
                handle.wait_ge(s, t)
            inst = o.fn(handle)
            if o.cc is not None:
                inst.then_inc(o.sem, 1)
            elif o.dma:
                inst.then_inc(o.sem, 16)
            elif o.needs_inc:
                inst.then_inc(o.sem, 1)
        if eng == "sp":
            for (q, si), t in self.final_dma.items():
                handle.wait_ge(self.dsems[q][si], t)


NH = 4
CTX = 256
NKT = (CTX + TK) // 128
CH = 32
NCH = T // CH
SEQ = 256
BIG = 65536.0
LAM_INIT = 0.2
RMS_EPS = 1e-6
ARENA_BYTES = 128 * 1024
PAIRS = [[0, 1], [2, 3], [4, 5], [6, 7]]
DEBUG = False


class Arena:
    def __init__(self, hb, hf):
        self.hb = hb
        self.hf = hf
        self.off = 0

    def mark(self):
        return self.off

    def reset(self, m=0):
        self.off = m

    def alloc(self, shape, dt):
        n = 1
        for v in shape:
            n *= v
        esz = 4 if dt == F32 else 2
        self.off = (self.off + 63) // 64 * 64
        o = self.off
        self.off += n * esz
        assert self.off <= ARENA_BYTES, ("arena overflow", self.off)
        if dt == F32:
            ap = self.hf[:, o // 4:o // 4 + n]
        else:
            ap = self.hb[:, o // 2:o // 2 + n]
        if len(shape) == 1:
            return ap
        if len(shape) == 2:
            return ap.rearrange("p (a b) -> p a b", b=shape[1])
        if len(shape) == 3:
            return ap.rearrange("p (a b c) -> p a b c", b=shape[1], c=shape[2])
        raise ValueError(shape)


def build_program():
    nc = bass.Bass("TRN2", target_bir_lowering=False)
    P = Prog()

    def din(name, shape, dt=F32):
        return nc.dram_tensor(name, list(shape), dt, kind="ExternalInput").ap()

    def dout(name, shape, dt=F32):
        return nc.dram_tensor(name, list(shape), dt, kind="ExternalOutput").ap()

    xT_d = din("xT", [D, T])
    cond_d = din("cond", [128, NC8])
    w_ada_d = din("w_ada", [D, 3072])
    w_ada0_d = din("w_ada0", [D, 3072])
    b_ada_d = din("b_ada", [128, 36])
    b_ada0_d = din("b_ada0", [128, 24])
    mb_src = nc.dram_tensor("mb_src", [128, 12], F32, kind="Internal").ap()
    mb_dst = nc.dram_tensor("mb_dst", [256, 12], F32, kind="Internal").ap()
    mc_src = nc.dram_tensor("mc_src", [128, 12], F32, kind="Internal").ap()
    mc_dst = nc.dram_tensor("mc_dst", [256, 12], F32, kind="Internal").ap()
    f1_win_d = din("ffn1_w_in", [D, 2 * DFF])
    f1_wout_d = din("ffn1_w_out", [DFF, D])
    f2_win_d = din("ffn2_w_in", [D, 2 * DFF])
    f2_wout_d = din("ffn2_w_out", [DFF, D])
    lng_d = din("ln_g", [128, 24])
    lnb_d = din("ln_b", [128, 24])
    wmix_d = din("w_mix_in", [D, 6144])
    wbra_d = din("w_branch_a", [512, D])
    wbrh_d = din("w_branch_h", [512, D])
    wmo_d = din("w_mix_out", [D, D])
    kctx_d = din("kctxT", [512, CTX])
    vctx_d = din("vctx", [CTX, 512])
    s0_d = din("s0", [2, NH, 128, 128])
    cos_d = din("cosT", [128, T])
    sin_d = din("sinT", [128, T])
    mbias_d = din("mbias", [128, NKT * (T // 256)])
    xw_d = din("xw", [128, 3])
    xk_src = nc.dram_tensor("xk_src", [512, T], BF16, kind="Internal").ap()
    xk_dst = nc.dram_tensor("xk_dst", [1024, T], BF16, kind="Internal").ap()
    xv_src = nc.dram_tensor("xv_src", [T, 512], BF16, kind="Internal").ap()
    xv_dst = nc.dram_tensor("xv_dst", [2 * T, 512], BF16, kind="Internal").ap()
    xs_src = nc.dram_tensor("xs_src", [512, 128], F32, kind="Internal").ap()
    xs_dst = nc.dram_tensor("xs_dst", [1024, 128], F32, kind="Internal").ap()
    keep_d = din("keep", [128, 1])
    lam_d = din("lamv", [128, 4, 64])
    subg_d = din("subln_g", [128, 1])
    hng_d = din("hnorm_g", [128, 1])
    lbl_d = din("lb_logits", [128, 2, 2, NH])
    ident_d = din("ident", [128, 128])
    rm_d = din("rotm", [128, 128])
    mf_d = din("maskf", [128, 128])
    mb_d = din("maskb", [128, 128])
    rowm_d = din("rowmask", [128, 4])

    yT_d = dout("yT", [D, T])
    kout_d = dout("kT_out", [512, T])
    vout_d = dout("v_out", [T, 512])
    sout_d = dout("s_out", [T // SEQ, 2, NH, 128, 128])
    if DEBUG:
        dbg_oa_d = dout("dbg_oa", [512, T])
        dbg_oh_d = dout("dbg_oh", [512, T])

    from contextlib import ExitStack
    es = ExitStack()

    def sb(name, shape, dt):
        return es.enter_context(nc.sbuf_tensor(name, list(shape), dt))

    def ps(name, shape, dt=F32):
        return es.enter_context(nc.psum_tensor(name, list(shape), dt))

    with es:
        x = sb("x", [128, NC8, T], F32)
        hmod = sb("hmod", [128, NC8, T], BF16)
        arena_b = sb("arena", [128, ARENA_BYTES // 2], BF16)
        arena_f = arena_b.bitcast(F32)
        AR = Arena(arena_b, arena_f)
        cond_sb = sb("cond_sb", [128, NC8], F32)
        scond = sb("scond", [128, NC8], BF16)
        bada = sb("bada", [128, 36], F32)
        modh = sb("modh", [128, 36], F32)
        bada0 = sb("bada0", [128, 24], F32)
        mod = sb("mod", [128, 72], F32)
        lng = sb("lng", [128, 24], F32)
        lnb = sb("lnb", [128, 24], F32)
        sc1 = sb("sc1", [128, 24], F32)
        gp = sb("gp", [128, 24], F32)
        lnA = sb("lnA", [128, 24], F32)
        lnB = sb("lnB", [128, 24], F32)
        ones_bf = sb("ones_bf", [128, 128], BF16)
        tmpf = [sb(f"tmpf{i}", [128, TB], F32) for i in range(9)]
        sg = tmpf[0:2]
        xn = tmpf[2:4]
        mean, msq, var, rstd, nmr = tmpf[4:9]
        rb = [sb(f"rb{i}", [128, TB], BF16) for i in range(2)]
        rsq = [sb(f"rsq{i}", [128, TB], BF16) for i in range(2)]
        ident = sb("ident_sb", [128, 128], BF16)
        rotm = sb("rotm_sb", [128, 128], F32)
        maskf = sb("maskf_sb", [128, 128], F32)
        maskb = sb("maskb_sb", [128, 128], F32)
        keep = sb("keep_sb", [128, 1], F32)
        xw = sb("xw_sb", [128, 3], F32)
        mbias = sb("mbias_sb", [128, NKT * (T // 256)], F32)
        rowmask = sb("rowmask_sb", [128, 4], F32)
        lamv = sb("lamv_sb", [128, 4, 64], F32)
        lamt = sb("lamt", [128, 2, 64], F32)
        lams = sb("lams", [128, 4], F32)
        neglam = sb("neglam", [128, 1], F32)
        subg = sb("subg", [128, 1], F32)
        hng = sb("hng", [128, 1], F32)
        lbl = sb("lbl", [128, 2, 2, NH], F32)
        lbv = sb("lbv", [128, 2, NH], F32)
        olb = sb("olb", [128, 2, NH], F32)
        dbar = sb("dbar", [128, 1], F32)
        pb = [ps(f"pb{i}", [128, 512]) for i in range(8)]
        pbb = [p_.bitcast(BF16) for p_ in pb]
        wada = [hmod[:, 4 * i:4 * i + 4, :].rearrange("p a (b n) -> p (a b) n", n=512) for i in range(2)]

        sems = {e: es.enter_context(nc.semaphore("s_" + e)) for e in ("pe", "act", "dve", "pool")}
        ccsem = [es.enter_context(nc.semaphore(f"cc{i}")) for i in range(5)]
        dsems = {q: [es.enter_context(nc.semaphore(f"d_{q}{i}")) for i in range(Prog.NDSEM)]
                 for q in ("sp", "pool", "act")}

        def DMA(q, out, in_, r=(), w=()):
            P.op(q, lambda h, out=out, in_=in_: h.dma_start(out=out, in_=in_), r=r, w=w, dma=True)

        def MM(out, lhsT, rhs, start, stop, r=(), w=()):
            P.op("pe", lambda h, out=out, lhsT=lhsT, rhs=rhs, start=start, stop=stop:
                 h.matmul(out, lhsT=lhsT, rhs=rhs, start=start, stop=stop), r=r, w=w)

        def TR(out, in_, r=(), w=()):
            P.op("pe", lambda h, out=out, in_=in_: h.transpose(out, in_, ident[:, :]), r=list(r) + ["ident"], w=w)

        def ACT(out, in_, func, r=(), w=(), bias=None, scale=None):
            def fn(h, out=out, in_=in_, func=func, bias=bias, scale=scale):
                kw = {}
                if bias is not None:
                    kw["bias"] = bias
                if scale is not None:
                    kw["scale"] = scale
                return h.activation(out=out, in_=in_, func=func, **kw)
            P.op("act", fn, r=r, w=w)

        def TS(eng, out, in0, s1, s2, op0, op1, r=(), w=()):
            P.op(eng, lambda h, out=out, in0=in0, s1=s1, s2=s2, op0=op0, op1=op1:
                 h.tensor_scalar(out=out, in0=in0, scalar1=s1, scalar2=s2, op0=op0, op1=op1), r=r, w=w)

        def TT(eng, out, in0, in1, op, r=(), w=()):
            P.op(eng, lambda h, out=out, in0=in0, in1=in1, op=op:
                 h.tensor_tensor(out=out, in0=in0, in1=in1, op=op), r=r, w=w)

        def STT(out, in0, scalar, in1, op0, op1, r=(), w=()):
            P.op("dve", lambda h, out=out, in0=in0, scalar=scalar, in1=in1, op0=op0, op1=op1:
                 h.scalar_tensor_tensor(out=out, in0=in0, scalar=scalar, in1=in1, op0=op0, op1=op1), r=r, w=w)

        def RECIP(out, in_, r=(), w=()):
            P.op("dve", lambda h, out=out, in_=in_: h.reciprocal(out=out, in_=in_), r=r, w=w)

        def BARRIER():
            P.barrier(lambda h: h.memset(dbar[:, :], 0.0))

        def tok(tb):
            return slice(tb * TB, (tb + 1) * TB)

        def wview(w_d, c0, n):
            return w_d[:, c0:c0 + n].rearrange("(k p) n -> p k n", p=128)

        P.op("dve", lambda h: h.memset(ones_bf[:, :], 1.0), w=["ones"])
        DMA("sp", cond_sb[:, :], cond_d[:, :], w=["cond"])
        DMA("sp", bada[:, :], b_ada_d[:, :], w=["bada"])
        DMA("sp", bada0[:, :], b_ada0_d[:, :], w=["bada0"])
        DMA("sp", lng[:, :], lng_d[:, :], w=["lng"])
        DMA("sp", lnb[:, :], lnb_d[:, :], w=["lnb"])
        for tb in range(NTB):
            for c in range(NC8):
                DMA("sp", x[:, c, tok(tb)], xT_d[c * 128:(c + 1) * 128, tok(tb)], w=[("x", c, tb)])
        DMA("sp", rotm[:, :], rm_d[:, :], w=["rotm"])
        DMA("sp", maskf[:, :], mf_d[:, :], w=["maskf"])
        DMA("sp", maskb[:, :], mb_d[:, :], w=["maskb"])
        DMA("sp", keep[:, :], keep_d[:, :], w=["keep"])
        DMA("sp", xw[:, :], xw_d[:, :], w=["xw"])
        DMA("sp", mbias[:, :], mbias_d[:, :], w=["mbias"])
        DMA("sp", rowmask[:, :], rowm_d[:, :], w=["rowmask"])
        DMA("sp", lamv[:, :, :], lam_d[:, :, :], w=["lamv"])
        DMA("sp", subg[:, :], subg_d[:, :], w=["subg"])
        DMA("sp", hng[:, :], hng_d[:, :], w=["hng"])
        DMA("sp", lbl[:, :, :, :], lbl_d[:, :, :, :], w=["lbl"])
        DMA("pool", ident[:, :], ident_d[:, :], w=["ident"])
        ACT(scond[:, :], cond_sb[:, :], AF.Silu, r=["cond"], w=["scond"])

        TT("dve", lamt[:, 0, :], lamv[:, 0, :], lamv[:, 1, :], ALU.mult, r=["lamv"], w=["lamt"])
        TT("dve", lamt[:, 1, :], lamv[:, 2, :], lamv[:, 3, :], ALU.mult, r=["lamv", "lamt"], w=["lamt"])
        for i in range(2):
            P.op("dve", lambda h, i=i: h.tensor_reduce(out=lams[:, i:i + 1], in_=lamt[:, i, :],
                                                       axis=mybir.AxisListType.X, op=ALU.add),
                 r=["lamt"], w=[("lams", i)])
        ACT(lams[:, 2:4], lams[:, 0:2], AF.Exp, r=[("lams", 0), ("lams", 1)], w=[("lams", 2)])
        TT("dve", neglam[:, :], lams[:, 3:4], lams[:, 2:3], ALU.subtract, r=[("lams", 2)], w=["neglam"])
        TS("dve", neglam[:, :], neglam[:, :], -LAM_INIT, None, ALU.add, ALU.bypass, r=["neglam"], w=["neglam"])
        TS("dve", subg[:, :], subg[:, :], 1.0 - LAM_INIT, None, ALU.mult, ALU.bypass, r=["subg"], w=["subg"])
        TT("dve", lbv[:, :, :], lbl[:, :, 0, :], lbl[:, :, 1, :], ALU.subtract, r=["lbl"], w=["lbv"])
        ACT(lbv[:, :, :], lbv[:, :, :], AF.Sigmoid, r=["lbv"], w=["lbv"])
        TS("dve", olb[:, :, :], lbv[:, :, :], -1.0, 1.0, ALU.mult, ALU.add, r=["lbv"], w=["olb"])

        state = {"win": 0, "wout": 0, "gu": 0, "dn": 0, "ln": 0}
        WIN_COLS = 256
        WOUT_COLS = 256
        def adaln_blocks(blks, stage, bank, col0, src=None):
            for blk in blks:
                slot = blk % 2
                if src is None:
                    DMA("pool", stage[slot], wview(w_ada_d, (blk - 3) * 512, 512), w=[("wada", slot)])
                else:
                    DMA("pool", stage[slot], wview(src, blk * 512, 512), w=[("wada", slot)])
                for jj in range(4):
                    j = blk * 4 + jj - col0
                    for k in range(NC8):
                        MM(pb[bank][:, j:j + 1], stage[slot][:, k, jj * 128:(jj + 1) * 128], scond[:, k:k + 1],
                           start=(k == 0), stop=(k == NC8 - 1),
                           r=[("wada", slot), "scond"], w=[("ps", bank)])

        def adaln_derive(s):
            base = s * 24
            gsc = (0.5 if s != 1 else 1.0) / ALPHA
            TS("dve", sc1[:, s * 8:(s + 1) * 8], mod[:, base + 8:base + 16], 1.0, None, ALU.add, ALU.bypass,
               r=[("mod", s)], w=[("sc1", s)])
            TS("dve", gp[:, s * 8:(s + 1) * 8], mod[:, base + 16:base + 24], gsc, None, ALU.mult, ALU.bypass,
               r=[("mod", s)], w=[("gp", s)])

        adaln_blocks(range(6), wada, 0, 0, src=w_ada0_d)
        TT("dve", mod[:, 0:24], pb[0][:, 0:24], bada0[:, :], ALU.add, r=[("ps", 0), "bada0"], w=[("mod", 0)])
        adaln_derive(0)
        pre_win = {}

        def prefetch_win(blk, win_g, win_u):
            slot = state["win"] % 2
            state["win"] += 1
            c0 = blk * WIN_COLS
            DMA("pool", win_g[slot], wview(f1_win_d, c0, WIN_COLS), w=[("wing", slot)])
            DMA("pool", win_u[slot], wview(f1_win_d, DFF + c0, WIN_COLS), w=[("winu", slot)])
            pre_win[blk] = slot

        PRE = {"fn": prefetch_win}
        def adaln_gather_a():
            pass

        rest_state = {}

        def ln_ab(s):
            TT("dve", lnA[:, s * 8:(s + 1) * 8], lng[:, s * 8:(s + 1) * 8], sc1[:, (s + 1) * 8:(s + 2) * 8],
               ALU.mult, r=["lng", ("sc1", s + 1)], w=[("lnA", s)])
            TT("dve", lnB[:, s * 8:(s + 1) * 8], lnb[:, s * 8:(s + 1) * 8], sc1[:, (s + 1) * 8:(s + 2) * 8],
               ALU.mult, r=["lnb", ("sc1", s + 1)], w=[("lnB", s)])
            TT("dve", lnB[:, s * 8:(s + 1) * 8], lnB[:, s * 8:(s + 1) * 8],
               mod[:, (s + 1) * 24:(s + 1) * 24 + 8], ALU.add,
               r=[("lnB", s), ("mod", s + 1)], w=[("lnB", s)])

        def adaln_gather(s, src_d, dst_d, sem):
            DMA("sp", src_d[:, :], modh[:, s * 12:(s + 1) * 12], r=[("modh", s)], w=[("mbs", s)])
            P.op("pool", lambda hh: hh.collective_compute("AllGather", ALU.bypass, replica_groups=PAIRS,
                                                          ins=[src_d], outs=[dst_d]),
                 r=[("mbs", s)], w=[("mbd", s)], cc=sem)
            for rk in range(2):
                DMA("sp", mod[:, s * 24 + rk * 12:s * 24 + (rk + 1) * 12], dst_d[rk * 128:(rk + 1) * 128, :],
                    r=[("mbd", s)], w=[("mod", s)])
            adaln_derive(s)
            ln_ab(s - 1)

        def adaln_rest_group(g):
            if g == 0:
                rest_state["stage"] = [AR.alloc([NC8, 512], BF16) for _ in range(2)]
                adaln_blocks([3, 4], rest_state["stage"], 7, 12)
            elif g == 1:
                adaln_blocks([5], rest_state["stage"], 7, 12)
            else:
                TT("dve", modh[:, 12:24], pb[7][:, 0:12], bada[:, 12:24], ALU.add, r=[("ps", 7), "bada"],
                   w=[("modh", 1)])
                adaln_gather(1, mb_src, mb_dst, ccsem[4])


        def layer_norm(s, tbs):
            for tb in tbs:
                for c in range(NC8):
                    bi = state["ln"] % 2
                    state["ln"] += 1
                    ACT(rb[bi][:, :], x[:, c, tok(tb)], AF.Copy, r=[("x", c, tb)], w=[("rb", bi)])
                    ACT(rsq[bi][:, :], x[:, c, tok(tb)], AF.Square, r=[("x", c, tb)], w=[("rsq", bi)])
                    MM(pb[6][:, :], ones_bf[:, :], rb[bi][:, :], start=(c == 0), stop=(c == NC8 - 1),
                       r=["ones", ("rb", bi)], w=[("ps", 6)])
                    MM(pb[7][:, :], ones_bf[:, :], rsq[bi][:, :], start=(c == 0), stop=(c == NC8 - 1),
                       r=["ones", ("rsq", bi)], w=[("ps", 7)])
                TS("dve", mean[:, :], pb[6][:, :], 1.0 / D, None, ALU.mult, ALU.bypass, r=[("ps", 6)], w=[("tmpf", 4)])
                TT("dve", msq[:, :], mean[:, :], mean[:, :], ALU.mult, r=[("tmpf", 4)], w=[("tmpf", 5)])
                STT(var[:, :], pb[7][:, :], 1.0 / D, msq[:, :], ALU.mult, ALU.subtract,
                    r=[("ps", 7), ("tmpf", 5)], w=[("tmpf", 6)])
                ACT(var[:, :], var[:, :], AF.Sqrt, r=[("tmpf", 6)], w=[("tmpf", 6)], bias=LN_EPS / (ALPHA * ALPHA))
                RECIP(rstd[:, :], var[:, :], r=[("tmpf", 6)], w=[("tmpf", 7)])
                STT(nmr[:, :], mean[:, :], -1.0, rstd[:, :], ALU.mult, ALU.mult, r=[("tmpf", 4), ("tmpf", 7)], w=[("tmpf", 8)])
                for c in range(NC8):
                    xi = c % 2
                    TT("dve", xn[xi][:, :], x[:, c, tok(tb)], rstd[:, :], ALU.mult,
                       r=[("x", c, tb), ("tmpf", 7)], w=[("tmpf", 2 + xi)])
                    TT("dve", xn[xi][:, :], xn[xi][:, :], nmr[:, :], ALU.add, r=[("tmpf", 2 + xi), ("tmpf", 8)], w=[("tmpf", 2 + xi)])
                    ACT(x[:, c, tok(tb)], xn[xi][:, :], AF.Identity, r=[("tmpf", 2 + xi), "lng", "lnb"], w=[("x", c, tb)],
                        scale=lng[:, s * 8 + c:s * 8 + c + 1], bias=lnb[:, s * 8 + c:s * 8 + c + 1])
                    if s < 2:
                        ACT(hmod[:, c, tok(tb)], xn[xi][:, :], AF.Identity,
                            r=[("tmpf", 2 + xi), ("lnA", s), ("lnB", s)], w=[("hmod", c, tb)],
                            scale=lnA[:, s * 8 + c:s * 8 + c + 1], bias=lnB[:, s * 8 + c:s * 8 + c + 1])

        def ffn_ln(s, win_d, wout_d, hook=None, pre=None, early_barrier=False):
            AR.reset(0)
            hbuf = AR.alloc([NF, 1024], BF16)
            win_g = [AR.alloc([NC8, WIN_COLS], BF16) for _ in range(2)]
            win_u = [AR.alloc([NC8, WIN_COLS], BF16) for _ in range(2)]
            wout = AR.alloc([NF, D], BF16)
            tbs = list(range(NTB))
            if pre is not None:
                pre(win_g, win_u)
            for blk in range(DFF // WIN_COLS):
                c0 = blk * WIN_COLS
                if blk in pre_win:
                    slot = pre_win.pop(blk)
                else:
                    slot = state["win"] % 2
                    state["win"] += 1
                    DMA("pool", win_g[slot], wview(win_d, c0, WIN_COLS), w=[("wing", slot)])
                    DMA("pool", win_u[slot], wview(win_d, DFF + c0, WIN_COLS), w=[("winu", slot)])
                wo0 = 3 if s == 0 else 2
                if wo0 <= blk < wo0 + 8:
                    p8 = blk - wo0
                    q4, jh = p8 // 2, p8 % 2
                    j0 = jh * (NF // 2)
                    DMA("pool", wout[:, j0:j0 + NF // 2, q4 * 256:(q4 + 1) * 256],
                        wout_d[j0 * 128:(j0 + NF // 2) * 128, q4 * 256:(q4 + 1) * 256].rearrange(
                            "(j p) n -> p j n", p=128), w=[("wout", q4, jh)])
                for jj in range(WIN_COLS // 128):
                    j = blk * (WIN_COLS // 128) + jj
                    for tb in tbs:
                        gi = state["gu"] % 2
                        state["gu"] += 1
                        pg, pu = pb[gi], pb[2 + gi]
                        for k in range(NC8):
                            MM(pg[:, :], win_g[slot][:, k, jj * 128:(jj + 1) * 128], hmod[:, k, tok(tb)],
                               start=(k == 0), stop=(k == NC8 - 1),
                               r=[("wing", slot), ("hmod", k, tb)], w=[("ps", gi)])
                        for k in range(NC8):
                            MM(pu[:, :], win_u[slot][:, k, jj * 128:(jj + 1) * 128], hmod[:, k, tok(tb)],
                               start=(k == 0), stop=(k == NC8 - 1),
                               r=[("winu", slot), ("hmod", k, tb)], w=[("ps", 2 + gi)])
                        ACT(sg[gi][:, :], pg[:, :], AF.Silu, r=[("ps", gi)], w=[("tmpf", gi)])
                        TT("dve", hbuf[:, j, tok(tb)], sg[gi][:, :], pu[:, :], ALU.mult,
                           r=[("tmpf", gi), ("ps", 2 + gi)], w=[("h", j, tb)])
            for tb in tbs:
                for c in range(NC8):
                    if hook is not None and tb == 0 and c in (2, 5):
                        hook((c - 2) // 3)
                    di = 4 + state["dn"] % 2
                    state["dn"] += 1
                    for j in range(NF):
                        MM(pb[di][:, :], wout[:, j, c * 128:(c + 1) * 128], hbuf[:, j, tok(tb)],
                           start=(j == 0), stop=(j == NF - 1),
                           r=[("wout", c // 2, 0), ("wout", c // 2, 1), ("h", j, tb)], w=[("ps", di)])
                    STT(x[:, c, tok(tb)], pb[di][:, :], gp[:, s * 8 + c:s * 8 + c + 1], x[:, c, tok(tb)],
                        ALU.mult, ALU.add, r=[("ps", di), ("gp", s), ("x", c, tb)], w=[("x", c, tb)])
                if hook is not None and tb == 0:
                    hook(2)
                if early_barrier and tb == tbs[-1]:
                    BARRIER()
                layer_norm(s, [tb])

        def ffn1_pre(win_g, win_u):
            prefetch_win(0, win_g, win_u)
            prefetch_win(1, win_g, win_u)
            adaln_gather_a()
            for tb in range(NTB):
                for c in range(NC8):
                    TS("dve", hmod[:, c, tok(tb)], x[:, c, tok(tb)], sc1[:, c:c + 1], mod[:, c:c + 1],
                       ALU.mult, ALU.add, r=[("x", c, tb), ("sc1", 0), ("mod", 0)], w=[("hmod", c, tb)])

        ffn_ln(0, f1_win_d, f1_wout_d, hook=adaln_rest_group, pre=ffn1_pre, early_barrier=True)

        AR.reset(0)
        oA = AR.alloc([NH, T], BF16)
        oH = AR.alloc([NH, T], BF16)
        m_mix = AR.mark()
        stg = tmpf

        cosT = AR.alloc([T], F32)
        sinT = AR.alloc([T], F32)
        wk = [AR.alloc([NC8, 128], BF16) for _ in range(2)]
        wv_all = AR.alloc([NC8, 512], BF16)
        wq = [AR.alloc([NC8, 128], BF16) for _ in range(2)]
        kown = [AR.alloc([T], BF16) for _ in range(2)]
        vown_all = AR.alloc([T // 128, 512], BF16)
        qPs = [AR.alloc([T // 256, 512], BF16) for _ in range(NH)]
        kT = [AR.alloc([CTX + TK], BF16) for _ in range(2)]
        Vt = [AR.alloc([NKT, 128], BF16) for _ in range(2)]
        Eb = [AR.alloc([TB], BF16) for _ in range(5)]
        sqb2 = [AR.alloc([256], BF16) for _ in range(2)]
        rms_eps_t = AR.alloc([1], F32)
        P.op("dve", lambda h: h.memset(rms_eps_t, RMS_EPS), w=["rmseps"])
        stage2 = [AR.alloc([NC8, 512], BF16) for _ in range(3)]
        DMA("sp", cosT, cos_d[:, :], w=["cos"])
        DMA("sp", sinT, sin_d[:, :], w=["sin"])
        for h_ in range(NH):
            P.op("dve", lambda h, h_=h_: h.memset(qPs[h_], 0.0), w=[("qT", h_, tb) for tb in range(NTB)])

        def rope(raw, rot_ps, key_raw, key_ps, tb, outs, okeys, wkeys, split=False):
            t1, t2 = stg[2], stg[3]
            TT("dve", t1[:, :], raw, cosT[:, tok(tb)], ALU.mult, r=[key_raw, "cos"], w=[("tmpf", 2)])
            TT("dve", t2[:, :], rot_ps, sinT[:, tok(tb)], ALU.mult, r=[key_ps, "sin"], w=[("tmpf", 3)])
            for (o_ap, psl) in outs:
                a1, a2 = t1[psl, :], t2[psl, :]
                if split:
                    a1 = a1.rearrange("p (a b) -> p a b", b=256)
                    a2 = a2.rearrange("p (a b) -> p a b", b=256)
                TT("dve", o_ap, a1, a2, ALU.add, r=[("tmpf", 2), ("tmpf", 3)] + okeys, w=wkeys)

        DMA("pool", wv_all, wview(wmix_d, 1024, 512), w=["wv_all"])
        for tt in range(T // 128):
            slot = 4 + tt % 2
            for k in range(NC8):
                MM(pb[slot][:, :], hmod[:, k, tt * 128:(tt + 1) * 128], wv_all[:, k, :],
                   start=(k == 0), stop=(k == NC8 - 1),
                   r=["wv_all", ("hmod", k, tt // 4)], w=[("ps", slot)])
            vst = stg[4 + tt % 2]
            ACT(vst[:, :], pb[slot][:, :], AF.Copy, r=[("ps", slot)], w=[("tmpf", 4 + tt % 2)])
            DMA("sp", vout_d[tt * 128:(tt + 1) * 128, :], vst[:, :], r=[("tmpf", 4 + tt % 2)])
            TS("dve", vown_all[:, tt, :], vst[:, :], 1.0, None, ALU.mult, ALU.bypass,
               r=[("tmpf", 4 + tt % 2)], w=["vown_all"])
        DMA("sp", xv_src.rearrange("(t p) n -> p t n", p=128), vown_all, r=["vown_all"], w=["xv_src"])
        for h in range(NH):
            sl = h % 2
            DMA("pool", wk[sl], wview(wmix_d, 512 + h * 128, 128), w=[("wk", sl)])
            if h == NH - 1:
                for i_ in range(3):
                    DMA("pool", stage2[i_], wview(w_ada_d, (3 + i_) * 512, 512), w=[("wada2", i_)])
            for tb in range(NTB):
                pp = pb[tb % 2]
                for k in range(NC8):
                    MM(pp[:, :], wk[sl][:, k, :], hmod[:, k, tok(tb)], start=(k == 0), stop=(k == NC8 - 1),
                       r=[("wk", sl), ("hmod", k, tb)], w=[("ps", tb % 2)])
            for tb in range(NTB):
                pp = pb[tb % 2]
                raw = stg[tb % 2]
                ACT(raw[:, :], pp[:, :], AF.Copy, r=[("ps", tb % 2)], w=[("tmpf", tb % 2)])
                DMA("sp", kout_d[h * 128:(h + 1) * 128, tok(tb)], raw[:, :], r=[("tmpf", tb % 2)])
                MM(pb[2 + tb % 2][:, :], rotm[:, :], raw[:, :], start=True, stop=True,
                   r=["rotm", ("tmpf", tb % 2)], w=[("ps", 2 + tb % 2)])
            for tb in range(NTB):
                raw = stg[tb % 2]
                rope(raw[:, :], pb[2 + tb % 2][:, :], ("tmpf", tb % 2), ("ps", 2 + tb % 2), tb,
                     [(kown[sl][:, tok(tb)], slice(0, 128))], [], [("kown", sl)])
            DMA("sp", xk_src[h * 128:(h + 1) * 128, :], kown[sl], r=[("kown", sl)], w=["xk_src"])
        def a2_loads(h):
            sl = h % 2
            DMA("pool", kT[sl][:, 0:CTX], kctx_d[h * 128:(h + 1) * 128, :], w=[("kT", sl)])
            DMA("pool", Vt[sl][:, 0:CTX // 128, :],
                vctx_d[:, h * 128:(h + 1) * 128].rearrange("(t p) n -> p t n", p=128), w=[("Vt", sl)])

        DMA("pool", wq[0], wview(wmix_d, 0, 128), w=[("wq", 0)])
        DMA("pool", wq[1], wview(wmix_d, 128, 128), w=[("wq", 1)])
        a2_loads(0)
        P.op("pool", lambda hh: hh.collective_compute("AllGather", ALU.bypass, replica_groups=PAIRS,
                                                      ins=[xk_src], outs=[xk_dst]),
             r=["xk_src"], w=["xk_dst"], cc=ccsem[0])
        P.op("pool", lambda hh: hh.collective_compute("AllGather", ALU.bypass, replica_groups=PAIRS,
                                                      ins=[xv_src], outs=[xv_dst]),
             r=["xv_src"], w=["xv_dst"], cc=ccsem[1])

        def adaln_b2():
            for i_ in range(3):
                for jj in range(4):
                    j = i_ * 4 + jj
                    for k in range(NC8):
                        MM(pb[7][:, j:j + 1], stage2[i_][:, k, jj * 128:(jj + 1) * 128], scond[:, k:k + 1],
                           start=(k == 0), stop=(k == NC8 - 1), r=[("wada2", i_), "scond"], w=[("ps", 7)])
            TT("dve", modh[:, 24:36], pb[7][:, 0:12], bada[:, 24:36], ALU.add, r=[("ps", 7), "bada"],
               w=[("modh", 2)])
            adaln_gather(2, mc_src, mc_dst, ccsem[3])

        for h in range(NH):
            sl = h % 2
            if h >= 2:
                DMA("pool", wq[sl], wview(wmix_d, h * 128, 128), w=[("wq", sl)])
            for tb in range(NTB):
                pp = pb[tb % 2]
                for k in range(NC8):
                    MM(pp[:, :], wq[sl][:, k, :], hmod[:, k, tok(tb)], start=(k == 0), stop=(k == NC8 - 1),
                       r=[("wq", sl), ("hmod", k, tb)], w=[("ps", tb % 2)])
                raw = stg[tb % 2]
                ACT(raw[:, :], pp[:, :], AF.Copy, r=[("ps", tb % 2)], w=[("tmpf", tb % 2)])
                MM(pb[2 + tb % 2][:, :], rotm[:, :], raw[:, :], start=True, stop=True,
                   r=["rotm", ("tmpf", tb % 2)], w=[("ps", 2 + tb % 2)])
                rope(raw[:, :], pb[2 + tb % 2][:, :], ("tmpf", tb % 2), ("ps", 2 + tb % 2), tb,
                     [(qPs[h][0:64, 2 * tb:2 * tb + 2, 0:256], slice(0, 64)),
                      (qPs[h][64:128, 2 * tb:2 * tb + 2, 256:512], slice(64, 128))],
                     [("qT", h, tb)], [("qT", h, tb)], split=True)

        cnt = {"e": 0, "s": 0, "it": 0}
        pend = []
        for h in range(NH):
            sl = h % 2
            if h > 0:
                a2_loads(h)
            for rk in range(2):
                DMA("sp", kT[sl][:, CTX + rk * T:CTX + (rk + 1) * T], xk_dst[rk * 512 + h * 128:rk * 512 + (h + 1) * 128, :],
                    r=["xk_dst"], w=[("kT", sl)])
                DMA("sp", Vt[sl][:, CTX // 128 + rk * (T // 128):CTX // 128 + (rk + 1) * (T // 128), :],
                    xv_dst[rk * T:(rk + 1) * T, h * 128:(h + 1) * 128].rearrange("(t p) n -> p t n", p=128),
                    r=["xv_dst"], w=[("Vt", sl)])
            if h == 0:
                adaln_b2()
            NQ = T // 256
            its = [(qi, kt) for qi in range(NQ) for kt in range(NKT)]
            slots = {}

            def front(i, its=its, slots=slots, sl=sl, h=h):
                qi, kt = its[i]
                si = (0, 1, 2, 7)[cnt["s"] % 4]
                cnt["s"] += 1
                ei = cnt["e"] % 5
                cnt["e"] += 1
                slots[i] = ei
                MM(pb[si][:, :], kT[sl][:, kt * 128:(kt + 1) * 128], qPs[h][:, qi, :], start=True, stop=True,
                   r=[("kT", sl), ("qT", h, qi // 2)], w=[("ps", si)])
                mcol = kt * NQ + qi
                ACT(Eb[ei][:, :], pb[si][:, :], AF.Exp, r=[("ps", si), "mbias"], w=[("E", ei)], scale=0.125,
                    bias=mbias[:, mcol:mcol + 1])

            def back(i, its=its, slots=slots, sl=sl):
                qi, kt = its[i]
                ei = slots[i]
                MM(pb[3 + qi % 2][:, :], Vt[sl][:, kt, :], Eb[ei][:, :], start=(kt == 0), stop=(kt == NKT - 1),
                   r=[("Vt", sl), ("E", ei)], w=[("ps", 3 + qi % 2)])
                MM(pb[5 + qi % 2][:, :], ones_bf[:, :], Eb[ei][:, :], start=(kt == 0), stop=(kt == NKT - 1),
                   r=["ones", ("E", ei)], w=[("ps", 5 + qi % 2)])

            def combine_stages(qi, h=h):
                par = qi % 2
                rz, tt_ = stg[4 + par], stg[6 + par]
                qs_ = slice(qi * 256, (qi + 1) * 256)

                def s1():
                    RECIP(rz[:, :], pb[5 + par][:, :], r=[("ps", 5 + par)], w=[("tmpf", 4 + par)])
                    TT("dve", tt_[:, :], pb[3 + par][:, :], rz[:, :], ALU.mult,
                       r=[("ps", 3 + par), ("tmpf", 4 + par)], w=[("tmpf", 6 + par)])
                    STT(tt_[:, 0:256], tt_[:, 256:512], neglam[:, 0:1], tt_[:, 0:256], ALU.mult, ALU.add,
                        r=[("tmpf", 6 + par), "neglam"], w=[("tmpf", 6 + par)])
                    TT("dve", sqb2[par][:, 0:256], tt_[:, 0:256], tt_[:, 0:256], ALU.mult,
                       r=[("tmpf", 6 + par)], w=[("sqb", par)])

                def s2():
                    MM(pb[3 + par][:, 0:256], ones_bf[:, :], sqb2[par][:, 0:256], start=True, stop=True,
                       r=["ones", ("sqb", par)], w=[("ps", 3 + par)])

                def s3():
                    ACT(rz[:, 0:256], pb[3 + par][:, 0:256], AF.Ln, r=[("ps", 3 + par)],
                        w=[("tmpf", 4 + par)], scale=1.0 / 128, bias=rms_eps_t[:, 0:1])
                    ACT(rz[:, 0:256], rz[:, 0:256], AF.Exp, r=[("tmpf", 4 + par)], w=[("tmpf", 4 + par)], scale=-0.5)
                    TT("dve", tt_[:, 0:256], tt_[:, 0:256], rz[:, 0:256], ALU.mult,
                       r=[("tmpf", 6 + par), ("tmpf", 4 + par)], w=[("tmpf", 6 + par)])
                    TS("dve", oA[:, h, qs_], tt_[:, 0:256], subg[:, 0:1], None, ALU.mult, ALU.bypass,
                       r=[("tmpf", 6 + par), "subg"], w=[("oA", h, qi // 2)])
                return [(0, s1), (5, s2), (9, s3)]

            PF = 3
            for i in range(min(PF, len(its))):
                front(i)
            for i in range(len(its)):
                if i + PF < len(its):
                    front(i + PF)
                back(i)
                if its[i][1] == NKT - 1:
                    for (dl, fn_) in combine_stages(its[i][0]):
                        pend.append([cnt["it"] + dl, fn_])
                cnt["it"] += 1
                for pe_ in [p_ for p_ in pend if p_[0] <= cnt["it"]]:
                    pe_[1]()
                    pend.remove(pe_)
        for pe_ in sorted(pend, key=lambda p_: p_[0]):
            pe_[1]()
        if DEBUG:
            for h in range(NH):
                DMA("pool", dbg_oa_d[h * 128:(h + 1) * 128, :], oA[:, h, :],
                    r=[("oA", h, qb) for qb in range(NTB)])
        BARRIER()

        AR.reset(m_mix)
        NB128 = T // 128
        Qt = [[AR.alloc([T], BF16) for _ in range(2)] for _ in range(NH)]
        Khm = [[AR.alloc([NB128, 128], BF16) for _ in range(2)] for _ in range(NH)]
        decs = [[AR.alloc([NCH], F32) for _ in range(2)] for _ in range(NH)]
        Vh = [AR.alloc([NB128, 128], BF16) for _ in range(NH)]
        gsil = [AR.alloc([T], BF16) for _ in range(NH)]
        opart = [AR.alloc([T], F32) for _ in range(NH)]
        Sf = [AR.alloc([128], F32) for _ in range(NH)]
        Sb = [AR.alloc([128], BF16) for _ in range(NH)]
        Sst = [AR.alloc([128], F32) for _ in range(4)]
        Sx = [[AR.alloc([128], F32) for _ in range(2)] for _ in range(NH)]
        ATm = [AR.alloc([128], BF16) for _ in range(4)]
        Vmk = [AR.alloc([128], BF16) for _ in range(4)]
        whq = [AR.alloc([NC8, 128], BF16) for _ in range(2)]
        whf0 = [AR.alloc([NC8, 128], BF16) for _ in range(2)]
        whf1 = [AR.alloc([NC8, 128], BF16) for _ in range(2)]
        whi = [AR.alloc([NC8, 128], BF16) for _ in range(2)]
        whg = [AR.alloc([NC8, 128], BF16) for _ in range(2)]
        tG2 = [AR.alloc([TB + 1], F32) for _ in range(2)]
        qs = tmpf[0][:, :]
        onesf = tmpf[1][:, :]
        osum = tmpf[2][:, :]
        tE = tmpf[3][:, :]
        tf2 = [tmpf[4][:, :], tmpf[5][:, :]]
        tlog2 = [tmpf[6][:, :], tmpf[7][:, :]]
        tkk2 = [tmpf[8][:, :], AR.alloc([TB], F32)]
        tE2 = [AR.alloc([TB], F32) for _ in range(2)]
        tEi2 = [AR.alloc([TB], F32) for _ in range(2)]
        Kt2 = [rb[1][:, :], rsq[1][:, :]]
        Kh2 = [rsq[0][:, :], AR.alloc([TB], BF16)]
        P.op("dve", lambda h: h.memset(onesf, 1.0), w=["onesf"])
        one_t = AR.alloc([1], F32)
        P.op("dve", lambda h: h.memset(one_t, 1.0), w=["one_t"])
        eps_t = AR.alloc([1], F32)
        P.op("dve", lambda h: h.memset(eps_t, RMS_EPS), w=["eps_t"])
        for d in range(2):
            P.op("dve", lambda h, d=d: h.memset(tG2[d][:, 0:1], 0.0), w=[("tG", d)])
        NCB = TB // CH
        scnt = {"st": 0, "kv": 0, "oi": 0, "at": 0}
        whf = (whf0, whf1)

        def head_loads(h):
            sl = h % 2
            DMA("pool", whi[sl], wview(wmix_d, 3072 + h * 128, 128), w=[("whi", sl)])
            DMA("pool", whq[sl], wview(wmix_d, 1536 + h * 128, 128), w=[("whq", sl)])
            DMA("pool", whg[sl], wview(wmix_d, 3584 + h * 128, 128), w=[("whg", sl)])
            DMA("pool", whf0[sl], wview(wmix_d, 2048 + h * 128, 128), w=[("whf", 0, sl)])
            DMA("pool", whf1[sl], wview(wmix_d, 2560 + h * 128, 128), w=[("whf", 1, sl)])
            DMA("sp", Sf[h], s0_d[0, h, :, :], w=[("Sf", h)])

        def vproj(h):
            sl = h % 2
            for tt in range(NB128):
                for k in range(NC8):
                    MM(pb[6][:, 0:128], hmod[:, k, tt * 128:(tt + 1) * 128], whi[sl][:, k, :],
                       start=(k == 0), stop=(k == NC8 - 1), r=[("whi", sl), ("hmod", k, tt // 4)], w=[("ps", 6)])
                ACT(Vh[h][:, tt, :], pb[6][:, 0:128], AF.Copy, r=[("ps", 6)], w=[("Vh", h)])

        def proj(h, tb):
            sl = h % 2
            for k in range(NC8):
                MM(pb[2][:, :], whq[sl][:, k, :], hmod[:, k, tok(tb)], start=(k == 0), stop=(k == NC8 - 1),
                   r=[("whq", sl), ("hmod", k, tb)], w=[("ps", 2)])
            for k in range(NC8):
                MM(pb[tb][:, :], whg[sl][:, k, :], hmod[:, k, tok(tb)], start=(k == 0), stop=(k == NC8 - 1),
                   r=[("whg", sl), ("hmod", k, tb)], w=[("ps", tb)])
            for d in range(2):
                for k in range(NC8):
                    MM(pb[4 + d][:, :], whf[d][sl][:, k, :], hmod[:, k, tok(tb)], start=(k == 0),
                       stop=(k == NC8 - 1), r=[("whf", d, sl), ("hmod", k, tb)], w=[("ps", 4 + d)])

        items = [(h, tb) for h in range(NH) for tb in range(NTB)]
        head_loads(0)
        vproj(0)
        proj(0, 0)
        for idx, (h, tb) in enumerate(items):
            sl = h % 2
            if True:
                ACT(qs, pb[2][:, :], AF.Silu, r=[("ps", 2)], w=["qs"])
                ACT(gsil[h][:, tok(tb)], pb[tb][:, :], AF.Silu, r=[("ps", tb)], w=[("gsil", h)])
                def gen(d, h=h, sl=sl, tb=tb):
                    tf, tlog, tkk, tG, tE, tEi, Kt, Kh = (tf2[d], tlog2[d], tkk2[d], tG2[d], tE2[d], tEi2[d],
                                                          Kt2[d], Kh2[d])
                    kd = lambda n: (n, d)
                    pz_ = pb[4 + d]
                    ACT(tf, pz_[:, :], AF.Sigmoid, r=[("ps", 4 + d)], w=[kd("tf")])
                    yield
                    TS("dve", tf, tf, olb[:, d, h:h + 1], lbv[:, d, h:h + 1], ALU.mult, ALU.add,
                       r=[kd("tf"), "olb", "lbv"], w=[kd("tf")])
                    yield
                    ACT(tlog, tf, AF.Ln, r=[kd("tf")], w=[kd("tlog")])
                    yield
                    ACT(tkk, tf, AF.Identity, r=[kd("tf")], w=[kd("tkk")], scale=-1.0, bias=one_t[:, 0:1])
                    P.op("dve", lambda hh, tG=tG, tlog=tlog: hh.tensor_tensor_scan(
                        out=tG[:, 1:TB + 1], data0=onesf, data1=tlog, initial=0.0, op0=ALU.mult, op1=ALU.add),
                         r=["onesf", kd("tlog")], w=[kd("tG")])
                    yield
                    G3 = tG[:, 1:TB + 1].rearrange("p (c j) -> p c j", j=CH)
                    if d == 0:
                        gprev = tG[:, 0:TB].rearrange("p (c j) -> p c j", j=CH)[:, :, 0:1].to_broadcast([128, NCB, CH])
                        TT("dve", tE.rearrange("p (c j) -> p c j", j=CH), G3, gprev, ALU.subtract,
                           r=[kd("tG")], w=[kd("tE")])
                    else:
                        gend = G3[:, :, CH - 1:CH].to_broadcast([128, NCB, CH])
                        TT("dve", tE, tlog, tG[:, 1:TB + 1], ALU.subtract, r=[kd("tG"), kd("tlog")], w=[kd("tE")])
                        yield
                        TT("dve", tE.rearrange("p (c j) -> p c j", j=CH), tE.rearrange("p (c j) -> p c j", j=CH),
                           gend, ALU.add, r=[kd("tE"), kd("tG")], w=[kd("tE")])
                    yield
                    ACT(tEi, tE, AF.Exp, r=[kd("tE")], w=[kd("tEi")], scale=-1.0)
                    ACT(tE, tE, AF.Exp, r=[kd("tE")], w=[kd("tE")])
                    yield
                    E3 = tE.rearrange("p (c j) -> p c j", j=CH)
                    edge = E3[:, :, CH - 1:CH] if d == 0 else E3[:, :, 0:1]
                    TT("dve", Qt[h][d][:, tok(tb)], qs, tE, ALU.mult, r=["qs", kd("tE")], w=[("Qt", h, d)])
                    TT("dve", Kt, tkk, tEi, ALU.mult, r=[kd("tkk"), kd("tEi")], w=[kd("Kt")])
                    yield
                    TT("dve", Kh.rearrange("p (c j) -> p c j", j=CH), Kt.rearrange("p (c j) -> p c j", j=CH),
                       edge.to_broadcast([128, NCB, CH]), ALU.mult, r=[kd("Kt"), kd("tE")], w=[kd("Kh")])
                    TS("dve", decs[h][d][:, tb * NCB:(tb + 1) * NCB].rearrange("p (c o) -> p c o", o=1), edge,
                       1.0, None, ALU.mult, ALU.bypass, r=[kd("tE")], w=[("dec", h, d)])
                    yield
                    tbank = 7
                    abank = 3 if d == 0 else 6
                    for bb in range(TB // 128):
                        blk = tb * 4 + bb
                        bs = slice(bb * 128, (bb + 1) * 128)
                        gs = slice(tb * TB + bb * 128, tb * TB + (bb + 1) * 128)
                        TR(pbb[tbank][:, 0:128], Kh[:, bs], r=[kd("Kh")], w=[("ps", tbank)])
                        ACT(Khm[h][d][:, blk, :], pbb[tbank][:, 0:128], AF.Copy, r=[("ps", tbank)], w=[("Khm", h, d)])
                        MM(pb[abank][:, 0:128], Kt[:, bs], Qt[h][d][:, gs], start=True, stop=True,
                           r=[kd("Kt"), ("Qt", h, d)], w=[("ps", abank)])
                        yield
                        ai = d * 2 + bb % 2
                        TT("dve", ATm[ai], pb[abank][:, 0:128], (maskf, maskb)[d][:, :], ALU.mult,
                           r=[("ps", abank), "maskf", "maskb"], w=[("ATm", ai)])
                        MM(pb[tb][:, bs], Vh[h][:, blk, :], ATm[ai], start=(d == 0 and bb == 0),
                           stop=(d == 1 and bb == TB // 128 - 1),
                           r=[("Vh", h), ("ATm", ai)], w=[("ps", tb)])
                        yield

                alive = [gen(0), gen(1)]
                rounds = 0
                while alive:
                    for g_ in list(alive):
                        try:
                            next(g_)
                        except StopIteration:
                            alive.remove(g_)
                    rounds += 1
                    if rounds == 1 and idx + 1 < len(items):
                        nh, ntb = items[idx + 1]
                        if ntb == 0:
                            head_loads(nh)
                            vproj(nh)
                        proj(nh, ntb)
                ACT(opart[h][:, tok(tb)], pb[tb][:, :], AF.Copy, r=[("ps", tb)], w=[("opart", h)])
                if tb == NTB - 1:
                    ACT(Sb[h], Sf[h], AF.Copy, r=[("Sf", h)], w=[("Sb", h)])

        def chain_step(h, d, c):
            gcol = slice(c * CH, (c + 1) * CH)
            blk = (c * CH) // 128
            jrow = ((c * CH) % 128) // CH
            oi = scnt["oi"] % 4
            scnt["oi"] += 1
            MM(pb[oi][:, 0:CH], Sb[h], Qt[h][d][:, gcol], start=True, stop=True,
               r=[("Sb", h), ("Qt", h, d)], w=[("ps", oi)])
            TT("dve", opart[h][:, gcol], opart[h][:, gcol], pb[oi][:, 0:CH], ALU.add,
               r=[("opart", h), ("ps", oi)], w=[("opart", h)])
            kv = 4 + scnt["kv"] % 4
            vmi = scnt["kv"] % 4
            scnt["kv"] += 1
            TS("pool", Vmk[vmi], Vh[h][:, blk, :], rowmask[:, jrow:jrow + 1], 1.0, ALU.mult, ALU.mult,
               r=[("Vh", h), "rowmask"], w=[("Vmk", vmi)])
            MM(pb[kv][:, 0:128], Khm[h][d][:, blk, :], Vmk[vmi], start=True, stop=True,
               r=[("Khm", h, d), ("Vmk", vmi)], w=[("ps", kv)])
            STT(Sf[h], Sf[h], decs[h][d][:, c:c + 1], pb[kv][:, 0:128], ALU.mult, ALU.add,
                r=[("Sf", h), ("dec", h, d), ("ps", kv)], w=[("Sf", h)])
            at_boundary = ((c + 1) % (SEQ // CH) == 0) if d == 0 else (c % (SEQ // CH) == 0)
            if at_boundary:
                seq = c // (SEQ // CH)
                si = scnt["st"] % 4
                scnt["st"] += 1
                ACT(Sst[si], Sf[h], AF.Copy, r=[("Sf", h)], w=[("Sst", si)])
                DMA("sp", sout_d[seq, d, h, :, :], Sst[si], r=[("Sst", si)])
                TS("dve", Sf[h], Sf[h], keep[:, 0:1], None, ALU.mult, ALU.bypass,
                   r=[("Sf", h), "keep"], w=[("Sf", h)])
            ACT(Sb[h], Sf[h], AF.Copy, r=[("Sf", h)], w=[("Sb", h)])

        for step in range(NCH):
            for h in range(NH):
                chain_step(h, 0, step)
        for h in range(NH):
            DMA("sp", xs_src[h * 128:(h + 1) * 128, :], Sf[h], r=[("Sf", h)], w=["xs_src"])
        P.op("pool", lambda hh: hh.collective_compute("AllGather", ALU.bypass, replica_groups=PAIRS,
                                                      ins=[xs_src], outs=[xs_dst]),
             r=["xs_src"], w=["xs_dst"], cc=ccsem[2])
        for h in range(NH):
            for rk in range(2):
                DMA("sp", Sx[h][rk], xs_dst[rk * 512 + h * 128:rk * 512 + (h + 1) * 128, :], r=["xs_dst"],
                    w=[("Sx", h, rk)])
            DMA("sp", Sf[h], s0_d[1, h, :, :], w=[("Sf", h)])
            TS("dve", Sf[h], Sf[h], xw[:, 2:3], None, ALU.mult, ALU.bypass, r=[("Sf", h), "xw"], w=[("Sf", h)])
            STT(Sf[h], Sx[h][0], xw[:, 0:1], Sf[h], ALU.mult, ALU.add, r=[("Sx", h, 0), "xw", ("Sf", h)],
                w=[("Sf", h)])
            STT(Sf[h], Sx[h][1], xw[:, 1:2], Sf[h], ALU.mult, ALU.add, r=[("Sx", h, 1), "xw", ("Sf", h)],
                w=[("Sf", h)])
            ACT(Sb[h], Sf[h], AF.Copy, r=[("Sf", h)], w=[("Sb", h)])
        for step in range(NCH):
            for h in range(NH):
                chain_step(h, 1, NCH - 1 - step)
        BARRIER()
        fsq = [rb[0][:, :], rb[1][:, :]]
        frs = [tmpf[4][:, :], tmpf[5][:, :]]
        fos = [tmpf[6][:, :], tmpf[7][:, :]]
        fi = 0
        for h in range(NH):
            for tb in range(NTB):
                q_ = fi % 2
                bank = fi % 4
                fi += 1
                TT("dve", fsq[q_], opart[h][:, tok(tb)], opart[h][:, tok(tb)], ALU.mult,
                   r=[("opart", h)], w=[("fsq", q_)])
                MM(pb[bank][:, :], ones_bf[:, :], fsq[q_], start=True, stop=True, r=["ones", ("fsq", q_)],
                   w=[("ps", bank)])
                ACT(frs[q_], pb[bank][:, :], AF.Ln, r=[("ps", bank)], w=[("frs", q_)], scale=1.0 / 128,
                    bias=eps_t[:, 0:1])
                ACT(frs[q_], frs[q_], AF.Exp, r=[("frs", q_)], w=[("frs", q_)], scale=-0.5)
                TT("dve", fos[q_], opart[h][:, tok(tb)], frs[q_], ALU.mult, r=[("opart", h), ("frs", q_)],
                   w=[("fos", q_)])
                STT(oH[:, h, tok(tb)], fos[q_], hng[:, 0:1], gsil[h][:, tok(tb)], ALU.mult, ALU.mult,
                    r=[("fos", q_), "hng", ("gsil", h)], w=[("oH", h, tb)])
        if DEBUG:
            for h in range(NH):
                DMA("pool", dbg_oh_d[h * 128:(h + 1) * 128, :], oH[:, h, :],
                    r=[("oH", h, tb) for tb in range(NTB)])
        BARRIER()

        AR.reset(m_mix)
        merged = AR.alloc([NC8, 1024], BF16)
        CB = 256
        wga = [AR.alloc([NC8, CB], BF16) for _ in range(2)]
        wgh = [AR.alloc([NC8, CB], BF16) for _ in range(2)]
        wba = [AR.alloc([NH, CB], BF16) for _ in range(2)]
        wbh = [AR.alloc([NH, CB], BF16) for _ in range(2)]
        wmo_all = AR.alloc([NC8, D], BF16)
        mcnt = {"w": 0, "o": 0}
        for half in range(T // 1024):
            tbs = [half * 2, half * 2 + 1]
            for blk in range(D // CB):
                sl = mcnt["w"] % 2
                mcnt["w"] += 1
                c0 = blk * CB
                DMA("pool", wga[sl], wview(wmix_d, 4096 + c0, CB), w=[("wga", sl)])
                DMA("pool", wgh[sl], wview(wmix_d, 5120 + c0, CB), w=[("wgh", sl)])
                DMA("pool", wba[sl], wview(wbra_d, c0, CB), w=[("wba", sl)])
                DMA("pool", wbh[sl], wview(wbrh_d, c0, CB), w=[("wbh", sl)])
                DMA("pool", wmo_all[:, :, c0:c0 + CB], wview(wmo_d, c0, CB), w=[("wmo", blk)])
                for cc in range(CB // 128):
                    dc = blk * (CB // 128) + cc
                    cs = slice(cc * 128, (cc + 1) * 128)
                    for tb in tbs:
                        lt = tb - half * 2
                        o4 = 4 * (mcnt["o"] % 2)
                        o2 = 2 * (mcnt["o"] % 2)
                        mcnt["o"] += 1
                        for k in range(NC8):
                            MM(pb[o4 + 0][:, :], wga[sl][:, k, cs], hmod[:, k, tok(tb)], start=(k == 0),
                               stop=(k == NC8 - 1), r=[("wga", sl), ("hmod", k, tb)], w=[("ps", o4 + 0)])
                        for k in range(NC8):
                            MM(pb[o4 + 1][:, :], wgh[sl][:, k, cs], hmod[:, k, tok(tb)], start=(k == 0),
                               stop=(k == NC8 - 1), r=[("wgh", sl), ("hmod", k, tb)], w=[("ps", o4 + 1)])
                        for hh in range(NH):
                            MM(pb[o4 + 2][:, :], wba[sl][:, hh, cs], oA[:, hh, tok(tb)], start=(hh == 0),
                               stop=(hh == NH - 1), r=[("wba", sl)], w=[("ps", o4 + 2)])
                        for hh in range(NH):
                            MM(pb[o4 + 3][:, :], wbh[sl][:, hh, cs], oH[:, hh, tok(tb)], start=(hh == 0),
                               stop=(hh == NH - 1), r=[("wbh", sl)], w=[("ps", o4 + 3)])
                        sa, sb_ = stg[o2], stg[o2 + 1]
                        ACT(sa[:, :], pb[o4 + 0][:, :], AF.Sigmoid, r=[("ps", o4 + 0)], w=[("tmpf", o2)])
                        ACT(sb_[:, :], pb[o4 + 1][:, :], AF.Sigmoid, r=[("ps", o4 + 1)], w=[("tmpf", o2 + 1)])
                        TT("dve", sa[:, :], sa[:, :], pb[o4 + 2][:, :], ALU.mult, r=[("tmpf", o2), ("ps", o4 + 2)],
                           w=[("tmpf", o2)])
                        TT("dve", sb_[:, :], sb_[:, :], pb[o4 + 3][:, :], ALU.mult,
                           r=[("tmpf", o2 + 1), ("ps", o4 + 3)], w=[("tmpf", o2 + 1)])
                        TT("dve", merged[:, dc, lt * TB:(lt + 1) * TB], sa[:, :], sb_[:, :], ALU.add,
                           r=[("tmpf", o2), ("tmpf", o2 + 1)], w=[("mg", dc, lt)])
        for tb in range(NTB):
            for dc in range(NC8):
                pi = 4 + (dc + tb) % 2
                cs = slice(dc * 128, (dc + 1) * 128)
                for k in range(NC8):
                    MM(pb[pi][:, :], wmo_all[:, k, cs], merged[:, k, tok(tb)],
                       start=(k == 0), stop=(k == NC8 - 1), r=[("wmo", dc // 2), ("mg", k, tb)], w=[("ps", pi)])
                STT(x[:, dc, tok(tb)], pb[pi][:, :], gp[:, 8 + dc:8 + dc + 1], x[:, dc, tok(tb)],
                    ALU.mult, ALU.add, r=[("ps", pi), ("gp", 1), ("x", dc, tb)], w=[("x", dc, tb)])
            if tb == NTB - 1:
                BARRIER()
            layer_norm(1, [tb])

        ffn_ln(2, f2_win_d, f2_wout_d)

        for c in range(NC8):
            for tb in range(NTB):
                DMA("sp", yT_d[c * 128:(c + 1) * 128, tok(tb)], x[:, c, tok(tb)], r=[("x", c, tb)])

        P.finalize(nc, sems, dsems)
        with nc.Block() as block:
            @block.tensor
            def _(h):
                P.emit("pe", h)

            @block.scalar
            def _(h):
                P.emit("act", h)

            @block.vector
            def _(h):
                P.emit("dve", h)

            @block.gpsimd
            def _(h):
                P.emit("pool", h)

            @block.sync
            def _(h):
                P.emit("sp", h)
    return nc


_NC_CACHE = {}


def _pc(v, ncols):
    return np.ascontiguousarray(np.asarray(v).reshape(ncols, 128).T)


def _host_constants():
    f32 = np.float32
    t = np.arange(TK)
    row = (t // 64).astype(f32)
    col = (t % 64).astype(f32)
    inv = (np.float32(10000.0) ** (-np.arange(0, 32, 2, dtype=f32) / np.float32(32))).astype(f32)
    ar = row[:, None] * inv
    ac = col[:, None] * inv
    ang = np.concatenate([ar, ar, ac, ac], axis=-1)
    cos = np.cos(ang).astype(f32).T
    sin = np.sin(ang).astype(f32).T
    d = np.arange(64)
    first = (d % 32) < 16
    sign = np.where(first, -1.0, 1.0).astype(f32)
    perm = np.where(first, d + 16, d - 16)
    cosT = np.concatenate([cos, cos], axis=0)
    sinT = np.concatenate([sin * sign[:, None], sin * sign[:, None]], axis=0)
    rot = np.zeros((128, 128), f32)
    for c in range(2):
        for dd in range(64):
            rot[c * 64 + perm[dd], c * 64 + dd] = 1.0
    i = np.arange(128)
    same = (i[:, None] // CH) == (i[None, :] // CH)
    maskf = (same & (i[:, None] <= i[None, :])).astype(f32)
    maskb = (same & (i[:, None] >= i[None, :])).astype(f32)
    nq = T // 256
    mbs = np.zeros((128, NKT * nq), f32)
    mbp = []
    for r in range(2):
        m = np.full((128, NKT * nq), -30000.0, f32)
        for kt in range(NKT):
            k0 = kt * 128 - CTX - r * T
            if 0 <= k0 < T:
                m[:, kt * nq + k0 // 256] = 0.0
        mbp.append(m)
    return dict(cos=[np.ascontiguousarray(cosT[:, 0:T]), np.ascontiguousarray(cosT[:, T:2 * T][:, ::-1])],
                sin=[np.ascontiguousarray(sinT[:, 0:T]), np.ascontiguousarray(sinT[:, T:2 * T][:, ::-1])],
                rot=rot, maskf=maskf, maskb=maskb, mbs=mbs, mbp=mbp, ident=np.eye(128, dtype=f32),
                rowmask=((i[:, None] // CH) == np.arange(4)[None, :]).astype(f32),
                cos1=np.ones((128, T), f32), sin0=np.zeros((128, T), f32))


def kernel(**inputs):
    inp = {k: np.asarray(v) for k, v in inputs.items()}
    if "nc" not in _NC_CACHE:
        _NC_CACHE["nc"] = build_program()
        _NC_CACHE["hc"] = _host_constants()
    nc = _NC_CACHE["nc"]
    hc = _NC_CACHE["hc"]
    f32 = np.float32
    xs, xp = inp["x_sample"], inp["x_prompt"]
    lamv = np.stack([inp["lambda_q1"][0], inp["lambda_k1"][0], inp["lambda_q2"][0], inp["lambda_k2"][0]], 0)
    lamv = np.ascontiguousarray(np.broadcast_to(lamv[None], (128, 4, 64))).astype(f32)
    lbl4 = inp["hgrn_lb_logits"].reshape(2, 2, NH, 128)
    lbl = np.ascontiguousarray(lbl4.transpose(3, 0, 1, 2)).astype(f32)
    lbl_sw = np.ascontiguousarray(lbl4[::-1].transpose(3, 0, 1, 2)).astype(f32)
    wmix = inp["w_mix_in"][0]
    wmix_sw = np.concatenate([wmix[:, :2048], wmix[:, 2560:3072], wmix[:, 2048:2560], wmix[:, 3072:]], axis=1)
    wmix_sw = np.ascontiguousarray(wmix_sw)
    wa = inp["w_ada"][0]
    ba = inp["b_ada"][0]
    wa_half = [np.ascontiguousarray(np.concatenate([wa[:, s * 3072 + r * 1536:s * 3072 + (r + 1) * 1536]
                                                    for s in (1, 2)], axis=1)) for r in range(2)]
    wa0 = np.ascontiguousarray(wa[:, 0:3072])
    ba0 = _pc(ba[0:3072], 24)
    ba_half = [_pc(np.concatenate([ba[s * 3072 + r * 1536:s * 3072 + (r + 1) * 1536] for s in range(3)]), 36)
               for r in range(2)]
    shared = {
        "ffn1_w_in": inp["ffn1_w_in"][0], "ffn1_w_out": inp["ffn1_w_out"][0],
        "ffn2_w_in": inp["ffn2_w_in"][0], "ffn2_w_out": inp["ffn2_w_out"][0],
        "ln_g": _pc(inp["ln_g"][0].reshape(-1), 24), "ln_b": _pc(inp["ln_b"][0].reshape(-1), 24),
        "w_branch_a": inp["w_branch_a"][0], "w_branch_h": inp["w_branch_h"][0],
        "w_mix_out": inp["w_mix_out"][0], "lamv": lamv,
        "subln_g": np.ascontiguousarray(inp["attn_subln_g"][0].reshape(128, 1)),
        "hnorm_g": np.ascontiguousarray(inp["hgrn_norm_g"][0].reshape(128, 1)),
        "ident": hc["ident"], "rotm": hc["rot"], "maskf": hc["maskf"], "maskb": hc["maskb"],
        "rowmask": hc["rowmask"],
    }
    in_maps = []
    for core in range(8):
        m = dict(shared)
        r = core % 2
        m["w_ada"], m["b_ada"] = wa_half[r], ba_half[r]
        m["w_ada0"], m["b_ada0"] = wa0, ba0
        if core < 4:
            b = core // 2
            xh = xs[b, r * T:(r + 1) * T]
            if r == 1:
                xh = xh[::-1]
            m["xT"] = np.ascontiguousarray(xh.T)
            m["cond"] = _pc(inp["c"][b], 8)
            m["kctxT"] = np.ascontiguousarray(inp["cache_k"][b, 0].reshape(CTX, 512).T)
            m["vctx"] = np.ascontiguousarray(inp["cache_v"][b, 0].reshape(CTX, 512))
            st = inp["state_hgrn"][b, 0]
            m["s0"] = np.ascontiguousarray(st if r == 0 else st[::-1])
            m["cosT"], m["sinT"] = hc["cos"][r], hc["sin"][r]
            m["mbias"] = hc["mbs"]
            m["keep"] = np.ones((128, 1), f32)
            xw = np.zeros((128, 3), f32)
            xw[:, 1 - r] = 1.0
            m["xw"] = xw
            m["w_mix_in"] = wmix if r == 0 else wmix_sw
            m["lb_logits"] = lbl if r == 0 else lbl_sw
        else:
            g0 = (core - 4) * 4
            m["xT"] = np.ascontiguousarray(xp[g0:g0 + 4].reshape(T, D).T)
            m["cond"] = _pc(inp["c_ctx"], 8)
            m["kctxT"] = np.zeros((512, CTX), f32)
            m["vctx"] = np.zeros((CTX, 512), f32)
            m["s0"] = np.zeros((2, NH, 128, 128), f32)
            m["cosT"], m["sinT"] = hc["cos1"], hc["sin0"]
            m["mbias"] = hc["mbp"][r]
            m["keep"] = np.zeros((128, 1), f32)
            m["xw"] = np.zeros((128, 3), f32)
            m["w_mix_in"] = wmix
            m["lb_logits"] = lbl
        in_maps.append(m)
    res = run_bass_kernel_spmd(nc, in_maps, core_ids=list(range(8)))
    R = res.results
    _NC_CACHE["last"] = R
    y_sample = np.stack([np.concatenate([R[2 * b]["yT"].T, R[2 * b + 1]["yT"].T[::-1]], axis=0) for b in range(2)], 0)
    y_prompt = np.concatenate([R[c]["yT"].T.reshape(4, SEQ, D) for c in range(4, 8)], axis=0)
    nk = np.concatenate([R[c]["kT_out"].T.reshape(4, SEQ, NH, 2, 64) for c in range(4, 8)], axis=0)
    nv = np.concatenate([R[c]["v_out"].reshape(4, SEQ, NH, 128) for c in range(4, 8)], axis=0)
    ns = np.concatenate([R[c]["s_out"] for c in range(4, 8)], axis=0)
    return (np.ascontiguousarray(y_prompt), np.ascontiguousarray(y_sample),
            np.ascontiguousarray(nk[:, None]), np.ascontiguousarray(nv[:, None]), np.ascontiguousarray(ns[:, None]))
```
